# Optimizing a Trainium2 kernel written in Bass

```python
import math
import jax, jax.numpy as jnp
from jax import lax
import numpy as np

D_MODEL = 2048
BATCH = 32
SEQ = 256
DEPTH = 2
DEC_BATCH = 4
DEC_SEQ = 1024
PAST_LEN = 256

GRID_W = 64
D_BRANCH = 512
D_MIX = 2048
A_HEADS = 4
A_KV_HEADS = 2
A_HEAD_DIM = 128
A_WINDOW = 128
A_BLOCK = 128
ROPE_THETA = 10000.0
B_HEADS = 8
B_HEAD_DIM = 64
B_GROUPS = 2
B_STATE = 128
B_CONV = 5
B_CHUNK = 128
B_XBC = 1024
POOL_WINDOWS = (2, 4, 8, 16)
N_POOL = 4
POOL_GROUP = 128
D_HEADS = 4
D_HEAD_DIM = 128
NA_KH = 8
NA_KW = 16
PROJ_SIZES = (512, 256, 256, 512,
              1024, 512, 16,
              512, 512,
              512, 512, 512, 512)
D_PROJ = 6160
LN_EPS = 1e-6
NEG_INF = -1e30
F32 = jnp.float32

kernel_name = 'hybrid_diffusion_parallel_heads_step'


def _ln(x):
    xf = x.astype(F32)
    mu = jnp.mean(xf, -1, keepdims=True)
    var = jnp.mean(jnp.square(xf - mu), -1, keepdims=True)
    return ((xf - mu) * lax.rsqrt(var + LN_EPS)).astype(x.dtype)


def _rms(x, w):
    xf = x.astype(F32)
    y = xf * lax.rsqrt(jnp.mean(xf * xf, -1, keepdims=True) + LN_EPS)
    return y.astype(x.dtype) * w


def _split_proj(proj):
    idx = []
    acc = 0
    for s in PROJ_SIZES[:-1]:
        acc += s
        idx.append(acc)
    return jnp.split(proj, idx, axis=-1)


def _grid_pos(T):
    t = jnp.arange(T)
    return (t // GRID_W).astype(F32), (t % GRID_W).astype(F32)


def _rope_axial(x, rows, cols):
    hd = x.shape[-1]
    ax = hd // 2
    nf = ax // 2
    inv = ROPE_THETA ** (-jnp.arange(nf, dtype=F32) / nf)

    def rot(xa, pos):
        ang = pos[:, None] * inv[None, :]
        cos = jnp.cos(ang)[None, :, None, :]
        sin = jnp.sin(ang)[None, :, None, :]
        x1 = xa[..., :nf].astype(F32)
        x2 = xa[..., nf:].astype(F32)
        return jnp.concatenate([x1 * cos - x2 * sin, x1 * sin + x2 * cos], -1)

    return jnp.concatenate([rot(x[..., :ax], rows), rot(x[..., ax:], cols)], -1).astype(x.dtype)


def _ctx_attention(q, k, v, sink):
    Bn, L, H, hd = q.shape
    KV = k.shape[2]
    G = H // KV
    qg = q.reshape(Bn, L, KV, G, hd)
    s = jnp.einsum('blkgd,bmkd->bkglm', qg, k, preferred_element_type=F32) * (hd ** -0.5)
    if sink is not None:
        s_sink = jnp.broadcast_to(sink.astype(F32).reshape(1, KV, G, 1, 1), s.shape[:-1] + (1,))
        s = jnp.concatenate([s, s_sink], -1)
    p = jax.nn.softmax(s, -1)[..., :L].astype(v.dtype)
    o = jnp.einsum('bkglm,bmkd->blkgd', p, v)
    return o.reshape(Bn, L, H * hd)


def _window_attention(q, k, v, k_ctx, v_ctx, sink):
    Bn, T, H, hd = q.shape
    KV = k.shape[2]
    G = H // KV
    nb = T // A_BLOCK
    Lc = k_ctx.shape[1]
    scale = hd ** -0.5
    qb = q.reshape(Bn, nb, A_BLOCK, KV, G, hd)

    def bands(a):
        ap = jnp.pad(a, ((0, 0), (A_BLOCK, A_BLOCK), (0, 0), (0, 0)))
        ap = ap.reshape(Bn, nb + 2, A_BLOCK, KV, hd)
        return jnp.concatenate([ap[:, :-2], ap[:, 1:-1], ap[:, 2:]], axis=2)

    kw = bands(k)
    vw = bands(v)
    s_loc = jnp.einsum('bnqkgd,bnmkd->bnkgqm', qb, kw, preferred_element_type=F32) * scale
    qpos = jnp.arange(nb)[:, None] * A_BLOCK + jnp.arange(A_BLOCK)[None, :]
    kpos = jnp.arange(nb)[:, None] * A_BLOCK - A_BLOCK + jnp.arange(3 * A_BLOCK)[None, :]
    valid = ((jnp.abs(kpos[:, None, :] - qpos[:, :, None]) <= A_WINDOW)
             & (kpos >= 0)[:, None, :] & (kpos < T)[:, None, :])
    s_loc = jnp.where(valid[None, :, None, None], s_loc, NEG_INF)
    s_ctx = jnp.einsum('bnqkgd,blkd->bnkgql', qb, k_ctx, preferred_element_type=F32) * scale
    s_sink = jnp.broadcast_to(sink.astype(F32).reshape(1, 1, KV, G, 1, 1), s_loc.shape[:-1] + (1,))
    p = jax.nn.softmax(jnp.concatenate([s_loc, s_ctx, s_sink], -1), -1)
    nw = 3 * A_BLOCK
    p_loc = p[..., :nw].astype(v.dtype)
    p_ctx = p[..., nw:nw + Lc].astype(v.dtype)
    o = (jnp.einsum('bnkgqm,bnmkd->bnqkgd', p_loc, vw)
         + jnp.einsum('bnkgql,blkd->bnqkgd', p_ctx, v_ctx))
    return o.reshape(Bn, T, H * hd)


def _neighbourhood_attention(q, k, v, k_ctx, v_ctx, rpb):
    Bn, T, H, hd = q.shape
    rows = T // GRID_W
    kh = min(NA_KH, rows)
    Lc = k_ctx.shape[1]
    scale = hd ** -0.5
    r = jnp.arange(rows)
    rs = jnp.clip(r - kh // 2, 0, rows - kh)
    key_rows = rs[:, None] + jnp.arange(kh)[None, :]
    c = jnp.arange(GRID_W)
    cs = jnp.clip(c - NA_KW // 2, 0, GRID_W - NA_KW)
    qg = q.reshape(Bn, rows, GRID_W, H, hd)
    kg = k.reshape(Bn, rows, GRID_W, H, hd)[:, key_rows]
    vg = v.reshape(Bn, rows, GRID_W, H, hd)[:, key_rows]
    s_loc = jnp.einsum('brqhd,brjkhd->brhqjk', qg, kg, preferred_element_type=F32) * scale
    dy = key_rows - r[:, None]
    dx = jnp.clip(c[None, :] - c[:, None], -(NA_KW - 1), NA_KW - 1)
    col_ok = (c[None, :] >= cs[:, None]) & (c[None, :] < cs[:, None] + NA_KW)
    bias = rpb[:, (dy + NA_KH - 1)[:, :, None, None], (dx + NA_KW - 1)[None, None, :, :]]
    bias = bias.transpose(1, 0, 3, 2, 4).astype(F32)
    s_loc = jnp.where(col_ok[:, None, :], s_loc + bias[None], NEG_INF)
    nl = kh * GRID_W
    s_loc = s_loc.reshape(Bn, rows, H, GRID_W, nl)
    s_ctx = jnp.einsum('brqhd,blhd->brhql', qg, k_ctx, preferred_element_type=F32) * scale
    p = jax.nn.softmax(jnp.concatenate([s_loc, s_ctx], -1), -1)
    p_loc = p[..., :nl].reshape(Bn, rows, H, GRID_W, kh, GRID_W).astype(v.dtype)
    p_ctx = p[..., nl:nl + Lc].astype(v.dtype)
    o = (jnp.einsum('brhqjk,brjkhd->brqhd', p_loc, vg)
         + jnp.einsum('brhql,blhd->brqhd', p_ctx, v_ctx))
    return o.reshape(Bn, T, H * hd)


def _pool_branch(p, w_pool, b_pool, pool_scale):
    Bn, T, C = p.shape
    cs = jnp.pad(jnp.cumsum(p.astype(F32), axis=1), ((0, 0), (1, 0), (0, 0)))
    t = jnp.arange(T)
    means = []
    for g, w in enumerate(POOL_WINDOWS):
        lo = jnp.clip(t - w // 2, 0, T)
        hi = jnp.clip(t - w // 2 + w, 0, T)
        sl = cs[..., g * POOL_GROUP:(g + 1) * POOL_GROUP]
        means.append((sl[:, hi] - sl[:, lo]) / (hi - lo).astype(F32)[None, :, None])
    pooled = (jnp.concatenate(means, -1) - p.astype(F32)).astype(p.dtype)
    pg = pooled.reshape(Bn, T, N_POOL, POOL_GROUP)
    out = jnp.einsum('btgc,gcd->btgd', pg, w_pool) + b_pool
    return out.reshape(Bn, T, C) * pool_scale


def _dwconv(x, w, b):
    K, C = w.shape
    y = lax.conv_general_dilated(x, w[:, None, :].astype(x.dtype), (1,), [(K // 2, K // 2)],
                                 dimension_numbers=('NWC', 'WIO', 'NWC'), feature_group_count=C)
    return y + b


def _ssd(x, dt, A, Bm, Cm, h0):
    Bn, T, H, P = x.shape
    G, N = Bm.shape[2], Bm.shape[3]
    Q = B_CHUNK
    nc = T // Q
    rep = H // G
    xc = x.astype(F32).reshape(Bn, nc, Q, H, P)
    Bc = jnp.repeat(Bm.astype(F32), rep, axis=2).reshape(Bn, nc, Q, H, N)
    Cc = jnp.repeat(Cm.astype(F32), rep, axis=2).reshape(Bn, nc, Q, H, N)
    dtc = dt.reshape(Bn, nc, Q, H)
    Lc = jnp.cumsum(dtc * A, axis=2)
    causal = jnp.tril(jnp.ones((Q, Q), bool))
    seg = Lc[:, :, :, None, :] - Lc[:, :, None, :, :]
    decay = jnp.exp(jnp.where(causal[None, None, :, :, None], seg, -jnp.inf))
    cb = jnp.einsum('bcihn,bcjhn->bcijh', Cc, Bc)
    y_intra = jnp.einsum('bcijh,bcjhp->bcihp', cb * decay * dtc[:, :, None, :, :], xc)
    to_end = jnp.exp(Lc[:, :, -1:, :] - Lc) * dtc
    chunk_states = jnp.einsum('bcjhn,bcjhp->bchpn', Bc * to_end[..., None], xc)
    chunk_decay = jnp.exp(Lc[:, :, -1, :])

    def step(h, inp):
        st, dcy = inp
        return dcy[:, :, None, None] * h + st, h

    h_last, h_start = lax.scan(step, h0.astype(F32),
                               (jnp.moveaxis(chunk_states, 1, 0), jnp.moveaxis(chunk_decay, 1, 0)))
    h_start = jnp.moveaxis(h_start, 0, 1)
    y_inter = jnp.einsum('bcihn,bchpn->bcihp', Cc * jnp.exp(Lc)[..., None], h_start)
    y = (y_intra + y_inter).reshape(Bn, T, H, P)
    return y, h_last


def _ssm_branch(xbc, z, dt_raw, h0_f, h0_b, lp):
    Bn, T, _ = xbc.shape
    xbc = jax.nn.silu(_dwconv(xbc, lp['ssm_conv_w'], lp['ssm_conv_b']))
    nx = B_HEADS * B_HEAD_DIM
    xs, Bm, Cm = jnp.split(xbc, [nx, nx + B_GROUPS * B_STATE], axis=-1)
    x = xs.reshape(Bn, T, B_HEADS, B_HEAD_DIM)
    Bm = Bm.reshape(Bn, T, B_GROUPS, B_STATE)
    Cm = Cm.reshape(Bn, T, B_GROUPS, B_STATE)
    dt = jax.nn.softplus(dt_raw.astype(F32).reshape(Bn, T, 2, B_HEADS) + lp['ssm_dt_bias'].astype(F32))
    A = -jnp.exp(lp['ssm_a_log'].astype(F32))
    y_f, h_f = _ssd(x, dt[:, :, 0], A[0], Bm, Cm, h0_f)
    y_b, h_b = _ssd(jnp.flip(x, 1), jnp.flip(dt[:, :, 1], 1), A[1],
                    jnp.flip(Bm, 1), jnp.flip(Cm, 1), h0_b)
    y = y_f + jnp.flip(y_b, 1) + lp['ssm_d'].astype(F32)[:, None] * x.astype(F32)
    y = y.reshape(Bn, T, D_BRANCH).astype(xbc.dtype)
    return _rms(y * jax.nn.silu(z), lp['ssm_norm_w']), h_f.astype(xbc.dtype), h_b.astype(xbc.dtype)


def _modulate_project(x, cvec, lp):
    mod = jax.nn.silu(cvec) @ lp['w_ada'] + lp['b_ada']
    shift, scale, gate = jnp.split(mod[:, None, :], 3, axis=-1)
    u = _ln(x) * (1.0 + scale) + shift
    return _split_proj(u @ lp['w_in']), gate


def _post_norm_residual(x, mixed, gate, lp, alpha):
    out = (mixed @ lp['w_out']) * gate
    return _ln(alpha * x + out) * lp['ln_g'] + lp['ln_b']


def _context_layer(x, c_ctx, lp, alpha):
    Bn, L, _ = x.shape
    parts, gate = _modulate_project(x, c_ctx[None, :], lp)
    qa, ka, va, ga, xbc, z, dt_raw, pc, gc, qd, kd, vd, gd = parts
    qa = qa.reshape(Bn, L, A_HEADS, A_HEAD_DIM)
    ka = ka.reshape(Bn, L, A_KV_HEADS, A_HEAD_DIM)
    va = va.reshape(Bn, L, A_KV_HEADS, A_HEAD_DIM)
    o_a = _ctx_attention(qa, ka, va, lp['attn_sink']) * jax.nn.silu(ga)
    h0 = jnp.zeros((Bn, B_HEADS, B_HEAD_DIM, B_STATE), x.dtype)
    o_b, h_f, h_b = _ssm_branch(xbc, z, dt_raw, h0, h0, lp)
    o_c = _pool_branch(pc, lp['pool_w'], lp['pool_b'], lp['pool_scale']) * jax.nn.silu(gc)
    qd = qd.reshape(Bn, L, D_HEADS, D_HEAD_DIM)
    kd = kd.reshape(Bn, L, D_HEADS, D_HEAD_DIM)
    vd = vd.reshape(Bn, L, D_HEADS, D_HEAD_DIM)
    o_d = _ctx_attention(qd, kd, vd, None) * jax.nn.silu(gd)
    mixed = jnp.concatenate([o_a, o_b, o_c, o_d], -1)
    return _post_norm_residual(x, mixed, gate, lp, alpha), (ka, va, kd, vd, h_f, h_b)


def _latent_layer(x, c, lp, ka_c, va_c, kd_c, vd_c, hf0, hb0, alpha):
    Bn, T, _ = x.shape
    parts, gate = _modulate_project(x, c, lp)
    qa, ka, va, ga, xbc, z, dt_raw, pc, gc, qd, kd, vd, gd = parts
    rows, cols = _grid_pos(T)
    qa = _rope_axial(qa.reshape(Bn, T, A_HEADS, A_HEAD_DIM), rows, cols)
    ka = _rope_axial(ka.reshape(Bn, T, A_KV_HEADS, A_HEAD_DIM), rows, cols)
    va = va.reshape(Bn, T, A_KV_HEADS, A_HEAD_DIM)
    o_a = _window_attention(qa, ka, va, ka_c, va_c, lp['attn_sink']) * jax.nn.silu(ga)
    o_b, _, _ = _ssm_branch(xbc, z, dt_raw, hf0, hb0, lp)
    o_c = _pool_branch(pc, lp['pool_w'], lp['pool_b'], lp['pool_scale']) * jax.nn.silu(gc)
    qd = qd.reshape(Bn, T, D_HEADS, D_HEAD_DIM)
    kd = kd.reshape(Bn, T, D_HEADS, D_HEAD_DIM)
    vd = vd.reshape(Bn, T, D_HEADS, D_HEAD_DIM)
    o_d = _neighbourhood_attention(qd, kd, vd, kd_c, vd_c, lp['na_rpb']) * jax.nn.silu(gd)
    mixed = jnp.concatenate([o_a, o_b, o_c, o_d], -1)
    return _post_norm_residual(x, mixed, gate, lp, alpha)


def setup_inputs(seed: int = 0) -> dict:
    key = jax.random.key(seed)
    ks = jax.random.split(key, 27)

    def nrm(k, shape, s=1.0):
        return jax.random.normal(k, shape, jnp.float32) * s

    out_scale = (8.0 * DEPTH) ** -0.25
    dt_init = jnp.exp(jax.random.uniform(ks[20], (DEPTH, 2, B_HEADS), jnp.float32,
                                         math.log(1e-3), math.log(1e-1)))
    return {
        'x_prompt': nrm(ks[0], (BATCH, SEQ, D_MODEL)),
        'x_sample': nrm(ks[1], (DEC_BATCH, DEC_SEQ, D_MODEL)),
        'cache_attn_k': nrm(ks[2], (DEC_BATCH, DEPTH, PAST_LEN, A_KV_HEADS, A_HEAD_DIM)),
        'cache_attn_v': nrm(ks[3], (DEC_BATCH, DEPTH, PAST_LEN, A_KV_HEADS, A_HEAD_DIM)),
        'cache_na_k': nrm(ks[4], (DEC_BATCH, DEPTH, PAST_LEN, D_HEADS, D_HEAD_DIM)),
        'cache_na_v': nrm(ks[5], (DEC_BATCH, DEPTH, PAST_LEN, D_HEADS, D_HEAD_DIM)),
        'state_ssm_fwd': nrm(ks[6], (DEC_BATCH, DEPTH, B_HEADS, B_HEAD_DIM, B_STATE), 0.5),
        'state_ssm_bwd': nrm(ks[7], (DEC_BATCH, DEPTH, B_HEADS, B_HEAD_DIM, B_STATE), 0.5),
        'c': nrm(ks[8], (DEC_BATCH, D_MODEL)),
        'c_ctx': nrm(ks[9], (D_MODEL,)),
        'w_ada': nrm(ks[10], (DEPTH, D_MODEL, 3 * D_MODEL), 0.5 * D_MODEL ** -0.5),
        'b_ada': nrm(ks[11], (DEPTH, 3 * D_MODEL), 0.02),
        'w_in': nrm(ks[12], (DEPTH, D_MODEL, D_PROJ), D_MODEL ** -0.5),
        'w_out': nrm(ks[13], (DEPTH, D_MIX, D_MODEL), out_scale * D_MIX ** -0.5),
        'ln_g': 1.0 + nrm(ks[14], (DEPTH, D_MODEL), 0.02),
        'ln_b': nrm(ks[15], (DEPTH, D_MODEL), 0.02),
        'attn_sink': nrm(ks[16], (DEPTH, A_HEADS), 0.5),
        'ssm_conv_w': nrm(ks[17], (DEPTH, B_CONV, B_XBC), B_CONV ** -0.5),
        'ssm_conv_b': nrm(ks[18], (DEPTH, B_XBC), 0.02),
        'ssm_a_log': jnp.log(jax.random.uniform(ks[19], (DEPTH, 2, B_HEADS), jnp.float32, 1.0, 16.0)),
        'ssm_dt_bias': dt_init + jnp.log(-jnp.expm1(-dt_init)),
        'ssm_d': 1.0 + nrm(ks[21], (DEPTH, B_HEADS), 0.1),
        'ssm_norm_w': 1.0 + nrm(ks[22], (DEPTH, D_BRANCH), 0.02),
        'pool_w': nrm(ks[23], (DEPTH, N_POOL, POOL_GROUP, POOL_GROUP), POOL_GROUP ** -0.5),
        'pool_b': nrm(ks[24], (DEPTH, N_POOL, POOL_GROUP), 0.02),
        'pool_scale': 1.0 + nrm(ks[25], (DEPTH, D_BRANCH), 0.05),
        'na_rpb': nrm(ks[26], (DEPTH, D_HEADS, 2 * NA_KH - 1, 2 * NA_KW - 1), 0.1),
    }


def reference(x_prompt, x_sample, cache_attn_k, cache_attn_v, cache_na_k, cache_na_v,
              state_ssm_fwd, state_ssm_bwd, c, c_ctx, w_ada, b_ada, w_in, w_out, ln_g, ln_b,
              attn_sink, ssm_conv_w, ssm_conv_b, ssm_a_log, ssm_dt_bias, ssm_d, ssm_norm_w,
              pool_w, pool_b, pool_scale, na_rpb):
    alpha = (2.0 * DEPTH) ** 0.25
    y_prompt = x_prompt
    y_sample = x_sample
    ctx_states = []
    for l in range(DEPTH):
        lp = {'w_ada': w_ada[l], 'b_ada': b_ada[l], 'w_in': w_in[l], 'w_out': w_out[l],
              'ln_g': ln_g[l], 'ln_b': ln_b[l], 'attn_sink': attn_sink[l],
              'ssm_conv_w': ssm_conv_w[l], 'ssm_conv_b': ssm_conv_b[l], 'ssm_a_log': ssm_a_log[l],
              'ssm_dt_bias': ssm_dt_bias[l], 'ssm_d': ssm_d[l], 'ssm_norm_w': ssm_norm_w[l],
              'pool_w': pool_w[l], 'pool_b': pool_b[l], 'pool_scale': pool_scale[l],
              'na_rpb': na_rpb[l]}
        y_prompt, st = _context_layer(y_prompt, c_ctx, lp, alpha)
        ctx_states.append(st)
        y_sample = _latent_layer(y_sample, c, lp, cache_attn_k[:, l], cache_attn_v[:, l],
                                 cache_na_k[:, l], cache_na_v[:, l],
                                 state_ssm_fwd[:, l], state_ssm_bwd[:, l], alpha)
    new_attn_k = jnp.stack([s[0] for s in ctx_states], axis=1)
    new_attn_v = jnp.stack([s[1] for s in ctx_states], axis=1)
    new_na_k = jnp.stack([s[2] for s in ctx_states], axis=1)
    new_na_v = jnp.stack([s[3] for s in ctx_states], axis=1)
    new_ssm_fwd = jnp.stack([s[4] for s in ctx_states], axis=1)
    new_ssm_bwd = jnp.stack([s[5] for s in ctx_states], axis=1)
    return (y_prompt, y_sample, new_attn_k, new_attn_v, new_na_k, new_na_v, new_ssm_fwd, new_ssm_bwd)
```

```python
import numpy as np
import concourse.bass as bass
import concourse.mybir as mybir
from concourse.bass_utils import run_bass_kernel_spmd

F32 = mybir.dt.float32
BF = mybir.dt.bfloat16
AF = mybir.ActivationFunctionType
ALU = mybir.AluOpType
AX = mybir.AxisListType

NCORES = 8
D = 2048
DP = 6160
T = 1024
NT = 8
EPS = 1e-6
ALPHA = 4.0 ** 0.25
SCALE = 128.0 ** -0.5
NEG = -1e30
DEBUG = False
STAGE = 99
FEAT = 99


class _Stop(Exception):
    pass

C_QA, C_KA, C_VA, C_GA, C_XBC, C_Z, C_DT, C_PC, C_GC, C_QD, C_KD, C_VD, C_GD = (
    0, 512, 768, 1024, 1536, 2560, 3072, 3088, 3600, 4112, 4624, 5136, 5648)


class Op:
    __slots__ = ("eng", "fn", "deps", "sem", "val", "needed", "isdma", "phase", "idx")


class Sched:
    ENG = ["pe", "act", "dve", "pool", "sp"]

    def __init__(self, nc, dma_sems, prog_sems):
        self.nc = nc
        self.ops = {e: [] for e in self.ENG}
        self.lastw = {}
        self.readers = {}
        self.pending = {e: [] for e in self.ENG}
        self.dma_sems = dma_sems
        self.dma_last = [None] * len(dma_sems)
        self.dma_cnt = [0] * len(dma_sems)
        self.dma_rr = 0
        self.dma_rr_pool = 0
        self.prog_sems = prog_sems
        self.phase = 0
        self.phase_dmas = []
        self.psn = 0
        self.nops = 0
        self.stopped = False

    def op(self, eng, fn, reads=(), writes=(), dma=False, nobarrier=False):
        if self.stopped:
            return None
        o = Op()
        o.eng, o.fn, o.isdma, o.needed, o.phase = eng, fn, dma, False, self.phase
        o.sem = None
        o.val = 0
        o.idx = self.nops
        self.nops += 1
        if nobarrier:
            deps = []
        else:
            deps = list(self.pending[eng])
            self.pending[eng] = []
        for k in reads:
            w = self.lastw.get(k)
            if w is not None:
                deps.append(w)
            self.readers.setdefault(k, []).append(o)
        for k in writes:
            w = self.lastw.get(k)
            if w is not None:
                deps.append(w)
            for r in self.readers.get(k, []):
                if r is not o:
                    deps.append(r)
            self.readers[k] = []
            self.lastw[k] = o
        if dma:
            half = len(self.dma_sems) // 2
            if eng == "pool":
                j = half + self.dma_rr_pool
                self.dma_rr_pool = (self.dma_rr_pool + 1) % half
            else:
                j = self.dma_rr
                self.dma_rr = (self.dma_rr + 1) % half
            if self.dma_last[j] is not None:
                deps.append(self.dma_last[j])
            self.dma_last[j] = o
            self.dma_cnt[j] += 16
            o.sem = self.dma_sems[j]
            o.val = self.dma_cnt[j]
            o.needed = True
            self.phase_dmas.append(o)
        fl = []
        for d in deps:
            if d is o:
                continue
            if (not d.isdma) and (not dma) and d.eng == eng:
                if eng == "pe":
                    continue
            fl.append(d)
            d.needed = True
        o.deps = fl
        self.ops[eng].append(o)
        return o

    def barrier(self, new_phase=True, final=False):
        if self.stopped:
            return
        lasts = []
        for e in self.ENG:
            for o in reversed(self.ops[e]):
                if not o.isdma:
                    lasts.append(o)
                    break
        lasts += self.phase_dmas
        self.phase_dmas = []
        for o in lasts:
            o.needed = True
        for e in self.ENG:
            if e == "pe" and not final:
                continue
            self.pending[e] = self.pending[e] + list(lasts)
        if new_phase:
            self.phase += 1

    def finalize(self):
        cnt = {}
        for e in self.ENG:
            for o in self.ops[e]:
                if o.isdma:
                    continue
                if o.needed:
                    k = (e, o.phase)
                    cnt[k] = cnt.get(k, 0) + 1
                    o.sem = self.prog_sems[k]
                    o.val = cnt[k]

    def emit(self, e, eng):
        seen = {}
        for o in self.ops[e]:
            for d in o.deps:
                sid = id(d.sem)
                if seen.get(sid, 0) < d.val:
                    eng.wait_ge(d.sem, d.val)
                    seen[sid] = d.val
            ins = o.fn(eng)
            if o.needed:
                ins.then_inc(o.sem, 16 if o.isdma else 1)


def _consts():
    c = {}
    i = np.arange(128)
    c["ident"] = np.eye(128, dtype=np.float32)
    c["ones"] = np.ones((128, 128), np.float32)
    c["trif"] = (i[:, None] <= i[None, :]).astype(np.float32)
    c["trib"] = (i[:, None] >= i[None, :]).astype(np.float32)
    nf = np.where(i[None, :] >= i[:, None], 0.0, NEG).astype(np.float32)
    nb = np.where(i[None, :] <= i[:, None], 0.0, NEG).astype(np.float32)
    c["negf"] = np.tile(nf, (1, 4))
    c["negb"] = np.tile(nb, (1, 4))
    c["mprev"] = (i[:, None] >= i[None, :]).astype(np.float32)
    c["mnext"] = (i[:, None] <= i[None, :]).astype(np.float32)
    sw = np.zeros((128, 128), np.float32)
    for m in range(128):
        p = m + 32 if (m % 64) < 32 else m - 32
        sw[p, m] = 1.0
    c["pswap"] = sw
    t = np.arange(1024)
    rows = (t // 64).astype(np.float32)
    cols = (t % 64).astype(np.float32)
    inv = (10000.0 ** (-np.arange(32, dtype=np.float32) / 32.0)).astype(np.float32)
    cos = np.zeros((128, 1024), np.float32)
    sin = np.zeros((128, 1024), np.float32)
    for m in range(128):
        pos = rows if m < 64 else cols
        ang = (pos * inv[m % 32]).astype(np.float32)
        cos[m] = np.cos(ang)
        sgn = -1.0 if (m % 64) < 32 else 1.0
        sin[m] = sgn * np.sin(ang)
    c["cos"] = cos
    c["sin"] = sin

    def rc(Tn):
        tt = np.arange(Tn)
        out = np.zeros((4, Tn), np.float32)
        for g, w in enumerate((2, 4, 8, 16)):
            lo = np.clip(tt - w // 2, 0, Tn)
            hi = np.clip(tt - w // 2 + w, 0, Tn)
            out[g] = 1.0 / (hi - lo).astype(np.float32)
        return out
    c["rcp"] = rc(256)
    c["rcs"] = rc(1024)
    cc = np.arange(64)
    cs = np.clip(cc - 8, 0, 48)
    ok = (cc[:, None] >= cs[None, :]) & (cc[:, None] < cs[None, :] + 16)
    c["colok"] = ok.astype(np.float32)
    c["colneg"] = np.where(ok, 0.0, NEG).astype(np.float32)
    c["jrev"] = np.eye(64, dtype=np.float32)[::-1].copy()
    s0 = np.zeros((64, 128), np.float32)
    s1 = np.zeros((64, 128), np.float32)
    s0[cc, cc] = 1.0
    s1[cc, cc + 64] = 1.0
    c["sel0"] = s0
    c["sel1"] = s1
    c["negblk"] = np.full((64, 64), NEG, np.float32)
    c["zeros"] = np.zeros((128, 2048), np.float32)
    return c


CONST_SHAPES = {k: v.shape for k, v in _consts().items()}

def _na_ranges():
    rs = [min(max(r - 4, 0), 8) for r in range(16)]
    R = []
    for kr in range(16):
        rr = [r for r in range(16) if rs[r] <= kr <= rs[r] + 7]
        R.append((rr[0], rr[-1]))
    return R


NA_R = _na_ranges()


def build_program():
    nc = bass.Bass("TRN2", target_bir_lowering=False)

    def din(name, shape, dt=F32):
        return nc.dram_tensor(name, list(shape), dt, kind="ExternalInput").ap()

    def dout(name, shape, dt=F32):
        return nc.dram_tensor(name, list(shape), dt, kind="ExternalOutput").ap()

    def dscr(name, shape, dt=F32):
        return nc.dram_tensor(name, list(shape), dt, kind="Internal").ap()

    xin = [din("xp", [T, D]), din("xs", [T, D])]
    cak = din("cak", [2, 256, 256]); cav = din("cav", [2, 256, 256])
    cnk = din("cnk", [2, 256, 512]); cnv = din("cnv", [2, 256, 512])
    st_in = [din("sf", [2, 512, 128]), din("sb", [2, 512, 128])]
    cvec = din("cvec", [2, D])
    w_ada = din("w_ada", [2, D, 3 * D]); b_ada = din("b_ada", [2, 3 * D])
    w_in = din("w_in", [2, D, DP]); w_out = din("w_out", [2, D, D])
    ln_g = din("ln_g", [2, D]); ln_b = din("ln_b", [2, D])
    attn_sink = din("attn_sink", [2, 4])
    conv_w = din("ssm_conv_w", [2, 5, 1024]); conv_b = din("ssm_conv_b", [2, 1024])
    a_log = din("ssm_a_log", [2, 16]); dt_bias = din("ssm_dt_bias", [2, 16])
    ssm_d = din("ssm_d", [2, 8]); norm_w = din("ssm_norm_w", [2, 512])
    pool_w = din("pool_w", [2, 4, 128, 128]); pool_b = din("pool_b", [2, 512])
    pool_scale = din("pool_scale", [2, 512]); rpb = din("na_rpb", [2, 4 * 15 * 31])
    cst = {k: din("c_" + k, list(s)) for k, s in CONST_SHAPES.items()}

    yout = [dout("yp", [T, D]), dout("ys", [T, D])]
    o_ak = dout("o_ak", [4, 2, 256, 256]); o_av = dout("o_av", [4, 2, 256, 256])
    o_nk = dout("o_nk", [4, 2, 256, 512]); o_nv = dout("o_nv", [4, 2, 256, 512])
    o_st = [dout("o_sf", [4, 2, 512, 128]), dout("o_sb", [4, 2, 512, 128])]
    if DEBUG:
        dbg_mixed = dout("dbg_mixed", [4, 128, 16, T])

    y1 = [dscr("y1p", [T, D]), dscr("y1s", [T, D])]
    modscr = dscr("modscr", [2, 2, 3 * D])
    RPAD = 64
    rpbscr = dscr("rpbscr", [2, RPAD + 4 * 15 * 31 + RPAD])

    from contextlib import ExitStack
    es = ExitStack()
    with es:
        ndma = 24
        dma_sems = [es.enter_context(nc.semaphore(f"dq{i}")) for i in range(ndma)]
        NPH = 12
        prog_sems = {}
        for e in Sched.ENG:
            for ph in range(NPH):
                prog_sems[(e, ph)] = es.enter_context(nc.semaphore(f"pg_{e}{ph}"))
        S = Sched(nc, dma_sems, prog_sems)

        _nm = [0]

        def sb(name, shape, dt, stack=es):
            _nm[0] += 1
            return stack.enter_context(nc.sbuf_tensor(f"{name}_{_nm[0]}", list(shape), dt))

        psum = [es.enter_context(nc.psum_tensor(f"ps{i}", [128, 512], F32)) for i in range(8)]

        def PS():
            i = S.psn
            S.psn = (S.psn + 1) % 8
            return ("ps", i), psum[i]

        def mm(out, lhsT, rhs, start, stop, reads, writes):
            return S.op("pe", lambda e: e.matmul(out, lhsT, rhs, start=start, stop=stop,
                                                 skip_group_check=True), reads, writes)

        def tr(out, in_, ident, reads, writes):
            return S.op("pe", lambda e: e.transpose(out, in_, ident), reads, writes)

        def act(out, in_, func, reads, writes, bias=0.0, scale=1.0):
            return S.op("act", lambda e: e.activation(out, in_, func, bias=bias, scale=scale),
                        reads, writes)

        def tt(eng, out, in0, in1, op, reads, writes):
            return S.op(eng, lambda e: e.tensor_tensor(out, in0, in1, op), reads, writes)

        def ts(eng, out, in0, s1, s2, op0, op1, reads, writes):
            if s2 is None:
                return S.op(eng, lambda e: e.tensor_scalar(out, in0, s1, None, op0), reads, writes)
            return S.op(eng, lambda e: e.tensor_scalar(out, in0, s1, s2, op0, op1), reads, writes)

        def stt(eng, out, in0, scalar, in1, op0, op1, reads, writes):
            return S.op(eng, lambda e: e.scalar_tensor_tensor(out, in0, scalar, in1, op0, op1),
                        reads, writes)

        def cp(eng, out, in_, reads, writes):
            if eng == "act":
                return S.op("act", lambda e: e.activation(out, in_, AF.Copy), reads, writes)
            return S.op(eng, lambda e: e.tensor_copy(out, in_), reads, writes)

        def dma(q, out, in_, reads, writes, nonc=False, nobarrier=False):
            if nonc:
                return S.op(q, lambda e: e.dma_start(out=out, in_=in_, allow_slow_non_contiguous=True),
                            reads, writes, dma=True, nobarrier=nobarrier)
            return S.op(q, lambda e: e.dma_start(out=out, in_=in_), reads, writes, dma=True, nobarrier=nobarrier)

        evrr = [0]

        def evac(out, in_, reads, writes, scale=None):
            evrr[0] ^= 1
            if evrr[0]:
                if scale is None:
                    return cp("act", out, in_, reads, writes)
                return act(out, in_, AF.Copy, reads, writes, scale=scale)
            if scale is None:
                return cp("dve", out, in_, reads, writes)
            return ts("dve", out, in_, scale, None, ALU.mult, None, reads, writes)

        def cload(name, shape, dt, src=None, q=None):
            t_ = sb("k_" + name, shape, dt)
            src = cst[name] if src is None else src
            dma("pool" if dt == BF else "sp", t_[:], src, [], [("k", name, str(dt))])
            return t_

        identf = cload("ident", [128, 128], F32); identb = cload("ident", [128, 128], BF)
        onesf = cload("ones", [128, 128], F32); onesb = cload("ones", [128, 128], BF)
        trif = cload("trif", [128, 128], F32); trib = cload("trib", [128, 128], F32)
        negf = cload("negf", [128, 512], BF); negb = cload("negb", [128, 512], BF)
        mprev = cload("mprev", [128, 128], BF); mnext = cload("mnext", [128, 128], BF)
        pswap = cload("pswap", [128, 128], BF)
        cosT = cload("cos", [128, 1024], F32); sinT = cload("sin", [128, 1024], F32)
        colok = cload("colok", [64, 64], F32); colneg = cload("colneg", [64, 64], F32)
        jrev = cload("jrev", [64, 64], F32)
        sel0 = cload("sel0", [64, 128], BF); sel1 = cload("sel1", [64, 128], BF)
        negblk = cload("negblk", [64, 64], BF)
        KALL = [("k", n, str(d)) for n in CONST_SHAPES for d in (F32, BF)]

        mixedT = sb("mixedT", [128, 16, T], BF)
        big = sb("big", [128, 32768], BF)
        uT = big[:, 0:16384].rearrange("p (a b) -> p a b", a=16)
        NW = 4
        wbuf = [big[:, 16384 + i * 4096:16384 + (i + 1) * 4096].rearrange("p (a b) -> p a b", a=16) for i in range(NW)]
        wo = big[:, 0:32768].rearrange("p (a b) -> p a b", a=16)
        from collections import deque
        epsT = sb("epsT", [128, 1], F32)
        S.op("pool", lambda e: e.memset(epsT[:], EPS), [], ["epsT"])

        def rstd_op(dst, var, kk):
            act(dst, var, AF.Sqrt, kk + ["epsT"], kk, bias=epsT[:, 0:1])
            S.op("dve", lambda e: e.reciprocal(dst, dst), kk, kk)

        def stage(n):
            if STAGE <= n and not S.stopped:
                S.barrier(False)
                S.stopped = True

        try:
          stage(0)
          modT = sb("modT", [128, 2, 48, 2], F32)
          bT = sb("ada_bT", [128, 2, 48], F32)
          scb = sb("ada_scb", [128, 16, 2], BF)
          adab = [sb(f"adab{i}", [128, 16, 256], BF) for i in range(2)]
          with ExitStack() as s0:
              sc = sb("ada_sc", [128, 16, 2], F32, s0)
              for v in range(2):
                  dma("sp", sc[:, :, v], cvec[v, :].rearrange("(kc p) -> p kc", p=128), [], ["ada_sc"], nonc=True)
              for l_ in range(2):
                  dma("sp", bT[:, l_, :], b_ada[l_, :].rearrange("(c p) -> p c", p=128), [], ["ada_bT"], nonc=True)
              act(scb[:], sc[:], AF.Silu, ["ada_sc"], ["ada_scb"])

              def ada_mm(l_, jb, wk, wb):
                  pk, ps = PS()
                  for half in range(2):
                      for kc in range(16):
                          mm(ps[:, half * 2:half * 2 + 2], wb[:, kc, half * 128:(half + 1) * 128], scb[:, kc, :],
                             kc == 0, kc == 15, ["ada_scb", wk], [pk])
                  tt("dve", modT[:, l_, jb * 2:jb * 2 + 2, :], ps[:, 0:4].rearrange("p (a b) -> p a b", a=2),
                     bT[:, l_, jb * 2:jb * 2 + 2].unsqueeze(2).to_broadcast([128, 2, 2]), ALU.add,
                     [pk, "ada_bT"], [("modT", l_, jb)])

              def ada_gate_store(l_):
                  for v in range(2):
                      dma("sp", modscr[l_, v, 2 * D:3 * D].rearrange("(kc p) -> p kc", p=128), modT[:, l_, 32:48, v],
                          [("modT", l_, jb) for jb in range(16, 24)], ["modscr"], nonc=True)

              for jb in range(24):
                  i_ = jb % NW
                  wk = ("w", i_)
                  dma("pool", wbuf[i_][:], w_ada[0, :, jb * 256:(jb + 1) * 256].rearrange("(kc p) n -> p kc n", p=128), [], [wk])
                  ada_mm(0, jb, wk, wbuf[i_])
              ada_gate_store(0)
              ada1 = {"next": 0, "pending": [], "free": [0, 1]}

              def ada_consume():
                  jb, bi = ada1["pending"].pop(0)
                  ada_mm(1, jb, ("adab", bi), adab[bi])
                  ada1["free"].append(bi)

              def ada_step():
                  if len(ada1["pending"]) == 2 or (ada1["next"] >= 24 and ada1["pending"]):
                      ada_consume()
                  jb = ada1["next"]
                  if jb < 24 and ada1["free"]:
                      bi = ada1["free"].pop(0)
                      dma("pool", adab[bi][:], w_ada[1, :, jb * 256:(jb + 1) * 256].rearrange("(kc p) n -> p kc n", p=128),
                          [], [("adab", bi)], nobarrier=True)
                      ada1["pending"].append((jb, bi))
                      ada1["next"] = jb + 1

              zt = sb("zt", [1, 2048], F32, s0)
              dma("sp", zt[:], cst["zeros"][0:1, :], [], ["zt"])
              for l in range(2):
                  dma("sp", rpbscr[l, :].rearrange("(o n) -> o n", o=1), zt[0:1, 0:RPAD * 2 + 1860],
                      ["zt"], [("rpbscr", l)])
                  dma("sp", rpbscr[l, RPAD:RPAD + 1860].rearrange("(o n) -> o n", o=1),
                      rpb[l, :].rearrange("(o n) -> o n", o=1), [], [("rpbscr", l)])
              S.barrier()
          stage(1)

          for l in range(2):
              for g in range(2):
                  smp = (g == 1)
                  nseq = 1 if smp else 4
                  SL = T // nseq
                  xsrc = xin[g] if l == 0 else y1[g]
                  ydst = y1[g] if l == 0 else yout[g]
                  with ExitStack() as s1:
                      wcnt = [0]
                      PRE = deque()

                      def _issue_w(c0, ncols, nobarrier):
                          i = wcnt[0] % NW
                          wcnt[0] += 1
                          key = ("w", i)
                          dma("pool", wbuf[i][:, :, 0:ncols],
                              w_in[l, :, c0:c0 + ncols].rearrange("(kc p) n -> p kc n", p=128), [], [key],
                              nobarrier=nobarrier)
                          return key, wbuf[i]

                      def load_w(c0, ncols):
                          if PRE:
                              (cc, kb) = PRE.popleft()
                              assert cc == (c0, ncols), (cc, c0, ncols)
                              return kb
                          return _issue_w(c0, ncols, False)

                      def prefetch(lst, nobarrier=True):
                          assert not PRE
                          for (c0, n_) in lst:
                              PRE.append(((c0, n_), _issue_w(c0, n_, nobarrier)))

                      def UKS(tq):
                          return [("uT", tq, kc) for kc in range(16)]

                      def proj_fm(wkey, wb, j0, evac_fn):
                          pss = [PS(), PS()]
                          for kc in range(16):
                              for tc in range(2):
                                  pk, ps = pss[tc]
                                  mm(ps[:, :], wb[:, kc, j0:j0 + 128], uT[:, kc, tc * 512:(tc + 1) * 512],
                                     kc == 0, kc == 15, [wkey, ("uT", tc, kc)], [pk])
                          for tc in range(2):
                              pk, ps = pss[tc]
                              evac_fn(pk, ps, tc)

                      def proj_tm(wkey, wb, ncols, evac_fn, tiles=range(NT)):
                          for t in tiles:
                              pk, ps = PS()
                              for kc in range(16):
                                  mm(ps[:, 0:ncols], uT[:, kc, t * 128:(t + 1) * 128], wb[:, kc, 0:ncols],
                                     kc == 0, kc == 15, [wkey, ("uT", t // 4, kc)], [pk])
                              evac_fn(pk, ps, t)

                      with ExitStack() as s2:
                          scT = sb("scT", [128, 16], F32, s2)
                          shT = sb("shT", [128, 16], F32, s2)
                          xt = [sb(f"xt{i}", [128, D], F32, s2) for i in range(3)]
                          xn = [sb(f"xn{i}", [128, D], BF, s2) for i in range(4)]
                          st6 = [sb(f"st6{i}", [128, 4, 6], F32, s2) for i in range(3)]
                          mv = [sb(f"mv{i}", [128, 4], F32, s2) for i in range(3)]
                          MK = [("modT", l, jb) for jb in range(16)]
                          cp("dve", shT[:], modT[:, l, 0:16, g], MK, ["shT"])
                          ts("dve", scT[:], modT[:, l, 16:32, g], 1.0, None, ALU.add, None, MK, ["scT"])
                          prefetch([(C_QA, 256), (C_QA + 256, 256), (C_KA, 256), (C_VA, 256)], nobarrier=False)
                          for tq in range(2):
                              for i in range(4):
                                  t = tq * 4 + i
                                  b = t % 3
                                  X, ST, MV, XN = xt[b], st6[b], mv[b], xn[i]
                                  kx, kst, kmv, kxn = ("xt", b), ("st6", b), ("mv", b), ("xn", i)
                                  dma("sp", X[:], xsrc[t * 128:(t + 1) * 128, :], [("y1", g)] if l else [], [kx])
                                  for c4 in range(4):
                                      S.op("dve", (lambda e, X=X, ST=ST, c4=c4: e.bn_stats(ST[:, c4, :], X[:, c4 * 512:(c4 + 1) * 512])),
                                           [kx], [kst])
                                  S.op("dve", (lambda e, ST=ST, MV=MV: e.bn_aggr(MV[:, 0:2], ST[:].rearrange("p a b -> p (a b)"))),
                                       [kst], [kmv])
                                  rstd_op(MV[:, 2:3], MV[:, 1:2], [kmv])
                                  stt("dve", MV[:, 3:4], MV[:, 0:1], -1.0, MV[:, 2:3], ALU.mult, ALU.mult, [kmv], [kmv])
                                  act(XN[:], X[:], AF.Identity, [kx, kmv], [kxn], bias=MV[:, 3:4], scale=MV[:, 2:3])
                              for kp in range(8):
                                  pk, ps = PS()
                                  pb = ps[:].bitcast(BF)
                                  for kk in range(2):
                                      kc = kp * 2 + kk
                                      for i in range(4):
                                          tr(pb[:, kk * 512 + i * 128:kk * 512 + (i + 1) * 128], xn[i][:, kc * 128:(kc + 1) * 128],
                                             identb[:], [("xn", i)] + KALL[:2], [pk])
                                  for kk in range(2):
                                      kc = kp * 2 + kk
                                      dst = uT[:, kc, tq * 512:(tq + 1) * 512]
                                      src_ = pb[:, kk * 512:(kk + 1) * 512]
                                      if kp % 2 == 0:
                                          act(dst, src_, AF.Identity, [pk, "scT", "shT"], [("uT", tq, kc)],
                                              bias=shT[:, kc:kc + 1], scale=scT[:, kc:kc + 1])
                                      else:
                                          ts("dve", dst, src_, scT[:, kc:kc + 1], shT[:, kc:kc + 1], ALU.mult, ALU.add,
                                             [pk, "scT", "shT"], [("uT", tq, kc)])
                          S.barrier(False)
                      stage((l * 2 + g) * 10 + 2)

                      def attn_branch(pref, nh, nkv, cq, ck, cv, cg, fc0, o_k, o_v, use_sink, cache_k, cache_v, na):
                          G = nh // nkv
                          kvw = nkv * 128
                          with ExitStack() as s2:
                              qT = sb(pref + "qT", [128, nh, T], BF, s2)
                              kT = sb(pref + "kT", [128, nkv, T], BF, s2)
                              vtm = sb(pref + "v", [128, NT, kvw], BF, s2)
                              sgT = sb(pref + "sg", [128, nh, T], BF, s2)
                              stg = [sb(pref + f"stg{i}", [128, 256], F32, s2) for i in range(2)]
                              NPT = 10
                              pt = [sb(pref + f"pt{i}", [128, 512], BF, s2) for i in range(NPT)]
                              rcp = [sb(pref + f"rc{i}", [128, 512], F32, s2) for i in range(2)]
                              kq, kk_, kv_, ksg = pref + "qT", pref + "kT", pref + "v", pref + "sg"
                              if use_sink:
                                  sinkx = sb(pref + "sink", [128, 4], F32, s2)
                                  dma("sp", sinkx[:], attn_sink[l, :].partition_broadcast(128), [], [pref + "sink"])
                                  act(sinkx[:], sinkx[:], AF.Exp, [pref + "sink"], [pref + "sink"])
                              if pref == "a_":
                                  stage((l * 2 + g) * 10 + 2.02)

                              for j in range(nh // 2):
                                  wk, wb = load_w(cq + j * 256, 256)
                                  for hh in range(2):
                                      h = j * 2 + hh
                                      proj_fm(wk, wb, hh * 128,
                                              lambda pk, ps, tc, h=h: evac(qT[:, h, tc * 512:(tc + 1) * 512], ps[:, :],
                                                                           [pk], [(kq, h, tc)], scale=SCALE))
                              if pref == "a_":
                                  stage((l * 2 + g) * 10 + 2.05)
                              if not smp:
                                  kTf = sb(pref + "kTf", [128, nkv, T], F32, s2)
                                  stgk = [sb(pref + f"stgk{i}", [128, kvw], F32, s2) for i in range(2)]
                              for j in range(nkv // 2):
                                  wk, wb = load_w(ck + j * 256, 256)
                                  for hh in range(2):
                                      h = j * 2 + hh

                                      def ev_kf(pk, ps, tc, h=h):
                                          sl_ = slice(tc * 512, (tc + 1) * 512)
                                          if smp:
                                              evac(kT[:, h, sl_], ps[:, :], [pk], [(kk_, h, tc)])
                                          else:
                                              cp("dve", kTf[:, h, sl_], ps[:, :], [pk], [(pref + "kTf", h, tc)])
                                              cp("act", kT[:, h, sl_], kTf[:, h, sl_], [(pref + "kTf", h, tc)], [(kk_, h, tc)])
                                      proj_fm(wk, wb, hh * 128, ev_kf)
                              if not smp:
                                  for t in range(NT):
                                      pk, ps = PS()
                                      for h in range(nkv):
                                          tr(ps[:, h * 128:(h + 1) * 128], kTf[:, h, t * 128:(t + 1) * 128], identf[:],
                                             [(pref + "kTf", h, t // 4)] + KALL, [pk])
                                      b = t % 2
                                      cp("dve", stgk[b][:], ps[:, 0:kvw], [pk], [(pref + "stgk", b)])
                                      s_, tt_ = divmod(t, 2)
                                      dma("sp", o_k[s_, l, tt_ * 128:(tt_ + 1) * 128, :], stgk[b][:],
                                          [(pref + "stgk", b)], [pref + "ok"])
                              if pref == "a_":
                                  stage((l * 2 + g) * 10 + 2.1)
                              for j in range(nkv // 2):
                                  wk, wb = load_w(cv + j * 256, 256)

                                  def ev_v(pk, ps, t, j=j):
                                      if smp:
                                          evac(vtm[:, t, j * 256:(j + 1) * 256], ps[:, 0:256], [pk], [(kv_, t)])
                                      else:
                                          b = t % 2
                                          cp("dve", stg[b][:], ps[:, 0:256], [pk], [(pref + "stg", b)])
                                          cp("act", vtm[:, t, j * 256:(j + 1) * 256], stg[b][:], [(pref + "stg", b)], [(kv_, t)])
                                          s_, tt_ = divmod(t, 2)
                                          dma("sp", o_v[s_, l, tt_ * 128:(tt_ + 1) * 128, j * 256:(j + 1) * 256], stg[b][:],
                                              [(pref + "stg", b)], [pref + "ov"])
                                  proj_tm(wk, wb, 256, ev_v)
                              if pref == "a_":
                                  stage((l * 2 + g) * 10 + 2.15)
                              for j in range(nh // 2):
                                  wk, wb = load_w(cg + j * 256, 256)
                                  for hh in range(2):
                                      h = j * 2 + hh
                                      proj_fm(wk, wb, hh * 128,
                                              lambda pk, ps, tc, h=h: act(sgT[:, h, tc * 512:(tc + 1) * 512], ps[:, :], AF.Silu,
                                                                          [pk], [(ksg, h, tc)]))

                              if pref == "a_":
                                  stage((l * 2 + g) * 10 + 2.2)
                                  prefetch([(C_QD, 256), (C_QD + 256, 256), (C_KD, 256), (C_KD + 256, 256)])
                              else:
                                  prefetch([(C_PC, 256), (C_PC + 256, 256), (C_GC, 256), (C_GC + 256, 256)])
                              pti = [0]

                              def getpt():
                                  i = pti[0] % NPT
                                  pti[0] += 1
                                  return (pref + "pt", i), pt[i]

                              ric = [0]

                              def finish(h, q0, nq, ok_, ops_, dk, dps):
                                  ri = ric[0]
                                  ric[0] += 1
                                  R_ = rcp[ri % 2]
                                  rk = (pref + "rc", ri % 2)
                                  if use_sink:
                                      ts("dve", R_[:, 0:nq], dps[:, 0:nq], sinkx[:, h:h + 1], None, ALU.add, None, [dk, pref + "sink"], [rk])
                                      S.op("dve", lambda e: e.reciprocal(R_[:, 0:nq], R_[:, 0:nq]), [rk], [rk])
                                  else:
                                      S.op("dve", lambda e: e.reciprocal(R_[:, 0:nq], dps[:, 0:nq]), [dk], [rk])
                                  tt("dve", R_[:, 0:nq], ops_[:, 0:nq], R_[:, 0:nq], ALU.mult, [ok_, rk], [rk])
                                  tt("dve", mixedT[:, fc0 + h, q0:q0 + nq], R_[:, 0:nq], sgT[:, h, q0:q0 + nq], ALU.mult,
                                     [rk] + [(ksg, h, tc) for tc in range(2)], [("mx", fc0 + h)])

                              def pv(contrib, h, q0, nq):
                                  ok_, ops_ = PS()
                                  dk, dps = PS()
                                  n_ = len(contrib)
                                  for i_, (ptk, pap, lv, vk, c0, ncl) in enumerate(contrib):
                                      mm(ops_[:, c0:c0 + ncl], lv, pap, i_ == 0, i_ == n_ - 1, [ptk, vk], [ok_])
                                  for i_, (ptk, pap, lv, vk, c0, ncl) in enumerate(contrib):
                                      mm(dps[:, c0:c0 + ncl], onesb[:], pap, i_ == 0, i_ == n_ - 1, [ptk] + KALL, [dk])
                                  finish(h, q0, nq, ok_, ops_, dk, dps)

                              if not smp:
                                  for s_ in range(4):
                                      t0 = s_ * 256
                                      for kv in range(nkv):
                                          pts = []
                                          for kt in range(2):
                                              pk, ps = PS()
                                              for hh in range(G):
                                                  h = kv * G + hh
                                                  mm(ps[:, hh * 256:(hh + 1) * 256], kT[:, kv, t0 + kt * 128:t0 + (kt + 1) * 128],
                                                     qT[:, h, t0:t0 + 256], True, True,
                                                     [(kk_, kv, s_ // 2), (kq, h, s_ // 2)], [pk])
                                              ptk, P_ = getpt()
                                              act(P_[:, 0:G * 256], ps[:, 0:G * 256], AF.Exp, [pk], [ptk])
                                              pts.append((ptk, P_))
                                          ok_, ops_ = PS()
                                          dk, dps = PS()
                                          for kt in range(2):
                                              mm(ops_[:, 0:G * 256], vtm[:, s_ * 2 + kt, kv * 128:(kv + 1) * 128], pts[kt][1][:, 0:G * 256],
                                                 kt == 0, kt == 1, [(kv_, s_ * 2 + kt), pts[kt][0]], [ok_])
                                          for kt in range(2):
                                              mm(dps[:, 0:G * 256], onesb[:], pts[kt][1][:, 0:G * 256], kt == 0, kt == 1,
                                                 [pts[kt][0]] + KALL, [dk])
                                          for hh in range(G):
                                              h = kv * G + hh
                                              finish(h, t0, 256, ok_, ops_[:, hh * 256:(hh + 1) * 256], dk,
                                                     dps[:, hh * 256:(hh + 1) * 256])
                              else:
                                  kct = sb(pref + "kct", [128, 2, kvw], BF, s2)
                                  kcT = sb(pref + "kcT", [128, nkv, 256], BF, s2)
                                  vct = sb(pref + "vct", [128, 2, kvw], BF, s2)
                                  dma("pool", kct[:], cache_k[l].rearrange("(a p) n -> p a n", p=128), [], [pref + "kct"])
                                  dma("pool", vct[:], cache_v[l].rearrange("(a p) n -> p a n", p=128), [], [pref + "vct"])
                                  for kv in range(nkv):
                                      pk, ps = PS()
                                      pb = ps[:].bitcast(BF)
                                      for a in range(2):
                                          tr(pb[:, a * 128:(a + 1) * 128], kct[:, a, kv * 128:(kv + 1) * 128], identb[:],
                                             [pref + "kct"] + KALL, [pk])
                                      cp("dve", kcT[:, kv, :], pb[:, 0:256], [pk], [(pref + "kcT", kv)])

                                  def ctx_contrib(h, kv, qc):
                                      out = []
                                      for a in range(2):
                                          pk, ps = PS()
                                          mm(ps[:, :], kcT[:, kv, a * 128:(a + 1) * 128], qT[:, h, qc * 512:(qc + 1) * 512], True, True,
                                             [(pref + "kcT", kv), (kq, h, qc)], [pk])
                                          ptk, P_ = getpt()
                                          act(P_[:, 0:512], ps[:, :], AF.Exp, [pk], [ptk])
                                          out.append((ptk, P_[:, 0:512], vct[:, a, kv * 128:(kv + 1) * 128], pref + "vct", 0, 512))
                                      return out

                                  if not na:
                                      rtmp = [sb(pref + f"rt{i}", [128, 512], F32, s2) for i in range(2)]
                                      for (buf, nm, hh_) in [(qT, kq, nh), (kT, kk_, nkv)]:
                                          for h in range(hh_):
                                              for tc in range(2):
                                                  sl = slice(tc * 512, (tc + 1) * 512)
                                                  pk, ps = PS()
                                                  kk2 = (nm, h, tc)
                                                  mm(ps[:, :], pswap[:], buf[:, h, sl], True, True, [kk2] + KALL, [pk])
                                                  b = (h + tc) % 2
                                                  tt("dve", rtmp[b][:], ps[:, :], sinT[:, sl], ALU.mult, [pk] + KALL, [(pref + "rt", b)])
                                                  tt("dve", buf[:, h, sl], buf[:, h, sl], cosT[:, sl], ALU.mult, [kk2, pk] + KALL, [kk2])
                                                  tt("dve", buf[:, h, sl], buf[:, h, sl], rtmp[b][:], ALU.add, [kk2, (pref + "rt", b)], [kk2])
                                      for h in range(nh):
                                          kv = h // G
                                          for qc in range(2):
                                              contrib = ctx_contrib(h, kv, qc)
                                              for kt in range(4 * qc - 1, 4 * qc + 5):
                                                  if kt < 0 or kt > 7:
                                                      continue
                                                  b0 = max(4 * qc, kt - 1)
                                                  b1 = min(4 * qc + 3, kt + 1)
                                                  nb_ = b1 - b0 + 1
                                                  pk, ps = PS()
                                                  mm(ps[:, 0:nb_ * 128], kT[:, kv, kt * 128:(kt + 1) * 128],
                                                     qT[:, h, b0 * 128:(b1 + 1) * 128], True, True,
                                                     [(kk_, kv, kt // 4), (kq, h, qc)], [pk])
                                                  ptk, P_ = getpt()
                                                  act(P_[:, 0:nb_ * 128], ps[:, 0:nb_ * 128], AF.Exp, [pk], [ptk])
                                                  for bq in range(b0, b1 + 1):
                                                      o_ = (bq - b0) * 128
                                                      if bq == kt + 1:
                                                          tt("dve", P_[:, o_:o_ + 128], P_[:, o_:o_ + 128], mprev[:], ALU.mult,
                                                             [ptk] + KALL, [ptk])
                                                      elif bq == kt - 1:
                                                          tt("dve", P_[:, o_:o_ + 128], P_[:, o_:o_ + 128], mnext[:], ALU.mult,
                                                             [ptk] + KALL, [ptk])
                                                  contrib.append((ptk, P_[:, 0:nb_ * 128], vtm[:, kt, kv * 128:(kv + 1) * 128],
                                                                  (kv_, kt), (b0 - 4 * qc) * 128, nb_ * 128))
                                              pv(contrib, h, qc * 512, 512)
                                  else:
                                      bmraw = sb(pref + "bmraw", [64, 60, 64], F32, s2)
                                      bmd = sb(pref + "bmd", [64, 4, 15, 64], BF, s2)
                                      btmp = [sb(pref + "btmp0", [64, 8, 64], F32, s2)] * 2
                                      src = bass.AP(tensor=rpbscr.tensor, offset=l * (2 * RPAD + 1860) + RPAD - 48,
                                                    ap=[[1, 64], [31, 60], [1, 64]])
                                      dma("sp", bmraw[:], src, [("rpbscr", l)], [pref + "bmraw"])
                                      bi = 0
                                      for h in range(4):
                                          for (e0, ne) in [(0, 8), (8, 7)]:
                                              pk, ps = PS()
                                              for i_ in range(ne):
                                                  row = h * 15 + 14 - (e0 + i_)
                                                  mm(ps[0:64, i_ * 64:(i_ + 1) * 64], bmraw[:, row, :], jrev[:], True, True,
                                                     [pref + "bmraw"] + KALL, [pk])
                                              bt = btmp[bi % 2]
                                              bk = (pref + "btmp", 0)
                                              bi += 1
                                              tt("dve", bt[:, 0:ne, :], ps[0:64, 0:ne * 64].rearrange("p (a b) -> p a b", a=ne),
                                                 colok[:].unsqueeze(1).to_broadcast([64, ne, 64]), ALU.mult, [pk] + KALL, [bk])
                                              tt("dve", bmd[:, h, e0:e0 + ne, :], bt[:, 0:ne, :],
                                                 colneg[:].unsqueeze(1).to_broadcast([64, ne, 64]), ALU.add, [bk] + KALL, [(pref + "bmd", h)])
                                      for h in range(nh):
                                          for qc in range(2):
                                              contrib = ctx_contrib(h, h, qc)
                                              for kt in range(8):
                                                  (a0, b0_) = NA_R[2 * kt]
                                                  (a1, b1_) = NA_R[2 * kt + 1]
                                                  r1 = max(min(a0, a1), 8 * qc)
                                                  r2 = min(max(b0_, b1_), 8 * qc + 7)
                                                  if r1 > r2:
                                                      continue
                                                  n = r2 - r1 + 1
                                                  pk, ps = PS()
                                                  seq_ = []
                                                  seq_.append((ps[:, 0:n * 64], kT[:, h, kt * 128:(kt + 1) * 128], qT[:, h, r1 * 64:(r2 + 1) * 64],
                                                               [(kk_, h, kt // 4), (kq, h, qc)]))
                                                  for j, sel in ((0, sel0), (1, sel1)):
                                                      kr = 2 * kt + j
                                                      (a, b) = NA_R[kr]
                                                      va, vb = max(a, r1), min(b, r2)
                                                      if va <= vb:
                                                          e1 = 7 - kr + va
                                                          seq_.append((ps[:, (va - r1) * 64:(vb - r1 + 1) * 64], sel[:],
                                                                       bmd[:, h, e1:e1 + (vb - va + 1), :].rearrange("p a b -> p (a b)"),
                                                                       [(pref + "bmd", h)] + KALL))
                                                      for r in range(r1, r2 + 1):
                                                          if r < a or r > b:
                                                              seq_.append((ps[:, (r - r1) * 64:(r - r1 + 1) * 64], sel[:], negblk[:], KALL))
                                                  for i_, (o_, l_, r_, rd) in enumerate(seq_):
                                                      mm(o_, l_, r_, i_ == 0, i_ == len(seq_) - 1, rd, [pk])
                                                  ptk, P_ = getpt()
                                                  act(P_[:, 0:n * 64], ps[:, 0:n * 64], AF.Exp, [pk], [ptk])
                                                  contrib.append((ptk, P_[:, 0:n * 64], vtm[:, kt, h * 128:(h + 1) * 128], (kv_, kt),
                                                                  (r1 - 8 * qc) * 64, n * 64))
                                              pv(contrib, h, qc * 512, 512)
                              S.barrier(False)

                      attn_branch("a_", 4, 2, C_QA, C_KA, C_VA, C_GA, 0, o_ak, o_av, True, cak, cav, False)
                      stage((l * 2 + g) * 10 + 2.5)
                      if FEAT >= 1:
                          attn_branch("d_", 4, 4, C_QD, C_KD, C_VD, C_GD, 12, o_nk, o_nv, False, cnk, cnv, smp)
                      else:
                          for fc in range(12, 16):
                              S.op("pool", (lambda e, fc=fc: e.memset(mixedT[:, fc, :], 0.0)), [], [("mx", fc)])

                      stage((l * 2 + g) * 10 + 2.6)
                      if FEAT >= 2:
                          with ExitStack() as s2:
                              SLp = SL + 16
                              pcT = sb("c_pc", [128, 4, nseq, SLp], F32, s2)
                              sgc = sb("c_sg", [128, 4, T], BF, s2)
                              pooled = sb("c_pl", [128, 4, T], BF, s2)
                              wA = sb("c_wa", [128, nseq, SLp], F32, s2)
                              wB = sb("c_wb", [128, nseq, SLp], F32, s2)
                              rct = sb("c_rc", [128, 4, SL], F32, s2)
                              pw = sb("c_pw", [128, 4, 128], BF, s2)
                              pbs = sb("c_pb", [128, 4], F32, s2)
                              psc = sb("c_ps", [128, 4], F32, s2)
                              ctmp = [sb(f"c_tmp{i}", [128, 512], F32, s2) for i in range(2)]
                              S.op("pool", (lambda e, pcT=pcT: e.memset(pcT[:], 0.0)), [], ["c_pc"])
                              dma("sp", rct[:], (cst["rcs"] if smp else cst["rcp"]).partition_broadcast(128), [], ["c_rc"])
                              dma("pool", pw[:], pool_w[l].rearrange("g c d -> c g d"), [], ["c_pw"])
                              dma("sp", pbs[:], pool_b[l, :].rearrange("(g p) -> p g", p=128), [], ["c_pb"], nonc=True)
                              dma("sp", psc[:], pool_scale[l, :].rearrange("(g p) -> p g", p=128), [], ["c_ps"], nonc=True)
                              for j in range(2):
                                  wk, wb = load_w(C_PC + j * 256, 256)
                                  for hh in range(2):
                                      gq = j * 2 + hh

                                      def ev_pc(pk, ps, tc, gq=gq):
                                          if smp:
                                              dst = pcT[:, gq, 0, 8 + tc * 512:8 + (tc + 1) * 512]
                                              srcp = ps[:, :]
                                          else:
                                              dst = pcT[:, gq, 2 * tc:2 * tc + 2, 8:8 + 256]
                                              srcp = ps[:, :].rearrange("p (s t) -> p s t", s=2)
                                          evac(dst, srcp, [pk, "c_pc"], [("c_pcg", gq)])
                                      proj_fm(wk, wb, hh * 128, ev_pc)
                              for j in range(2):
                                  wk, wb = load_w(C_GC + j * 256, 256)
                                  for hh in range(2):
                                      gq = j * 2 + hh
                                      proj_fm(wk, wb, hh * 128,
                                              lambda pk, ps, tc, gq=gq: act(sgc[:, gq, tc * 512:(tc + 1) * 512], ps[:, :], AF.Silu,
                                                                            [pk], [("c_sg", gq, tc)]))
                              prefetch([(C_XBC, 256), (C_XBC + 256, 256), (C_XBC + 512, 256), (C_XBC + 768, 256)])
                              for gq in range(4):
                                  P_ = pcT[:, gq]
                                  kp = ("c_pcg", gq)
                                  tt("dve", wA[:, :, 1:SLp], P_[:, :, 0:SLp - 1], P_[:, :, 1:SLp], ALU.add, [kp], ["c_wa"])
                                  cur, ck_ = wA, "c_wa"
                                  if gq >= 1:
                                      tt("dve", wB[:, :, 2:SLp - 1], wA[:, :, 1:SLp - 2], wA[:, :, 3:SLp], ALU.add, ["c_wa"], ["c_wb"])
                                      cur, ck_ = wB, "c_wb"
                                  if gq >= 2:
                                      tt("dve", wA[:, :, 4:SLp - 3], wB[:, :, 2:SLp - 5], wB[:, :, 6:SLp - 1], ALU.add, ["c_wb"], ["c_wa"])
                                      cur, ck_ = wA, "c_wa"
                                  if gq >= 3:
                                      tt("dve", wB[:, :, 8:SLp - 7], wA[:, :, 4:SLp - 11], wA[:, :, 12:SLp - 3], ALU.add, ["c_wa"], ["c_wb"])
                                      cur, ck_ = wB, "c_wb"
                                  tt("dve", cur[:, :, 8:8 + SL], cur[:, :, 8:8 + SL],
                                     rct[:, gq, :].unsqueeze(1).to_broadcast([128, nseq, SL]), ALU.mult, [ck_, "c_rc"], [ck_])
                                  tt("dve", pooled[:, gq, :].rearrange("p (s t) -> p s t", s=nseq), cur[:, :, 8:8 + SL],
                                     P_[:, :, 8:8 + SL], ALU.subtract, [ck_, kp], [("c_pl", gq)])
                                  for tc in range(2):
                                      pk, ps = PS()
                                      mm(ps[:, :], pw[:, gq, :], pooled[:, gq, tc * 512:(tc + 1) * 512], True, True,
                                         [("c_pl", gq), "c_pw"], [pk])
                                      b = tc % 2
                                      ts("dve", ctmp[b][:], ps[:, :], pbs[:, gq:gq + 1], psc[:, gq:gq + 1], ALU.add, ALU.mult,
                                         [pk, "c_pb", "c_ps"], [("c_tmp", b)])
                                      tt("dve", mixedT[:, 8 + gq, tc * 512:(tc + 1) * 512], ctmp[b][:], sgc[:, gq, tc * 512:(tc + 1) * 512],
                                         ALU.mult, [("c_tmp", b), ("c_sg", gq, tc)], [("mx", 8 + gq)])
                              S.barrier(False)
                      else:
                          for fc in range(8, 12):
                              S.op("pool", (lambda e, fc=fc: e.memset(mixedT[:, fc, :], 0.0)), [], [("mx", fc)])

                      stage((l * 2 + g) * 10 + 2.7)
                      if FEAT >= 3:
                          with ExitStack() as s2:
                              xsT = sb("b_xsT", [128, 4, T], BF, s2)
                              bcT = sb("b_bcT", [128, 4, T], BF, s2)
                              xs = sb("b_xs", [128, NT, 512], BF, s2)
                              btm = sb("b_btm", [128, NT, 256], BF, s2)
                              sz = sb("b_sz", [128, NT, 512], BF, s2)
                              dtr = sb("b_dt", [128, NT, 16], F32, s2)
                              dta = sb("b_dta", [128, NT, 16], F32, s2)
                              ysum = sb("b_y", [128, NT, 512], F32, s2)
                              cw = sb("b_cw", [128, 8, 5], F32, s2)
                              cb = sb("b_cb", [128, 8], F32, s2)
                              dtb = sb("b_dtb", [128, 16], F32, s2)
                              nga = sb("b_nga", [128, 16], F32, s2)
                              dbc = sb("b_dbc", [128, 8], F32, s2)
                              nwb = sb("b_nwb", [128, 512], F32, s2)
                              for k5 in range(5):
                                  dma("sp", cw[:, :, k5], conv_w[l, k5, :].rearrange("(b p) -> p b", p=128), [], ["b_cw"], nonc=True)
                              dma("sp", cb[:], conv_b[l, :].rearrange("(b p) -> p b", p=128), [], ["b_cb"], nonc=True)
                              dma("sp", dtb[:], dt_bias[l, :].partition_broadcast(128), [], ["b_dtb"])
                              dma("sp", nga[:], a_log[l, :].partition_broadcast(128), [], ["b_nga"])
                              dma("sp", dbc[:], ssm_d[l, :].partition_broadcast(128), [], ["b_dbc"])
                              dma("sp", nwb[:], norm_w[l, :].partition_broadcast(128), [], ["b_nwb"])
                              act(nga[:], nga[:], AF.Exp, ["b_nga"], ["b_nga"])
                              ts("dve", nga[:], nga[:], -1.0, None, ALU.mult, None, ["b_nga"], ["b_nga"])
                              with ExitStack() as s3:
                                  raw = [sb(f"b_raw{i}", [128, nseq, SL + 4], F32, s3) for i in range(2)]
                                  acc = [sb(f"b_acc{i}", [128, nseq, SL], F32, s3) for i in range(2)]
                                  for i in range(2):
                                      S.op("pool", (lambda e, r=raw[i]: e.memset(r[:], 0.0)), [], [("b_rawz", i)])
                                  for j in range(4):
                                      wk, wb = load_w(C_XBC + j * 256, 256)
                                      for hh in range(2):
                                          blk = j * 2 + hh
                                          rb = blk % 2
                                          R_, A_ = raw[rb], acc[rb]

                                          def ev_x(pk, ps, tc, R_=R_, rb=rb):
                                              if smp:
                                                  dst = R_[:, 0, 2 + tc * 512:2 + (tc + 1) * 512]
                                                  srcp = ps[:, :]
                                              else:
                                                  dst = R_[:, 2 * tc:2 * tc + 2, 2:2 + 256]
                                                  srcp = ps[:, :].rearrange("p (s t) -> p s t", s=2)
                                              evac(dst, srcp, [pk, ("b_rawz", rb)], [("b_raw", rb, tc)])
                                          proj_fm(wk, wb, hh * 128, ev_x)
                                          rk = [("b_raw", rb, 0), ("b_raw", rb, 1)]
                                          ak = ("b_acc", rb)
                                          ts("dve", A_[:], R_[:, :, 2:2 + SL], cw[:, blk, 2:3], None, ALU.mult, None, rk + ["b_cw"], [ak])
                                          for k in (0, 1, 3, 4):
                                              stt("dve", A_[:], R_[:, :, k:k + SL], cw[:, blk, k:k + 1], A_[:], ALU.mult, ALU.add,
                                                  rk + [ak, "b_cw"], [ak])
                                          if blk < 4:
                                              dst = xsT[:, blk, :].rearrange("p (s t) -> p s t", s=nseq)
                                              dk_ = ("b_xsT", blk)
                                          else:
                                              dst = bcT[:, blk - 4, :].rearrange("p (s t) -> p s t", s=nseq)
                                              dk_ = ("b_bcT", blk - 4)
                                          act(dst, A_[:], AF.Silu, [ak, "b_cb"], [dk_], bias=cb[:, blk:blk + 1])
                              for j in range(2):
                                  wk, wb = load_w(C_Z + j * 256, 256)
                                  proj_tm(wk, wb, 256, lambda pk, ps, t, j=j: act(sz[:, t, j * 256:(j + 1) * 256], ps[:, 0:256], AF.Silu,
                                                                                  [pk], [("b_sz", t)]))
                              wk, wb = load_w(C_DT, 16)
                              proj_tm(wk, wb, 16, lambda pk, ps, t: tt("dve", dtr[:, t, :], ps[:, 0:16], dtb[:], ALU.add,
                                                                        [pk, "b_dtb"], ["b_dt"]))
                              act(dtr[:], dtr[:], AF.Exp, ["b_dt"], ["b_dt"])
                              act(dtr[:], dtr[:], AF.Ln, ["b_dt"], ["b_dt"], bias=1.0)
                              tt("dve", dta[:], dtr[:], nga[:].unsqueeze(1).to_broadcast([128, NT, 16]), ALU.mult, ["b_dt", "b_nga"], ["b_dta"])
                              for q4 in range(4):
                                  if q4 < 2:
                                      wr = [("uT", tq, kc) for tq in range(2) for kc in range(q4 * 8, (q4 + 1) * 8)]
                                  elif q4 == 2:
                                      wr = [("w", 0), ("w", 1)]
                                  else:
                                      wr = [("w", 2), ("w", 3)]
                                  dma("pool", wo[:, q4 * 4:(q4 + 1) * 4, :],
                                      w_out[l, q4 * 512:(q4 + 1) * 512, :].rearrange("(kc p) n -> p kc n", p=128),
                                      [], wr + [("wo", q4)], nobarrier=True)
                              for t in range(NT):
                                  tsl = slice(t * 128, (t + 1) * 128)
                                  pk, ps = PS()
                                  pb = ps[:].bitcast(BF)
                                  for blk in range(4):
                                      tr(pb[:, blk * 128:(blk + 1) * 128], xsT[:, blk, tsl], identb[:], [("b_xsT", blk)] + KALL, [pk])
                                  evac(xs[:, t, :], pb[:, 0:512], [pk], [("b_xs", t)])
                                  pk, ps = PS()
                                  pb = ps[:].bitcast(BF)
                                  for gq in range(2):
                                      tr(pb[:, gq * 128:(gq + 1) * 128], bcT[:, gq, tsl], identb[:], [("b_bcT", gq)] + KALL, [pk])
                                  evac(btm[:, t, :], pb[:, 0:256], [pk], [("b_btm", t)])
                              with ExitStack() as s3:
                                  dec = [sb(f"b_dec{i}", [128, 8, 128], BF, s3) for i in range(2)]
                                  MT = [sb(f"b_mt{i}", [128, 8, 128], BF, s3) for i in range(2)]
                                  xdt = [sb(f"b_xdt{i}", [128, 8, 64], BF, s3) for i in range(2)]
                                  xdw = [sb(f"b_xdw{i}", [128, 8, 64], BF, s3) for i in range(2)]
                                  cbs = [sb(f"b_cbs{i}", [128, 2, 128], F32, s3) for i in range(2)]
                                  lcs = [sb(f"b_lcs{i}", [128, 48], F32, s3) for i in range(2)]
                                  tmp = [sb(f"b_tmp{i}", [128, 512], F32, s3) for i in range(2)]
                                  hT = sb("b_hT", [128, 512], F32, s3)
                                  hTb = sb("b_hTb", [128, 512], BF, s3)
                                  hst = sb("b_hst", [128, 4, 128], F32, s3)
                                  nch = SL // 128
                                  items = []
                                  for s_ in range(nseq):
                                      for d in range(2):
                                          order = list(range(nch)) if d == 0 else list(range(nch - 1, -1, -1))
                                          for oi, c in enumerate(order):
                                              items.append((s_, d, c, oi == 0, oi == nch - 1))
                                  ctxs = {}

                                  def alpha(n):
                                      s_, d, c, first, last = items[n]
                                      tri = trif if d == 0 else trib
                                      neg4 = negf if d == 0 else negb
                                      t = s_ * nch + c
                                      tsl = slice(t * 128, (t + 1) * 128)
                                      b = n % 2
                                      if l == 0:
                                          ada_step()
                                      a_ = dta[:, t, d * 8:(d + 1) * 8]
                                      dt_ = dtr[:, t, d * 8:(d + 1) * 8]
                                      DE, M_, XD, XW, CB, LC = dec[b], MT[b], xdt[b], xdw[b], cbs[b], lcs[b]
                                      kDE, kM, kXD, kXW, kCB, kLC = [(n_, b) for n_ in
                                                                     ("b_dec", "b_mt", "b_xdt", "b_xdw", "b_cbs", "b_lcs")]
                                      pkL, psL = PS()
                                      mm(psL[:, 0:8], tri[:], a_, True, True, ["b_dta"] + KALL, [pkL])
                                      mm(psL[:, 8:16], onesf[:], a_, True, True, ["b_dta"] + KALL, [pkL])
                                      cp("dve", LC[:, 0:16], psL[:, 0:16], [pkL], [kLC])
                                      ts("dve", LC[:, 16:24], LC[:, 0:8], -1.0, None, ALU.mult, None, [kLC], [kLC])
                                      tt("dve", LC[:, 32:40], LC[:, 8:16], LC[:, 0:8], ALU.subtract, [kLC], [kLC])
                                      act(LC[:, 24:32], LC[:, 0:8], AF.Exp, [kLC], [kLC])
                                      act(LC[:, 32:40], LC[:, 32:40], AF.Exp, [kLC], [kLC])
                                      act(LC[:, 40:48], LC[:, 8:16], AF.Exp, [kLC], [kLC])
                                      for hg in range(2):
                                          pkD, psD = PS()
                                          mm(psD[:, :], identb[:], neg4[:], True, False, KALL, [pkD])
                                          for hh in range(4):
                                              h = hg * 4 + hh
                                              mm(psD[:, hh * 128:(hh + 1) * 128], a_[:, h:h + 1].to_broadcast([128, 128]), tri[:],
                                                 False, hh == 3, ["b_dta"] + KALL, [pkD])
                                          for hh in range(4):
                                              h = hg * 4 + hh
                                              act(DE[:, h, :], psD[:, hh * 128:(hh + 1) * 128], AF.Exp, [pkD, kLC], [kDE],
                                                  bias=LC[:, 16 + h:17 + h])
                                      pkC, psC = PS()
                                      for gq in range(2):
                                          mm(psC[:, gq * 128:(gq + 1) * 128], bcT[:, gq, tsl], bcT[:, 2 + gq, tsl], True, True,
                                             [("b_bcT", gq), ("b_bcT", 2 + gq)], [pkC])
                                      evac(CB[:].rearrange("p a b -> p (a b)"), psC[:, 0:256], [pkC], [kCB])
                                      for gq in range(2):
                                          tt("dve", M_[:, gq * 4:(gq + 1) * 4, :], DE[:, gq * 4:(gq + 1) * 4, :],
                                             CB[:, gq, :].unsqueeze(1).to_broadcast([128, 4, 128]), ALU.mult, [kDE, kCB], [kM])
                                      tt("pool", XD[:], xs[:, t, :].rearrange("p (h q) -> p h q", h=8),
                                         dt_.unsqueeze(2).to_broadcast([128, 8, 64]), ALU.mult, [("b_xs", t), "b_dt"], [kXD])
                                      tt("pool", XW[:], XD[:], LC[:, 32:40].unsqueeze(2).to_broadcast([128, 8, 64]), ALU.mult,
                                         [kXD, kLC], [kXW])

                                  def beta(n):
                                      s_, d, c, first, last = items[n]
                                      t = s_ * nch + c
                                      tsl = slice(t * 128, (t + 1) * 128)
                                      b = n % 2
                                      M_, XD, XW, LC, TM = MT[b], xdt[b], xdw[b], lcs[b], tmp[b]
                                      kM, kXD, kXW, kLC, kTM = [(n_, b) for n_ in ("b_mt", "b_xdt", "b_xdw", "b_lcs", "b_tmp")]
                                      if first:
                                          if smp:
                                              dma("sp", hst[:], st_in[d][l].rearrange("(a p) n -> p a n", p=128), [], ["b_hst"])
                                              pk, ps = PS()
                                              for a in range(4):
                                                  tr(ps[:, a * 128:(a + 1) * 128], hst[:, a, :], identf[:], ["b_hst"] + KALL, [pk])
                                              cp("dve", hT[:], ps[:, :], [pk], ["b_hT"])
                                              cp("act", hTb[:], hT[:], ["b_hT"], ["b_hTb"])
                                          else:
                                              S.op("pool", (lambda e, hT=hT: e.memset(hT[:], 0.0)), [], ["b_hT"])
                                              S.op("pool", (lambda e, hTb=hTb: e.memset(hTb[:], 0.0)), [], ["b_hTb"])
                                      pkY, psY = PS()
                                      for h in range(8):
                                          mm(psY[:, h * 64:(h + 1) * 64], M_[:, h, :], XD[:, h, :], True, True, [kM, kXD], [pkY])
                                      pkY2, psY2 = PS()
                                      for gq in range(2):
                                          mm(psY2[:, gq * 256:(gq + 1) * 256], bcT[:, 2 + gq, tsl], hTb[:, gq * 256:(gq + 1) * 256],
                                             True, True, [("b_bcT", 2 + gq), "b_hTb"], [pkY2])
                                      tt("dve", TM[:].rearrange("p (h q) -> p h q", h=8), psY2[:, :].rearrange("p (h q) -> p h q", h=8),
                                         LC[:, 24:32].unsqueeze(2).to_broadcast([128, 8, 64]), ALU.mult, [pkY2, kLC], [kTM])
                                      if d == 0:
                                          tt("dve", ysum[:, t, :], psY[:, :], TM[:], ALU.add, [pkY, kTM], [("b_y", t)])
                                      else:
                                          tt("dve", TM[:], psY[:, :], TM[:], ALU.add, [pkY, kTM], [kTM])
                                          tt("pool", ysum[:, t, :], ysum[:, t, :], TM[:], ALU.add, [kTM, ("b_y", t)], [("b_y", t)])
                                      pkS, psS = PS()
                                      for gq in range(2):
                                          mm(psS[:, gq * 256:(gq + 1) * 256], btm[:, t, gq * 128:(gq + 1) * 128],
                                             XW[:, gq * 4:(gq + 1) * 4, :].rearrange("p a b -> p (a b)"), True, True,
                                             [("b_btm", t), kXW], [pkS])
                                      tt("dve", hT[:].rearrange("p (h q) -> p h q", h=8), hT[:].rearrange("p (h q) -> p h q", h=8),
                                         LC[:, 40:48].unsqueeze(2).to_broadcast([128, 8, 64]), ALU.mult, ["b_hT", kLC], ["b_hT"])
                                      tt("dve", hT[:], hT[:], psS[:, :], ALU.add, ["b_hT", pkS], ["b_hT"])
                                      cp("act", hTb[:], hT[:], ["b_hT"], ["b_hTb"])
                                      if last and not smp:
                                          pk, ps = PS()
                                          for a in range(4):
                                              tr(ps[:, a * 128:(a + 1) * 128], hT[:, a * 128:(a + 1) * 128], identf[:], ["b_hT"] + KALL, [pk])
                                          cp("dve", hst[:].rearrange("p a b -> p (a b)"), ps[:, :], [pk], ["b_hst"])
                                          dma("sp", o_st[d][s_, l].rearrange("(a p) n -> p a n", p=128), hst[:], ["b_hst"], ["o_st"])

                                  alpha(0)
                                  for n in range(len(items)):
                                      if n + 1 < len(items):
                                          alpha(n + 1)
                                      beta(n)
                                  ob = [dec[i][:, 0:4, :].rearrange("p a b -> p (a b)") for i in range(2)]
                                  st6 = [sb(f"b_st{i}", [128, 8], F32, s3) for i in range(2)]
                                  for t in range(NT):
                                      b = t % 2
                                      TM, OB, ST = tmp[b], ob[b], st6[b]
                                      kTM, kOB, kST = ("b_tmp", b), ("b_ob", b), ("b_st", b)
                                      yk = ("b_y", t)
                                      tt("pool", TM[:].rearrange("p (h q) -> p h q", h=8), xs[:, t, :].rearrange("p (h q) -> p h q", h=8),
                                         dbc[:].unsqueeze(2).to_broadcast([128, 8, 64]), ALU.mult, [("b_xs", t), "b_dbc"], [kTM])
                                      tt("dve", ysum[:, t, :], ysum[:, t, :], TM[:], ALU.add, [yk, kTM], [yk])
                                      tt("dve", ysum[:, t, :], ysum[:, t, :], sz[:, t, :], ALU.mult, [yk, ("b_sz", t)], [yk])
                                      S.op("dve", (lambda e, ST=ST, t=t, ysum=ysum: e.bn_stats(ST[:, 0:6], ysum[:, t, :])), [yk], [kST])
                                      S.op("dve", (lambda e, ST=ST: e.bn_aggr(ST[:, 6:8], ST[:, 0:6])), [kST], [kST])
                                      stt("dve", ST[:, 0:1], ST[:, 6:7], ST[:, 6:7], ST[:, 7:8], ALU.mult, ALU.add, [kST], [kST])
                                      rstd_op(ST[:, 1:2], ST[:, 0:1], [kST])
                                      stt("dve", OB[:], ysum[:, t, :], ST[:, 1:2], nwb[:], ALU.mult, ALU.mult, [yk, kST, "b_nwb"], [kOB])
                                      pk, ps = PS()
                                      pb = ps[:].bitcast(BF)
                                      for blk in range(4):
                                          tr(pb[:, blk * 128:(blk + 1) * 128], OB[:, blk * 128:(blk + 1) * 128], identb[:], [kOB] + KALL, [pk])
                                      evac(mixedT[:, 4:8, t * 128:(t + 1) * 128], pb[:, 0:512].rearrange("p (a b) -> p a b", a=4),
                                           [pk], [("mx", 4 + i) for i in range(4)])
                              S.barrier(False)
                      else:
                          for fc in range(4, 8):
                              S.op("pool", (lambda e, fc=fc: e.memset(mixedT[:, fc, :], 0.0)), [], [("mx", fc)])

                      S.barrier(False)
                      if DEBUG and l == 0:
                          dma("pool", dbg_mixed[l * 2 + g], mixedT[:], [("mx", i) for i in range(16)], ["dbg"])
                          S.barrier(False)
                      stage((l * 2 + g) * 10 + 3)

                  S.barrier(False)
                  if S.stopped:
                      break

                  stage((l * 2 + g) * 10 + 4)
                  if S.stopped:
                      break
                  with ExitStack() as s1:
                      gt = sb("gt", [128, D], F32, s1)
                      lg = sb("lg", [128, D], F32, s1)
                      lb = sb("lb", [128, D], F32, s1)
                      xt = [sb(f"oxt{i}", [128, D], F32, s1) for i in range(3)]
                      rr = [sb(f"orr{i}", [128, D], F32, s1) for i in range(2)]
                      st6 = [sb(f"ost6{i}", [128, 4, 6], F32, s1) for i in range(2)]
                      mv = [sb(f"omv{i}", [128, 4], F32, s1) for i in range(2)]
                      dma("sp", gt[:], modscr[l, g, 2 * D:3 * D].partition_broadcast(128), ["modscr"], ["gt"])
                      dma("sp", lg[:], ln_g[l, :].partition_broadcast(128), [], ["lg"])
                      dma("sp", lb[:], ln_b[l, :].partition_broadcast(128), [], ["lb"])
                      def xload(t_):
                          dma("sp", xt[t_ % 3][:], xsrc[t_ * 128:(t_ + 1) * 128, :], [("y1", g)] if l else [], [("oxt", t_ % 3)])

                      xload(0)
                      xload(1)
                      for t in range(NT):
                          b = t % 2
                          X, R_, ST, MV = xt[t % 3], rr[b], st6[b], mv[b]
                          kx, kr, kst, kmv = ("oxt", t % 3), ("orr", b), ("ost6", b), ("omv", b)
                          if t + 2 < NT:
                              xload(t + 2)
                          act(X[:], X[:], AF.Copy, [kx], [kx], scale=ALPHA)
                          for c4 in range(4):
                              pk, ps = PS()
                              for kc in range(16):
                                  mm(ps[:, :], mixedT[:, kc, t * 128:(t + 1) * 128], wo[:, kc, c4 * 512:(c4 + 1) * 512],
                                     kc == 0, kc == 15, [("mx", kc), ("wo", kc // 4)], [pk])
                              csl = slice(c4 * 512, (c4 + 1) * 512)
                              tt("dve", R_[:, csl], ps[:, :], gt[:, csl], ALU.mult, [pk, "gt"], [(kr, c4)])
                              tt("dve", R_[:, csl], R_[:, csl], X[:, csl], ALU.add, [kx, (kr, c4)], [(kr, c4)])
                              S.op("dve", (lambda e, R_=R_, ST=ST, c4=c4, csl=csl: e.bn_stats(ST[:, c4, :], R_[:, csl])),
                                   [(kr, c4)], [kst])
                          RK = [(kr, c4) for c4 in range(4)]
                          S.op("dve", (lambda e, ST=ST, MV=MV: e.bn_aggr(MV[:, 0:2], ST[:].rearrange("p a b -> p (a b)"))),
                               [kst], [kmv])
                          rstd_op(MV[:, 2:3], MV[:, 1:2], [kmv])
                          stt("dve", MV[:, 3:4], MV[:, 0:1], -1.0, MV[:, 2:3], ALU.mult, ALU.mult, [kmv], [kmv])
                          act(R_[:], R_[:], AF.Identity, RK + [kmv], RK, bias=MV[:, 3:4], scale=MV[:, 2:3])
                          tt("dve", R_[:], R_[:], lg[:], ALU.mult, RK + ["lg"], RK)
                          tt("pool", R_[:], R_[:], lb[:], ALU.add, RK + ["lb"], RK)
                          dma("sp", ydst[t * 128:(t + 1) * 128, :], R_[:], RK, [("y1", g)] if l == 0 else ["yout"])
                      S.barrier()
                  stage((l * 2 + g) * 10 + 5)
                  if S.stopped:
                      break
                  if l == 0 and g == 1:
                      while ada1["pending"] or ada1["next"] < 24:
                          ada_step()
                      ada_gate_store(1)
                      S.barrier(False)
              if S.stopped:
                  break

        except _Stop:
            pass

        S.stopped = False
        for e in Sched.ENG:
            S.op(e, lambda en: en.nop(), [], [])
        S.finalize()

        with nc.Block() as block:
            @block.tensor
            def _(eng):
                S.emit("pe", eng)

            @block.scalar
            def _(eng):
                S.emit("act", eng)

            @block.vector
            def _(eng):
                S.emit("dve", eng)

            @block.gpsimd
            def _(eng):
                S.emit("pool", eng)

            @block.sync
            def _(eng):
                S.emit("sp", eng)
    return nc


_NC_CACHE = {}


def kernel(x_prompt, x_sample, cache_attn_k, cache_attn_v, cache_na_k, cache_na_v,
           state_ssm_fwd, state_ssm_bwd, c, c_ctx, w_ada, b_ada, w_in, w_out, ln_g, ln_b,
           attn_sink, ssm_conv_w, ssm_conv_b, ssm_a_log, ssm_dt_bias, ssm_d, ssm_norm_w,
           pool_w, pool_b, pool_scale, na_rpb):
    f = lambda a: np.ascontiguousarray(np.asarray(a, dtype=np.float32))
    if "nc" not in _NC_CACHE:
        _NC_CACHE["nc"] = build_program()
    nc = _NC_CACHE["nc"]
    consts = _consts()
    shared = {
        "w_ada": f(w_ada), "b_ada": f(b_ada), "w_in": f(w_in), "w_out": f(w_out),
        "ln_g": f(ln_g), "ln_b": f(ln_b), "attn_sink": f(attn_sink),
        "ssm_conv_w": f(ssm_conv_w), "ssm_conv_b": f(ssm_conv_b),
        "ssm_a_log": f(ssm_a_log).reshape(2, 16), "ssm_dt_bias": f(ssm_dt_bias).reshape(2, 16),
        "ssm_d": f(ssm_d), "ssm_norm_w": f(ssm_norm_w), "pool_w": f(pool_w),
        "pool_b": f(pool_b).reshape(2, 512), "pool_scale": f(pool_scale),
        "na_rpb": f(na_rpb).reshape(2, 4 * 15 * 31),
    }
    for k, v in consts.items():
        shared["c_" + k] = v
    xp = f(x_prompt); xs = f(x_sample)
    in_maps = []
    for ci in range(NCORES):
        b = ci % 4
        m = dict(shared)
        m["xp"] = xp[4 * ci:4 * ci + 4].reshape(T, D)
        m["xs"] = xs[b].reshape(T, D)
        m["cak"] = f(cache_attn_k)[b].reshape(2, 256, 256)
        m["cav"] = f(cache_attn_v)[b].reshape(2, 256, 256)
        m["cnk"] = f(cache_na_k)[b].reshape(2, 256, 512)
        m["cnv"] = f(cache_na_v)[b].reshape(2, 256, 512)
        m["sf"] = f(state_ssm_fwd)[b].reshape(2, 512, 128)
        m["sb"] = f(state_ssm_bwd)[b].reshape(2, 512, 128)
        m["cvec"] = np.stack([f(c_ctx), f(c)[b]], 0)
        in_maps.append(m)
    res = run_bass_kernel_spmd(nc, in_maps, core_ids=list(range(NCORES)))
    R = res.results
    _NC_CACHE["last"] = R
    y_prompt = np.concatenate([np.asarray(R[ci]["yp"]).reshape(4, 256, D) for ci in range(NCORES)], 0)
    y_sample = np.stack([np.asarray(R[ci]["ys"]).reshape(1024, D) for ci in range(4)], 0)
    cat = lambda name, shp: np.concatenate([np.asarray(R[ci][name]).reshape(shp) for ci in range(NCORES)], 0)
    nak = cat("o_ak", (4, 2, 256, 2, 128)); nav = cat("o_av", (4, 2, 256, 2, 128))
    nnk = cat("o_nk", (4, 2, 256, 4, 128)); nnv = cat("o_nv", (4, 2, 256, 4, 128))
    nsf = cat("o_sf", (4, 2, 8, 64, 128)); nsb = cat("o_sb", (4, 2, 8, 64, 128))
    return (y_prompt.astype(np.float32), y_sample.astype(np.float32), nak, nav, nnk, nnv, nsf, nsb)
```

```python
import numpy as np
import concourse.bass as bass
import concourse.mybir as mybir
from concourse.bass_utils import run_bass_kernel_spmd

F32 = mybir.dt.float32
BF = mybir.dt.bfloat16
AF = mybir.ActivationFunctionType
ALU = mybir.AluOpType
AX = mybir.AxisListType

NCORES = 8
D = 2048
DP = 6160
T = 1024
NT = 8
EPS = 1e-6
ALPHA = 4.0 ** 0.25
SCALE = 128.0 ** -0.5
NEG = -1e30
DEBUG = False
STAGE = 99
FEAT = 99


class _Stop(Exception):
    pass

C_QA, C_KA, C_VA, C_GA, C_XBC, C_Z, C_DT, C_PC, C_GC, C_QD, C_KD, C_VD, C_GD = (
    0, 512, 768, 1024, 1536, 2560, 3072, 3088, 3600, 4112, 4624, 5136, 5648)


class Op:
    __slots__ = ("eng", "fn", "deps", "sem", "val", "needed", "isdma", "phase", "idx")


class Sched:
    ENG = ["pe", "act", "dve", "pool", "sp"]

    def __init__(self, nc, dma_sems, prog_sems):
        self.nc = nc
        self.ops = {e: [] for e in self.ENG}
        self.lastw = {}
        self.readers = {}
        self.pending = {e: [] for e in self.ENG}
        self.dma_sems = dma_sems
        self.dma_last = [None] * len(dma_sems)
        self.dma_cnt = [0] * len(dma_sems)
        self.dma_rr = 0
        self.dma_rr_pool = 0
        self.prog_sems = prog_sems
        self.phase = 0
        self.phase_dmas = []
        self.psn = 0
        self.nops = 0
        self.stopped = False

    def op(self, eng, fn, reads=(), writes=(), dma=False, nobarrier=False):
        if self.stopped:
            return None
        o = Op()
        o.eng, o.fn, o.isdma, o.needed, o.phase = eng, fn, dma, False, self.phase
        o.sem = None
        o.val = 0
        o.idx = self.nops
        self.nops += 1
        if nobarrier:
            deps = []
        else:
            deps = list(self.pending[eng])
            self.pending[eng] = []
        for k in reads:
            w = self.lastw.get(k)
            if w is not None:
                deps.append(w)
            self.readers.setdefault(k, []).append(o)
        for k in writes:
            w = self.lastw.get(k)
            if w is not None:
                deps.append(w)
            for r in self.readers.get(k, []):
                if r is not o:
                    deps.append(r)
            self.readers[k] = []
            self.lastw[k] = o
        if dma:
            half = len(self.dma_sems) // 2
            if eng == "pool":
                j = half + self.dma_rr_pool
                self.dma_rr_pool = (self.dma_rr_pool + 1) % half
            else:
                j = self.dma_rr
                self.dma_rr = (self.dma_rr + 1) % half
            if self.dma_last[j] is not None:
                deps.append(self.dma_last[j])
            self.dma_last[j] = o
            self.dma_cnt[j] += 16
            o.sem = self.dma_sems[j]
            o.val = self.dma_cnt[j]
            o.needed = True
            self.phase_dmas.append(o)
        fl = []
        for d in deps:
            if d is o:
                continue
            if (not d.isdma) and (not dma) and d.eng == eng:
                if eng == "pe":
                    continue
            fl.append(d)
            d.needed = True
        o.deps = fl
        self.ops[eng].append(o)
        return o

    def barrier(self, new_phase=True, final=False):
        if self.stopped:
            return
        lasts = []
        for e in self.ENG:
            for o in reversed(self.ops[e]):
                if not o.isdma:
                    lasts.append(o)
                    break
        lasts += self.phase_dmas
        self.phase_dmas = []
        for o in lasts:
            o.needed = True
        for e in self.ENG:
            if e == "pe" and not final:
                continue
            self.pending[e] = self.pending[e] + list(lasts)
        if new_phase:
            self.phase += 1

    def finalize(self):
        cnt = {}
        for e in self.ENG:
            for o in self.ops[e]:
                if o.isdma:
                    continue
                if o.needed:
                    k = (e, o.phase)
                    cnt[k] = cnt.get(k, 0) + 1
                    o.sem = self.prog_sems[k]
                    o.val = cnt[k]

    def emit(self, e, eng):
        seen = {}
        for o in self.ops[e]:
            for d in o.deps:
                sid = id(d.sem)
                if seen.get(sid, 0) < d.val:
                    eng.wait_ge(d.sem, d.val)
                    seen[sid] = d.val
            ins = o.fn(eng)
            if o.needed:
                ins.then_inc(o.sem, 16 if o.isdma else 1)


def _consts():
    c = {}
    i = np.arange(128)
    c["ident"] = np.eye(128, dtype=np.float32)
    c["ones"] = np.ones((128, 128), np.float32)
    c["trif"] = (i[:, None] <= i[None, :]).astype(np.float32)
    c["trib"] = (i[:, None] >= i[None, :]).astype(np.float32)
    nf = np.where(i[None, :] >= i[:, None], 0.0, NEG).astype(np.float32)
    nb = np.where(i[None, :] <= i[:, None], 0.0, NEG).astype(np.float32)
    c["negf"] = np.tile(nf, (1, 4))
    c["negb"] = np.tile(nb, (1, 4))
    c["mprev"] = (i[:, None] >= i[None, :]).astype(np.float32)
    c["mnext"] = (i[:, None] <= i[None, :]).astype(np.float32)
    sw = np.zeros((128, 128), np.float32)
    for m in range(128):
        p = m + 32 if (m % 64) < 32 else m - 32
        sw[p, m] = 1.0
    c["pswap"] = sw
    t = np.arange(1024)
    rows = (t // 64).astype(np.float32)
    cols = (t % 64).astype(np.float32)
    inv = (10000.0 ** (-np.arange(32, dtype=np.float32) / 32.0)).astype(np.float32)
    cos = np.zeros((128, 1024), np.float32)
    sin = np.zeros((128, 1024), np.float32)
    for m in range(128):
        pos = rows if m < 64 else cols
        ang = (pos * inv[m % 32]).astype(np.float32)
        cos[m] = np.cos(ang)
        sgn = -1.0 if (m % 64) < 32 else 1.0
        sin[m] = sgn * np.sin(ang)
    c["cos"] = cos
    c["sin"] = sin

    def rc(Tn):
        tt = np.arange(Tn)
        out = np.zeros((4, Tn), np.float32)
        for g, w in enumerate((2, 4, 8, 16)):
            lo = np.clip(tt - w // 2, 0, Tn)
            hi = np.clip(tt - w // 2 + w, 0, Tn)
            out[g] = 1.0 / (hi - lo).astype(np.float32)
        return out
    c["rcp"] = rc(256)
    c["rcs"] = rc(1024)
    cc = np.arange(64)
    cs = np.clip(cc - 8, 0, 48)
    ok = (cc[:, None] >= cs[None, :]) & (cc[:, None] < cs[None, :] + 16)
    c["colok"] = ok.astype(np.float32)
    c["colneg"] = np.where(ok, 0.0, NEG).astype(np.float32)
    c["jrev"] = np.eye(64, dtype=np.float32)[::-1].copy()
    s0 = np.zeros((64, 128), np.float32)
    s1 = np.zeros((64, 128), np.float32)
    s0[cc, cc] = 1.0
    s1[cc, cc + 64] = 1.0
    c["sel0"] = s0
    c["sel1"] = s1
    c["negblk"] = np.full((64, 64), NEG, np.float32)
    c["zeros"] = np.zeros((128, 2048), np.float32)
    return c


CONST_SHAPES = {k: v.shape for k, v in _consts().items()}

def _na_ranges():
    rs = [min(max(r - 4, 0), 8) for r in range(16)]
    R = []
    for kr in range(16):
        rr = [r for r in range(16) if rs[r] <= kr <= rs[r] + 7]
        R.append((rr[0], rr[-1]))
    return R


NA_R = _na_ranges()


def build_program():
    nc = bass.Bass("TRN2", target_bir_lowering=False)

    def din(name, shape, dt=F32):
        return nc.dram_tensor(name, list(shape), dt, kind="ExternalInput").ap()

    def dout(name, shape, dt=F32):
        return nc.dram_tensor(name, list(shape), dt, kind="ExternalOutput").ap()

    def dscr(name, shape, dt=F32):
        return nc.dram_tensor(name, list(shape), dt, kind="Internal").ap()

    xin = [din("xp", [T, D]), din("xs", [T, D])]
    cak = din("cak", [2, 256, 256]); cav = din("cav", [2, 256, 256])
    cnk = din("cnk", [2, 256, 512]); cnv = din("cnv", [2, 256, 512])
    st_in = [din("sf", [2, 512, 128]), din("sb", [2, 512, 128])]
    cvec = din("cvec", [2, D])
    w_ada = din("w_ada", [2, D, 3 * D]); b_ada = din("b_ada", [2, 3 * D])
    w_in = din("w_in", [2, D, DP]); w_out = din("w_out", [2, D, D])
    ln_g = din("ln_g", [2, D]); ln_b = din("ln_b", [2, D])
    attn_sink = din("attn_sink", [2, 4])
    conv_w = din("ssm_conv_w", [2, 5, 1024]); conv_b = din("ssm_conv_b", [2, 1024])
    a_log = din("ssm_a_log", [2, 16]); dt_bias = din("ssm_dt_bias", [2, 16])
    ssm_d = din("ssm_d", [2, 8]); norm_w = din("ssm_norm_w", [2, 512])
    pool_w = din("pool_w", [2, 4, 128, 128]); pool_b = din("pool_b", [2, 512])
    pool_scale = din("pool_scale", [2, 512]); rpb = din("na_rpb", [2, 4 * 15 * 31])
    cst = {k: din("c_" + k, list(s)) for k, s in CONST_SHAPES.items()}

    yout = [dout("yp", [T, D]), dout("ys", [T, D])]
    o_ak = dout("o_ak", [4, 2, 256, 256]); o_av = dout("o_av", [4, 2, 256, 256])
    o_nk = dout("o_nk", [4, 2, 256, 512]); o_nv = dout("o_nv", [4, 2, 256, 512])
    o_st = [dout("o_sf", [4, 2, 512, 128]), dout("o_sb", [4, 2, 512, 128])]
    if DEBUG:
        dbg_mixed = dout("dbg_mixed", [4, 128, 16, T])

    y1 = [dscr("y1p", [T, D]), dscr("y1s", [T, D])]
    modscr = dscr("modscr", [2, 2, 3 * D])
    RPAD = 64
    rpbscr = dscr("rpbscr", [2, RPAD + 4 * 15 * 31 + RPAD])

    from contextlib import ExitStack
    es = ExitStack()
    with es:
        ndma = 24
        dma_sems = [es.enter_context(nc.semaphore(f"dq{i}")) for i in range(ndma)]
        NPH = 12
        prog_sems = {}
        for e in Sched.ENG:
            for ph in range(NPH):
                prog_sems[(e, ph)] = es.enter_context(nc.semaphore(f"pg_{e}{ph}"))
        S = Sched(nc, dma_sems, prog_sems)

        _nm = [0]

        def sb(name, shape, dt, stack=es):
            _nm[0] += 1
            return stack.enter_context(nc.sbuf_tensor(f"{name}_{_nm[0]}", list(shape), dt))

        psum = [es.enter_context(nc.psum_tensor(f"ps{i}", [128, 512], F32)) for i in range(8)]

        def PS():
            i = S.psn
            S.psn = (S.psn + 1) % 8
            return ("ps", i), psum[i]

        def mm(out, lhsT, rhs, start, stop, reads, writes):
            return S.op("pe", lambda e: e.matmul(out, lhsT, rhs, start=start, stop=stop,
                                                 skip_group_check=True), reads, writes)

        def tr(out, in_, ident, reads, writes):
            return S.op("pe", lambda e: e.transpose(out, in_, ident), reads, writes)

        def act(out, in_, func, reads, writes, bias=0.0, scale=1.0):
            return S.op("act", lambda e: e.activation(out, in_, func, bias=bias, scale=scale),
                        reads, writes)

        def tt(eng, out, in0, in1, op, reads, writes):
            return S.op(eng, lambda e: e.tensor_tensor(out, in0, in1, op), reads, writes)

        def ts(eng, out, in0, s1, s2, op0, op1, reads, writes):
            if s2 is None:
                return S.op(eng, lambda e: e.tensor_scalar(out, in0, s1, None, op0), reads, writes)
            return S.op(eng, lambda e: e.tensor_scalar(out, in0, s1, s2, op0, op1), reads, writes)

        def stt(eng, out, in0, scalar, in1, op0, op1, reads, writes):
            return S.op(eng, lambda e: e.scalar_tensor_tensor(out, in0, scalar, in1, op0, op1),
                        reads, writes)

        def cp(eng, out, in_, reads, writes):
            if eng == "act":
                return S.op("act", lambda e: e.activation(out, in_, AF.Copy), reads, writes)
            return S.op(eng, lambda e: e.tensor_copy(out, in_), reads, writes)

        def dma(q, out, in_, reads, writes, nonc=False, nobarrier=False):
            if nonc:
                return S.op(q, lambda e: e.dma_start(out=out, in_=in_, allow_slow_non_contiguous=True),
                            reads, writes, dma=True, nobarrier=nobarrier)
            return S.op(q, lambda e: e.dma_start(out=out, in_=in_), reads, writes, dma=True, nobarrier=nobarrier)

        evrr = [0]

        def evac(out, in_, reads, writes, scale=None):
            evrr[0] ^= 1
            if evrr[0]:
                if scale is None:
                    return cp("act", out, in_, reads, writes)
                return act(out, in_, AF.Copy, reads, writes, scale=scale)
            if scale is None:
                return cp("dve", out, in_, reads, writes)
            return ts("dve", out, in_, scale, None, ALU.mult, None, reads, writes)

        def cload(name, shape, dt, src=None, q=None):
            t_ = sb("k_" + name, shape, dt)
            src = cst[name] if src is None else src
            dma("pool" if dt == BF else "sp", t_[:], src, [], [("k", name, str(dt))])
            return t_

        identf = cload("ident", [128, 128], F32); identb = cload("ident", [128, 128], BF)
        onesf = cload("ones", [128, 128], F32); onesb = cload("ones", [128, 128], BF)
        trif = cload("trif", [128, 128], F32); trib = cload("trib", [128, 128], F32)
        negf = cload("negf", [128, 512], BF); negb = cload("negb", [128, 512], BF)
        mprev = cload("mprev", [128, 128], BF); mnext = cload("mnext", [128, 128], BF)
        pswap = cload("pswap", [128, 128], BF)
        cosT = cload("cos", [128, 1024], F32); sinT = cload("sin", [128, 1024], F32)
        colok = cload("colok", [64, 64], F32); colneg = cload("colneg", [64, 64], F32)
        jrev = cload("jrev", [64, 64], F32)
        sel0 = cload("sel0", [64, 128], BF); sel1 = cload("sel1", [64, 128], BF)
        negblk = cload("negblk", [64, 64], BF)
        KALL = [("k", n, str(d)) for n in CONST_SHAPES for d in (F32, BF)]

        mixedT = sb("mixedT", [128, 16, T], BF)
        big = sb("big", [128, 32768], BF)
        uT = big[:, 0:16384].rearrange("p (a b) -> p a b", a=16)
        NW = 4
        wbuf = [big[:, 16384 + i * 4096:16384 + (i + 1) * 4096].rearrange("p (a b) -> p a b", a=16) for i in range(NW)]
        wo = big[:, 0:32768].rearrange("p (a b) -> p a b", a=16)
        from collections import deque
        epsT = sb("epsT", [128, 1], F32)
        S.op("pool", lambda e: e.memset(epsT[:], EPS), [], ["epsT"])

        def rstd_op(dst, var, kk):
            act(dst, var, AF.Sqrt, kk + ["epsT"], kk, bias=epsT[:, 0:1])
            S.op("dve", lambda e: e.reciprocal(dst, dst), kk, kk)

        def stage(n):
            if STAGE <= n and not S.stopped:
                S.barrier(False)
                S.stopped = True

        try:
          stage(0)
          modT = sb("modT", [128, 2, 48, 2], F32)
          bT = sb("ada_bT", [128, 2, 48], F32)
          scb = sb("ada_scb", [128, 16, 2], BF)
          adab = [sb(f"adab{i}", [128, 16, 256], BF) for i in range(2)]
          with ExitStack() as s0:
              sc = sb("ada_sc", [128, 16, 2], F32, s0)
              for v in range(2):
                  dma("sp", sc[:, :, v], cvec[v, :].rearrange("(kc p) -> p kc", p=128), [], ["ada_sc"], nonc=True)
              for l_ in range(2):
                  dma("sp", bT[:, l_, :], b_ada[l_, :].rearrange("(c p) -> p c", p=128), [], ["ada_bT"], nonc=True)
              act(scb[:], sc[:], AF.Silu, ["ada_sc"], ["ada_scb"])

              def ada_mm(l_, jb, wk, wb):
                  pk, ps = PS()
                  for half in range(2):
                      for kc in range(16):
                          mm(ps[:, half * 2:half * 2 + 2], wb[:, kc, half * 128:(half + 1) * 128], scb[:, kc, :],
                             kc == 0, kc == 15, ["ada_scb", wk], [pk])
                  tt("dve", modT[:, l_, jb * 2:jb * 2 + 2, :], ps[:, 0:4].rearrange("p (a b) -> p a b", a=2),
                     bT[:, l_, jb * 2:jb * 2 + 2].unsqueeze(2).to_broadcast([128, 2, 2]), ALU.add,
                     [pk, "ada_bT"], [("modT", l_, jb)])

              def ada_gate_store(l_):
                  for v in range(2):
                      dma("sp", modscr[l_, v, 2 * D:3 * D].rearrange("(kc p) -> p kc", p=128), modT[:, l_, 32:48, v],
                          [("modT", l_, jb) for jb in range(16, 24)], ["modscr"], nonc=True)

              for jb in range(24):
                  i_ = jb % NW
                  wk = ("w", i_)
                  dma("pool", wbuf[i_][:], w_ada[0, :, jb * 256:(jb + 1) * 256].rearrange("(kc p) n -> p kc n", p=128), [], [wk])
                  ada_mm(0, jb, wk, wbuf[i_])
              ada_gate_store(0)
              ada1 = {"next": 0, "pending": [], "free": [0, 1]}

              def ada_consume():
                  jb, bi = ada1["pending"].pop(0)
                  ada_mm(1, jb, ("adab", bi), adab[bi])
                  ada1["free"].append(bi)

              def ada_step():
                  if len(ada1["pending"]) == 2 or (ada1["next"] >= 24 and ada1["pending"]):
                      ada_consume()
                  jb = ada1["next"]
                  if jb < 24 and ada1["free"]:
                      bi = ada1["free"].pop(0)
                      dma("pool", adab[bi][:], w_ada[1, :, jb * 256:(jb + 1) * 256].rearrange("(kc p) n -> p kc n", p=128),
                          [], [("adab", bi)], nobarrier=True)
                      ada1["pending"].append((jb, bi))
                      ada1["next"] = jb + 1

              zt = sb("zt", [1, 2048], F32, s0)
              dma("sp", zt[:], cst["zeros"][0:1, :], [], ["zt"])
              for l in range(2):
                  dma("sp", rpbscr[l, :].rearrange("(o n) -> o n", o=1), zt[0:1, 0:RPAD * 2 + 1860],
                      ["zt"], [("rpbscr", l)])
                  dma("sp", rpbscr[l, RPAD:RPAD + 1860].rearrange("(o n) -> o n", o=1),
                      rpb[l, :].rearrange("(o n) -> o n", o=1), [], [("rpbscr", l)])
              S.barrier()
          stage(1)

          for l in range(2):
              for g in range(2):
                  smp = (g == 1)
                  nseq = 1 if smp else 4
                  SL = T // nseq
                  xsrc = xin[g] if l == 0 else y1[g]
                  ydst = y1[g] if l == 0 else yout[g]
                  with ExitStack() as s1:
                      wcnt = [0]
                      PRE = deque()

                      def _issue_w(c0, ncols, nobarrier):
                          i = wcnt[0] % NW
                          wcnt[0] += 1
                          key = ("w", i)
                          dma("pool", wbuf[i][:, :, 0:ncols],
                              w_in[l, :, c0:c0 + ncols].rearrange("(kc p) n -> p kc n", p=128), [], [key],
                              nobarrier=nobarrier)
                          return key, wbuf[i]

                      def load_w(c0, ncols):
                          if PRE:
                              (cc, kb) = PRE.popleft()
                              assert cc == (c0, ncols), (cc, c0, ncols)
                              return kb
                          return _issue_w(c0, ncols, False)

                      def prefetch(lst, nobarrier=True):
                          assert not PRE
                          for (c0, n_) in lst:
                              PRE.append(((c0, n_), _issue_w(c0, n_, nobarrier)))

                      def UKS(tq):
                          return [("uT", tq, kc) for kc in range(16)]

                      def proj_fm(wkey, wb, j0, evac_fn):
                          pss = [PS(), PS()]
                          for kc in range(16):
                              for tc in range(2):
                                  pk, ps = pss[tc]
                                  mm(ps[:, :], wb[:, kc, j0:j0 + 128], uT[:, kc, tc * 512:(tc + 1) * 512],
                                     kc == 0, kc == 15, [wkey, ("uT", tc, kc)], [pk])
                          for tc in range(2):
                              pk, ps = pss[tc]
                              evac_fn(pk, ps, tc)

                      def proj_tm(wkey, wb, ncols, evac_fn, tiles=range(NT)):
                          for t in tiles:
                              pk, ps = PS()
                              for kc in range(16):
                                  mm(ps[:, 0:ncols], uT[:, kc, t * 128:(t + 1) * 128], wb[:, kc, 0:ncols],
                                     kc == 0, kc == 15, [wkey, ("uT", t // 4, kc)], [pk])
                              evac_fn(pk, ps, t)

                      with ExitStack() as s2:
                          scT = sb("scT", [128, 16], F32, s2)
                          shT = sb("shT", [128, 16], F32, s2)
                          xt = [sb(f"xt{i}", [128, D], F32, s2) for i in range(3)]
                          xn = [sb(f"xn{i}", [128, D], BF, s2) for i in range(4)]
                          st6 = [sb(f"st6{i}", [128, 4, 6], F32, s2) for i in range(3)]
                          mv = [sb(f"mv{i}", [128, 4], F32, s2) for i in range(3)]
                          MK = [("modT", l, jb) for jb in range(16)]
                          cp("dve", shT[:], modT[:, l, 0:16, g], MK, ["shT"])
                          ts("dve", scT[:], modT[:, l, 16:32, g], 1.0, None, ALU.add, None, MK, ["scT"])
                          prefetch([(C_QA, 256), (C_QA + 256, 256), (C_KA, 256), (C_VA, 256)], nobarrier=False)
                          for tq in range(2):
                              for i in range(4):
                                  t = tq * 4 + i
                                  b = t % 3
                                  X, ST, MV, XN = xt[b], st6[b], mv[b], xn[i]
                                  kx, kst, kmv, kxn = ("xt", b), ("st6", b), ("mv", b), ("xn", i)
                                  dma("sp", X[:], xsrc[t * 128:(t + 1) * 128, :], [("y1", g)] if l else [], [kx])
                                  for c4 in range(4):
                                      S.op("dve", (lambda e, X=X, ST=ST, c4=c4: e.bn_stats(ST[:, c4, :], X[:, c4 * 512:(c4 + 1) * 512])),
                                           [kx], [kst])
                                  S.op("dve", (lambda e, ST=ST, MV=MV: e.bn_aggr(MV[:, 0:2], ST[:].rearrange("p a b -> p (a b)"))),
                                       [kst], [kmv])
                                  rstd_op(MV[:, 2:3], MV[:, 1:2], [kmv])
                                  stt("dve", MV[:, 3:4], MV[:, 0:1], -1.0, MV[:, 2:3], ALU.mult, ALU.mult, [kmv], [kmv])
                                  act(XN[:], X[:], AF.Identity, [kx, kmv], [kxn], bias=MV[:, 3:4], scale=MV[:, 2:3])
                              for kp in range(8):
                                  pk, ps = PS()
                                  pb = ps[:].bitcast(BF)
                                  for kk in range(2):
                                      kc = kp * 2 + kk
                                      for i in range(4):
                                          tr(pb[:, kk * 512 + i * 128:kk * 512 + (i + 1) * 128], xn[i][:, kc * 128:(kc + 1) * 128],
                                             identb[:], [("xn", i)] + KALL[:2], [pk])
                                  for kk in range(2):
                                      kc = kp * 2 + kk
                                      dst = uT[:, kc, tq * 512:(tq + 1) * 512]
                                      src_ = pb[:, kk * 512:(kk + 1) * 512]
                                      if kp % 2 == 0:
                                          act(dst, src_, AF.Identity, [pk, "scT", "shT"], [("uT", tq, kc)],
                                              bias=shT[:, kc:kc + 1], scale=scT[:, kc:kc + 1])
                                      else:
                                          ts("dve", dst, src_, scT[:, kc:kc + 1], shT[:, kc:kc + 1], ALU.mult, ALU.add,
                                             [pk, "scT", "shT"], [("uT", tq, kc)])
                          S.barrier(False)
                      stage((l * 2 + g) * 10 + 2)

                      def attn_branch(pref, nh, nkv, cq, ck, cv, cg, fc0, o_k, o_v, use_sink, cache_k, cache_v, na):
                          G = nh // nkv
                          kvw = nkv * 128
                          with ExitStack() as s2:
                              qT = sb(pref + "qT", [128, nh, T], BF, s2)
                              kT = sb(pref + "kT", [128, nkv, T], BF, s2)
                              vtm = sb(pref + "v", [128, NT, kvw], BF, s2)
                              sgT = sb(pref + "sg", [128, nh, T], BF, s2)
                              stg = [sb(pref + f"stg{i}", [128, 256], F32, s2) for i in range(2)]
                              NPT = 10
                              pt = [sb(pref + f"pt{i}", [128, 512], BF, s2) for i in range(NPT)]
                              rcp = [sb(pref + f"rc{i}", [128, 512], F32, s2) for i in range(2)]
                              kq, kk_, kv_, ksg = pref + "qT", pref + "kT", pref + "v", pref + "sg"
                              if smp:
                                  kct = sb(pref + "kct", [128, 2, kvw], BF, s2)
                                  vct = sb(pref + "vct", [128, 2, kvw], BF, s2)
                                  dma("pool", kct[:], cache_k[l].rearrange("(a p) n -> p a n", p=128), [], [pref + "kct"])
                                  dma("pool", vct[:], cache_v[l].rearrange("(a p) n -> p a n", p=128), [], [pref + "vct"])
                                  if na:
                                      bmraw = sb(pref + "bmraw", [64, 60, 64], F32, s2)
                                      src = bass.AP(tensor=rpbscr.tensor, offset=l * (2 * RPAD + 1860) + RPAD - 48,
                                                    ap=[[1, 64], [31, 60], [1, 64]])
                                      dma("sp", bmraw[:], src, [("rpbscr", l)], [pref + "bmraw"])
                              if use_sink:
                                  sinkx = sb(pref + "sink", [128, 4], F32, s2)
                                  dma("sp", sinkx[:], attn_sink[l, :].partition_broadcast(128), [], [pref + "sink"])
                                  act(sinkx[:], sinkx[:], AF.Exp, [pref + "sink"], [pref + "sink"])
                              if pref == "a_":
                                  stage((l * 2 + g) * 10 + 2.02)

                              for j in range(nh // 2):
                                  wk, wb = load_w(cq + j * 256, 256)
                                  for hh in range(2):
                                      h = j * 2 + hh
                                      proj_fm(wk, wb, hh * 128,
                                              lambda pk, ps, tc, h=h: evac(qT[:, h, tc * 512:(tc + 1) * 512], ps[:, :],
                                                                           [pk], [(kq, h, tc)], scale=SCALE))
                              if pref == "a_":
                                  stage((l * 2 + g) * 10 + 2.05)
                              if not smp:
                                  kTf = sb(pref + "kTf", [128, nkv, T], F32, s2)
                                  stgk = [sb(pref + f"stgk{i}", [128, kvw], F32, s2) for i in range(2)]
                              for j in range(nkv // 2):
                                  wk, wb = load_w(ck + j * 256, 256)
                                  for hh in range(2):
                                      h = j * 2 + hh

                                      def ev_kf(pk, ps, tc, h=h):
                                          sl_ = slice(tc * 512, (tc + 1) * 512)
                                          if smp:
                                              evac(kT[:, h, sl_], ps[:, :], [pk], [(kk_, h, tc)])
                                          else:
                                              cp("dve", kTf[:, h, sl_], ps[:, :], [pk], [(pref + "kTf", h, tc)])
                                              cp("act", kT[:, h, sl_], kTf[:, h, sl_], [(pref + "kTf", h, tc)], [(kk_, h, tc)])
                                      proj_fm(wk, wb, hh * 128, ev_kf)
                              if not smp:
                                  for t in range(NT):
                                      pk, ps = PS()
                                      for h in range(nkv):
                                          tr(ps[:, h * 128:(h + 1) * 128], kTf[:, h, t * 128:(t + 1) * 128], identf[:],
                                             [(pref + "kTf", h, t // 4)] + KALL, [pk])
                                      b = t % 2
                                      cp("dve", stgk[b][:], ps[:, 0:kvw], [pk], [(pref + "stgk", b)])
                                      s_, tt_ = divmod(t, 2)
                                      dma("sp", o_k[s_, l, tt_ * 128:(tt_ + 1) * 128, :], stgk[b][:],
                                          [(pref + "stgk", b)], [pref + "ok"])
                              if pref == "a_":
                                  stage((l * 2 + g) * 10 + 2.1)
                              for j in range(nkv // 2):
                                  wk, wb = load_w(cv + j * 256, 256)

                                  def ev_v(pk, ps, t, j=j):
                                      if smp:
                                          evac(vtm[:, t, j * 256:(j + 1) * 256], ps[:, 0:256], [pk], [(kv_, t)])
                                      else:
                                          b = t % 2
                                          cp("dve", stg[b][:], ps[:, 0:256], [pk], [(pref + "stg", b)])
                                          cp("act", vtm[:, t, j * 256:(j + 1) * 256], stg[b][:], [(pref + "stg", b)], [(kv_, t)])
                                          s_, tt_ = divmod(t, 2)
                                          dma("sp", o_v[s_, l, tt_ * 128:(tt_ + 1) * 128, j * 256:(j + 1) * 256], stg[b][:],
                                              [(pref + "stg", b)], [pref + "ov"])
                                  proj_tm(wk, wb, 256, ev_v)
                              if pref == "a_":
                                  stage((l * 2 + g) * 10 + 2.15)
                              for j in range(nh // 2):
                                  wk, wb = load_w(cg + j * 256, 256)
                                  for hh in range(2):
                                      h = j * 2 + hh
                                      proj_fm(wk, wb, hh * 128,
                                              lambda pk, ps, tc, h=h: act(sgT[:, h, tc * 512:(tc + 1) * 512], ps[:, :], AF.Silu,
                                                                          [pk], [(ksg, h, tc)]))

                              if pref == "a_":
                                  stage((l * 2 + g) * 10 + 2.2)
                                  prefetch([(C_QD, 256), (C_QD + 256, 256), (C_KD, 256), (C_KD + 256, 256)])
                              else:
                                  prefetch([(C_PC, 256), (C_PC + 256, 256), (C_GC, 256), (C_GC + 256, 256)])
                              pti = [0]

                              def getpt():
                                  i = pti[0] % NPT
                                  pti[0] += 1
                                  return (pref + "pt", i), pt[i]

                              ric = [0]

                              def finish(h, q0, nq, ok_, ops_, dk, dps):
                                  ri = ric[0]
                                  ric[0] += 1
                                  R_ = rcp[ri % 2]
                                  rk = (pref + "rc", ri % 2)
                                  if use_sink:
                                      ts("dve", R_[:, 0:nq], dps[:, 0:nq], sinkx[:, h:h + 1], None, ALU.add, None, [dk, pref + "sink"], [rk])
                                      S.op("dve", lambda e: e.reciprocal(R_[:, 0:nq], R_[:, 0:nq]), [rk], [rk])
                                  else:
                                      S.op("dve", lambda e: e.reciprocal(R_[:, 0:nq], dps[:, 0:nq]), [dk], [rk])
                                  tt("dve", R_[:, 0:nq], ops_[:, 0:nq], R_[:, 0:nq], ALU.mult, [ok_, rk], [rk])
                                  tt("dve", mixedT[:, fc0 + h, q0:q0 + nq], R_[:, 0:nq], sgT[:, h, q0:q0 + nq], ALU.mult,
                                     [rk] + [(ksg, h, tc) for tc in range(2)], [("mx", fc0 + h)])

                              def pv(contrib, h, q0, nq):
                                  ok_, ops_ = PS()
                                  dk, dps = PS()
                                  n_ = len(contrib)
                                  for i_, (ptk, pap, lv, vk, c0, ncl) in enumerate(contrib):
                                      mm(ops_[:, c0:c0 + ncl], lv, pap, i_ == 0, i_ == n_ - 1, [ptk, vk], [ok_])
                                  for i_, (ptk, pap, lv, vk, c0, ncl) in enumerate(contrib):
                                      mm(dps[:, c0:c0 + ncl], onesb[:], pap, i_ == 0, i_ == n_ - 1, [ptk] + KALL, [dk])
                                  finish(h, q0, nq, ok_, ops_, dk, dps)

                              if not smp:
                                  for s_ in range(4):
                                      t0 = s_ * 256
                                      for kv in range(nkv):
                                          pts = []
                                          for kt in range(2):
                                              pk, ps = PS()
                                              for hh in range(G):
                                                  h = kv * G + hh
                                                  mm(ps[:, hh * 256:(hh + 1) * 256], kT[:, kv, t0 + kt * 128:t0 + (kt + 1) * 128],
                                                     qT[:, h, t0:t0 + 256], True, True,
                                                     [(kk_, kv, s_ // 2), (kq, h, s_ // 2)], [pk])
                                              ptk, P_ = getpt()
                                              act(P_[:, 0:G * 256], ps[:, 0:G * 256], AF.Exp, [pk], [ptk])
                                              pts.append((ptk, P_))
                                          ok_, ops_ = PS()
                                          dk, dps = PS()
                                          for kt in range(2):
                                              mm(ops_[:, 0:G * 256], vtm[:, s_ * 2 + kt, kv * 128:(kv + 1) * 128], pts[kt][1][:, 0:G * 256],
                                                 kt == 0, kt == 1, [(kv_, s_ * 2 + kt), pts[kt][0]], [ok_])
                                          for kt in range(2):
                                              mm(dps[:, 0:G * 256], onesb[:], pts[kt][1][:, 0:G * 256], kt == 0, kt == 1,
                                                 [pts[kt][0]] + KALL, [dk])
                                          for hh in range(G):
                                              h = kv * G + hh
                                              finish(h, t0, 256, ok_, ops_[:, hh * 256:(hh + 1) * 256], dk,
                                                     dps[:, hh * 256:(hh + 1) * 256])
                              else:
                                  kcT = sb(pref + "kcT", [128, nkv, 256], BF, s2)
                                  for kv in range(nkv):
                                      pk, ps = PS()
                                      pb = ps[:].bitcast(BF)
                                      for a in range(2):
                                          tr(pb[:, a * 128:(a + 1) * 128], kct[:, a, kv * 128:(kv + 1) * 128], identb[:],
                                             [pref + "kct"] + KALL, [pk])
                                      cp("dve", kcT[:, kv, :], pb[:, 0:256], [pk], [(pref + "kcT", kv)])

                                  def ctx_contrib(h, kv, qc):
                                      out = []
                                      for a in range(2):
                                          pk, ps = PS()
                                          mm(ps[:, :], kcT[:, kv, a * 128:(a + 1) * 128], qT[:, h, qc * 512:(qc + 1) * 512], True, True,
                                             [(pref + "kcT", kv), (kq, h, qc)], [pk])
                                          ptk, P_ = getpt()
                                          act(P_[:, 0:512], ps[:, :], AF.Exp, [pk], [ptk])
                                          out.append((ptk, P_[:, 0:512], vct[:, a, kv * 128:(kv + 1) * 128], pref + "vct", 0, 512))
                                      return out

                                  if not na:
                                      rtmp = [sb(pref + f"rt{i}", [128, 512], F32, s2) for i in range(2)]
                                      for (buf, nm, hh_) in [(qT, kq, nh), (kT, kk_, nkv)]:
                                          for h in range(hh_):
                                              for tc in range(2):
                                                  sl = slice(tc * 512, (tc + 1) * 512)
                                                  pk, ps = PS()
                                                  kk2 = (nm, h, tc)
                                                  mm(ps[:, :], pswap[:], buf[:, h, sl], True, True, [kk2] + KALL, [pk])
                                                  b = (h + tc) % 2
                                                  tt("dve", rtmp[b][:], ps[:, :], sinT[:, sl], ALU.mult, [pk] + KALL, [(pref + "rt", b)])
                                                  tt("dve", buf[:, h, sl], buf[:, h, sl], cosT[:, sl], ALU.mult, [kk2, pk] + KALL, [kk2])
                                                  tt("dve", buf[:, h, sl], buf[:, h, sl], rtmp[b][:], ALU.add, [kk2, (pref + "rt", b)], [kk2])
                                      for h in range(nh):
                                          kv = h // G
                                          for qc in range(2):
                                              contrib = ctx_contrib(h, kv, qc)
                                              for kt in range(4 * qc - 1, 4 * qc + 5):
                                                  if kt < 0 or kt > 7:
                                                      continue
                                                  b0 = max(4 * qc, kt - 1)
                                                  b1 = min(4 * qc + 3, kt + 1)
                                                  nb_ = b1 - b0 + 1
                                                  pk, ps = PS()
                                                  mm(ps[:, 0:nb_ * 128], kT[:, kv, kt * 128:(kt + 1) * 128],
                                                     qT[:, h, b0 * 128:(b1 + 1) * 128], True, True,
                                                     [(kk_, kv, kt // 4), (kq, h, qc)], [pk])
                                                  ptk, P_ = getpt()
                                                  act(P_[:, 0:nb_ * 128], ps[:, 0:nb_ * 128], AF.Exp, [pk], [ptk])
                                                  for bq in range(b0, b1 + 1):
                                                      o_ = (bq - b0) * 128
                                                      if bq == kt + 1:
                                                          tt("dve", P_[:, o_:o_ + 128], P_[:, o_:o_ + 128], mprev[:], ALU.mult,
                                                             [ptk] + KALL, [ptk])
                                                      elif bq == kt - 1:
                                                          tt("dve", P_[:, o_:o_ + 128], P_[:, o_:o_ + 128], mnext[:], ALU.mult,
                                                             [ptk] + KALL, [ptk])
                                                  contrib.append((ptk, P_[:, 0:nb_ * 128], vtm[:, kt, kv * 128:(kv + 1) * 128],
                                                                  (kv_, kt), (b0 - 4 * qc) * 128, nb_ * 128))
                                              pv(contrib, h, qc * 512, 512)
                                  else:
                                      bmd = sb(pref + "bmd", [64, 4, 15, 64], BF, s2)
                                      btmp = [sb(pref + "btmp0", [64, 8, 64], F32, s2)] * 2
                                      bi = 0
                                      for h in range(4):
                                          for (e0, ne) in [(0, 8), (8, 7)]:
                                              pk, ps = PS()
                                              for i_ in range(ne):
                                                  row = h * 15 + 14 - (e0 + i_)
                                                  mm(ps[0:64, i_ * 64:(i_ + 1) * 64], bmraw[:, row, :], jrev[:], True, True,
                                                     [pref + "bmraw"] + KALL, [pk])
                                              bt = btmp[bi % 2]
                                              bk = (pref + "btmp", 0)
                                              bi += 1
                                              tt("dve", bt[:, 0:ne, :], ps[0:64, 0:ne * 64].rearrange("p (a b) -> p a b", a=ne),
                                                 colok[:].unsqueeze(1).to_broadcast([64, ne, 64]), ALU.mult, [pk] + KALL, [bk])
                                              tt("dve", bmd[:, h, e0:e0 + ne, :], bt[:, 0:ne, :],
                                                 colneg[:].unsqueeze(1).to_broadcast([64, ne, 64]), ALU.add, [bk] + KALL, [(pref + "bmd", h)])
                                      for h in range(nh):
                                          for qc in range(2):
                                              contrib = ctx_contrib(h, h, qc)
                                              for kt in range(8):
                                                  (a0, b0_) = NA_R[2 * kt]
                                                  (a1, b1_) = NA_R[2 * kt + 1]
                                                  r1 = max(min(a0, a1), 8 * qc)
                                                  r2 = min(max(b0_, b1_), 8 * qc + 7)
                                                  if r1 > r2:
                                                      continue
                                                  n = r2 - r1 + 1
                                                  pk, ps = PS()
                                                  seq_ = []
                                                  seq_.append((ps[:, 0:n * 64], kT[:, h, kt * 128:(kt + 1) * 128], qT[:, h, r1 * 64:(r2 + 1) * 64],
                                                               [(kk_, h, kt // 4), (kq, h, qc)]))
                                                  for j, sel in ((0, sel0), (1, sel1)):
                                                      kr = 2 * kt + j
                                                      (a, b) = NA_R[kr]
                                                      va, vb = max(a, r1), min(b, r2)
                                                      if va <= vb:
                                                          e1 = 7 - kr + va
                                                          seq_.append((ps[:, (va - r1) * 64:(vb - r1 + 1) * 64], sel[:],
                                                                       bmd[:, h, e1:e1 + (vb - va + 1), :].rearrange("p a b -> p (a b)"),
                                                                       [(pref + "bmd", h)] + KALL))
                                                      for r in range(r1, r2 + 1):
                                                          if r < a or r > b:
                                                              seq_.append((ps[:, (r - r1) * 64:(r - r1 + 1) * 64], sel[:], negblk[:], KALL))
                                                  for i_, (o_, l_, r_, rd) in enumerate(seq_):
                                                      mm(o_, l_, r_, i_ == 0, i_ == len(seq_) - 1, rd, [pk])
                                                  ptk, P_ = getpt()
                                                  act(P_[:, 0:n * 64], ps[:, 0:n * 64], AF.Exp, [pk], [ptk])
                                                  contrib.append((ptk, P_[:, 0:n * 64], vtm[:, kt, h * 128:(h + 1) * 128], (kv_, kt),
                                                                  (r1 - 8 * qc) * 64, n * 64))
                                              pv(contrib, h, qc * 512, 512)
                              S.barrier(False)

                      attn_branch("a_", 4, 2, C_QA, C_KA, C_VA, C_GA, 0, o_ak, o_av, True, cak, cav, False)
                      stage((l * 2 + g) * 10 + 2.5)
                      if FEAT >= 1:
                          attn_branch("d_", 4, 4, C_QD, C_KD, C_VD, C_GD, 12, o_nk, o_nv, False, cnk, cnv, smp)
                      else:
                          for fc in range(12, 16):
                              S.op("pool", (lambda e, fc=fc: e.memset(mixedT[:, fc, :], 0.0)), [], [("mx", fc)])

                      stage((l * 2 + g) * 10 + 2.6)
                      if FEAT >= 2:
                          with ExitStack() as s2:
                              SLp = SL + 16
                              pcT = sb("c_pc", [128, 4, nseq, SLp], F32, s2)
                              sgc = sb("c_sg", [128, 4, T], BF, s2)
                              pooled = sb("c_pl", [128, 4, T], BF, s2)
                              wA = sb("c_wa", [128, nseq, SLp], F32, s2)
                              wB = sb("c_wb", [128, nseq, SLp], F32, s2)
                              rct = sb("c_rc", [128, 4, SL], F32, s2)
                              pw = sb("c_pw", [128, 4, 128], BF, s2)
                              pbs = sb("c_pb", [128, 4], F32, s2)
                              psc = sb("c_ps", [128, 4], F32, s2)
                              ctmp = [sb(f"c_tmp{i}", [128, 512], F32, s2) for i in range(2)]
                              S.op("pool", (lambda e, pcT=pcT: e.memset(pcT[:], 0.0)), [], ["c_pc"])
                              dma("sp", rct[:], (cst["rcs"] if smp else cst["rcp"]).partition_broadcast(128), [], ["c_rc"])
                              dma("pool", pw[:], pool_w[l].rearrange("g c d -> c g d"), [], ["c_pw"])
                              dma("sp", pbs[:], pool_b[l, :].rearrange("(g p) -> p g", p=128), [], ["c_pb"], nonc=True)
                              dma("sp", psc[:], pool_scale[l, :].rearrange("(g p) -> p g", p=128), [], ["c_ps"], nonc=True)
                              for j in range(2):
                                  wk, wb = load_w(C_PC + j * 256, 256)
                                  for hh in range(2):
                                      gq = j * 2 + hh

                                      def ev_pc(pk, ps, tc, gq=gq):
                                          if smp:
                                              dst = pcT[:, gq, 0, 8 + tc * 512:8 + (tc + 1) * 512]
                                              srcp = ps[:, :]
                                          else:
                                              dst = pcT[:, gq, 2 * tc:2 * tc + 2, 8:8 + 256]
                                              srcp = ps[:, :].rearrange("p (s t) -> p s t", s=2)
                                          evac(dst, srcp, [pk, "c_pc"], [("c_pcg", gq)])
                                      proj_fm(wk, wb, hh * 128, ev_pc)
                              for j in range(2):
                                  wk, wb = load_w(C_GC + j * 256, 256)
                                  for hh in range(2):
                                      gq = j * 2 + hh
                                      proj_fm(wk, wb, hh * 128,
                                              lambda pk, ps, tc, gq=gq: act(sgc[:, gq, tc * 512:(tc + 1) * 512], ps[:, :], AF.Silu,
                                                                            [pk], [("c_sg", gq, tc)]))
                              prefetch([(C_XBC, 256), (C_XBC + 256, 256), (C_XBC + 512, 256), (C_XBC + 768, 256)])
                              for gq in range(4):
                                  P_ = pcT[:, gq]
                                  kp = ("c_pcg", gq)
                                  tt("dve", wA[:, :, 1:SLp], P_[:, :, 0:SLp - 1], P_[:, :, 1:SLp], ALU.add, [kp], ["c_wa"])
                                  cur, ck_ = wA, "c_wa"
                                  if gq >= 1:
                                      tt("dve", wB[:, :, 2:SLp - 1], wA[:, :, 1:SLp - 2], wA[:, :, 3:SLp], ALU.add, ["c_wa"], ["c_wb"])
                                      cur, ck_ = wB, "c_wb"
                                  if gq >= 2:
                                      tt("dve", wA[:, :, 4:SLp - 3], wB[:, :, 2:SLp - 5], wB[:, :, 6:SLp - 1], ALU.add, ["c_wb"], ["c_wa"])
                                      cur, ck_ = wA, "c_wa"
                                  if gq >= 3:
                                      tt("dve", wB[:, :, 8:SLp - 7], wA[:, :, 4:SLp - 11], wA[:, :, 12:SLp - 3], ALU.add, ["c_wa"], ["c_wb"])
                                      cur, ck_ = wB, "c_wb"
                                  tt("dve", cur[:, :, 8:8 + SL], cur[:, :, 8:8 + SL],
                                     rct[:, gq, :].unsqueeze(1).to_broadcast([128, nseq, SL]), ALU.mult, [ck_, "c_rc"], [ck_])
                                  tt("dve", pooled[:, gq, :].rearrange("p (s t) -> p s t", s=nseq), cur[:, :, 8:8 + SL],
                                     P_[:, :, 8:8 + SL], ALU.subtract, [ck_, kp], [("c_pl", gq)])
                                  for tc in range(2):
                                      pk, ps = PS()
                                      mm(ps[:, :], pw[:, gq, :], pooled[:, gq, tc * 512:(tc + 1) * 512], True, True,
                                         [("c_pl", gq), "c_pw"], [pk])
                                      b = tc % 2
                                      ts("dve", ctmp[b][:], ps[:, :], pbs[:, gq:gq + 1], psc[:, gq:gq + 1], ALU.add, ALU.mult,
                                         [pk, "c_pb", "c_ps"], [("c_tmp", b)])
                                      tt("dve", mixedT[:, 8 + gq, tc * 512:(tc + 1) * 512], ctmp[b][:], sgc[:, gq, tc * 512:(tc + 1) * 512],
                                         ALU.mult, [("c_tmp", b), ("c_sg", gq, tc)], [("mx", 8 + gq)])
                              S.barrier(False)
                      else:
                          for fc in range(8, 12):
                              S.op("pool", (lambda e, fc=fc: e.memset(mixedT[:, fc, :], 0.0)), [], [("mx", fc)])

                      stage((l * 2 + g) * 10 + 2.7)
                      if FEAT >= 3:
                          with ExitStack() as s2:
                              xsT = sb("b_xsT", [128, 4, T], BF, s2)
                              bcT = sb("b_bcT", [128, 4, T], BF, s2)
                              xs = sb("b_xs", [128, NT, 512], BF, s2)
                              btm = sb("b_btm", [128, NT, 256], BF, s2)
                              sz = sb("b_sz", [128, NT, 512], BF, s2)
                              dtr = sb("b_dt", [128, NT, 16], F32, s2)
                              dta = sb("b_dta", [128, NT, 16], F32, s2)
                              ysum = sb("b_y", [128, NT, 512], F32, s2)
                              cw = sb("b_cw", [128, 8, 5], F32, s2)
                              cb = sb("b_cb", [128, 8], F32, s2)
                              dtb = sb("b_dtb", [128, 16], F32, s2)
                              nga = sb("b_nga", [128, 16], F32, s2)
                              dbc = sb("b_dbc", [128, 8], F32, s2)
                              nwb = sb("b_nwb", [128, 512], F32, s2)
                              for k5 in range(5):
                                  dma("sp", cw[:, :, k5], conv_w[l, k5, :].rearrange("(b p) -> p b", p=128), [], ["b_cw"], nonc=True)
                              dma("sp", cb[:], conv_b[l, :].rearrange("(b p) -> p b", p=128), [], ["b_cb"], nonc=True)
                              dma("sp", dtb[:], dt_bias[l, :].partition_broadcast(128), [], ["b_dtb"])
                              dma("sp", nga[:], a_log[l, :].partition_broadcast(128), [], ["b_nga"])
                              dma("sp", dbc[:], ssm_d[l, :].partition_broadcast(128), [], ["b_dbc"])
                              dma("sp", nwb[:], norm_w[l, :].partition_broadcast(128), [], ["b_nwb"])
                              act(nga[:], nga[:], AF.Exp, ["b_nga"], ["b_nga"])
                              ts("dve", nga[:], nga[:], -1.0, None, ALU.mult, None, ["b_nga"], ["b_nga"])
                              with ExitStack() as s3:
                                  raw = [sb(f"b_raw{i}", [128, nseq, SL + 4], F32, s3) for i in range(2)]
                                  acc = [sb(f"b_acc{i}", [128, nseq, SL], F32, s3) for i in range(2)]
                                  for i in range(2):
                                      S.op("pool", (lambda e, r=raw[i]: e.memset(r[:], 0.0)), [], [("b_rawz", i)])
                                  for j in range(4):
                                      wk, wb = load_w(C_XBC + j * 256, 256)
                                      for hh in range(2):
                                          blk = j * 2 + hh
                                          rb = blk % 2
                                          R_, A_ = raw[rb], acc[rb]

                                          def ev_x(pk, ps, tc, R_=R_, rb=rb):
                                              if smp:
                                                  dst = R_[:, 0, 2 + tc * 512:2 + (tc + 1) * 512]
                                                  srcp = ps[:, :]
                                              else:
                                                  dst = R_[:, 2 * tc:2 * tc + 2, 2:2 + 256]
                                                  srcp = ps[:, :].rearrange("p (s t) -> p s t", s=2)
                                              evac(dst, srcp, [pk, ("b_rawz", rb)], [("b_raw", rb, tc)])
                                          proj_fm(wk, wb, hh * 128, ev_x)
                                          rk = [("b_raw", rb, 0), ("b_raw", rb, 1)]
                                          ak = ("b_acc", rb)
                                          ts("dve", A_[:], R_[:, :, 2:2 + SL], cw[:, blk, 2:3], None, ALU.mult, None, rk + ["b_cw"], [ak])
                                          for k in (0, 1, 3, 4):
                                              stt("dve", A_[:], R_[:, :, k:k + SL], cw[:, blk, k:k + 1], A_[:], ALU.mult, ALU.add,
                                                  rk + [ak, "b_cw"], [ak])
                                          if blk < 4:
                                              dst = xsT[:, blk, :].rearrange("p (s t) -> p s t", s=nseq)
                                              dk_ = ("b_xsT", blk)
                                          else:
                                              dst = bcT[:, blk - 4, :].rearrange("p (s t) -> p s t", s=nseq)
                                              dk_ = ("b_bcT", blk - 4)
                                          act(dst, A_[:], AF.Silu, [ak, "b_cb"], [dk_], bias=cb[:, blk:blk + 1])
                              for j in range(2):
                                  wk, wb = load_w(C_Z + j * 256, 256)
                                  proj_tm(wk, wb, 256, lambda pk, ps, t, j=j: act(sz[:, t, j * 256:(j + 1) * 256], ps[:, 0:256], AF.Silu,
                                                                                  [pk], [("b_sz", t)]))
                              wk, wb = load_w(C_DT, 16)
                              proj_tm(wk, wb, 16, lambda pk, ps, t: tt("dve", dtr[:, t, :], ps[:, 0:16], dtb[:], ALU.add,
                                                                        [pk, "b_dtb"], ["b_dt"]))
                              act(dtr[:], dtr[:], AF.Exp, ["b_dt"], ["b_dt"])
                              act(dtr[:], dtr[:], AF.Ln, ["b_dt"], ["b_dt"], bias=1.0)
                              tt("dve", dta[:], dtr[:], nga[:].unsqueeze(1).to_broadcast([128, NT, 16]), ALU.mult, ["b_dt", "b_nga"], ["b_dta"])
                              for q4 in range(4):
                                  if q4 < 2:
                                      wr = [("uT", tq, kc) for tq in range(2) for kc in range(q4 * 8, (q4 + 1) * 8)]
                                  elif q4 == 2:
                                      wr = [("w", 0), ("w", 1)]
                                  else:
                                      wr = [("w", 2), ("w", 3)]
                                  dma("pool", wo[:, q4 * 4:(q4 + 1) * 4, :],
                                      w_out[l, q4 * 512:(q4 + 1) * 512, :].rearrange("(kc p) n -> p kc n", p=128),
                                      [], wr + [("wo", q4)], nobarrier=True)
                              for t in range(NT):
                                  tsl = slice(t * 128, (t + 1) * 128)
                                  pk, ps = PS()
                                  pb = ps[:].bitcast(BF)
                                  for blk in range(4):
                                      tr(pb[:, blk * 128:(blk + 1) * 128], xsT[:, blk, tsl], identb[:], [("b_xsT", blk)] + KALL, [pk])
                                  evac(xs[:, t, :], pb[:, 0:512], [pk], [("b_xs", t)])
                                  pk, ps = PS()
                                  pb = ps[:].bitcast(BF)
                                  for gq in range(2):
                                      tr(pb[:, gq * 128:(gq + 1) * 128], bcT[:, gq, tsl], identb[:], [("b_bcT", gq)] + KALL, [pk])
                                  evac(btm[:, t, :], pb[:, 0:256], [pk], [("b_btm", t)])
                              with ExitStack() as s3:
                                  dec = [sb(f"b_dec{i}", [128, 8, 128], BF, s3) for i in range(2)]
                                  MT = [sb(f"b_mt{i}", [128, 8, 128], BF, s3) for i in range(2)]
                                  xdt = [sb(f"b_xdt{i}", [128, 8, 64], BF, s3) for i in range(2)]
                                  xdw = [sb(f"b_xdw{i}", [128, 8, 64], BF, s3) for i in range(2)]
                                  cbs = [sb(f"b_cbs{i}", [128, 2, 128], F32, s3) for i in range(2)]
                                  lcs = [sb(f"b_lcs{i}", [128, 48], F32, s3) for i in range(2)]
                                  tmp = [sb(f"b_tmp{i}", [128, 512], F32, s3) for i in range(2)]
                                  hT = sb("b_hT", [128, 512], F32, s3)
                                  hTb = sb("b_hTb", [128, 512], BF, s3)
                                  hst = sb("b_hst", [128, 4, 128], F32, s3)
                                  nch = SL // 128
                                  items = []
                                  for s_ in range(nseq):
                                      for d in range(2):
                                          order = list(range(nch)) if d == 0 else list(range(nch - 1, -1, -1))
                                          for oi, c in enumerate(order):
                                              items.append((s_, d, c, oi == 0, oi == nch - 1))
                                  ctxs = {}

                                  def alpha(n):
                                      s_, d, c, first, last = items[n]
                                      tri = trif if d == 0 else trib
                                      neg4 = negf if d == 0 else negb
                                      t = s_ * nch + c
                                      tsl = slice(t * 128, (t + 1) * 128)
                                      b = n % 2
                                      if l == 0:
                                          ada_step()
                                      a_ = dta[:, t, d * 8:(d + 1) * 8]
                                      dt_ = dtr[:, t, d * 8:(d + 1) * 8]
                                      DE, M_, XD, XW, CB, LC = dec[b], MT[b], xdt[b], xdw[b], cbs[b], lcs[b]
                                      kDE, kM, kXD, kXW, kCB, kLC = [(n_, b) for n_ in
                                                                     ("b_dec", "b_mt", "b_xdt", "b_xdw", "b_cbs", "b_lcs")]
                                      pkL, psL = PS()
                                      mm(psL[:, 0:8], tri[:], a_, True, True, ["b_dta"] + KALL, [pkL])
                                      mm(psL[:, 8:16], onesf[:], a_, True, True, ["b_dta"] + KALL, [pkL])
                                      cp("dve", LC[:, 0:16], psL[:, 0:16], [pkL], [kLC])
                                      ts("dve", LC[:, 16:24], LC[:, 0:8], -1.0, None, ALU.mult, None, [kLC], [kLC])
                                      tt("dve", LC[:, 32:40], LC[:, 8:16], LC[:, 0:8], ALU.subtract, [kLC], [kLC])
                                      act(LC[:, 24:32], LC[:, 0:8], AF.Exp, [kLC], [kLC])
                                      act(LC[:, 32:40], LC[:, 32:40], AF.Exp, [kLC], [kLC])
                                      act(LC[:, 40:48], LC[:, 8:16], AF.Exp, [kLC], [kLC])
                                      for hg in range(2):
                                          pkD, psD = PS()
                                          mm(psD[:, :], identb[:], neg4[:], True, False, KALL, [pkD])
                                          for hh in range(4):
                                              h = hg * 4 + hh
                                              mm(psD[:, hh * 128:(hh + 1) * 128], a_[:, h:h + 1].to_broadcast([128, 128]), tri[:],
                                                 False, hh == 3, ["b_dta"] + KALL, [pkD])
                                          for hh in range(4):
                                              h = hg * 4 + hh
                                              act(DE[:, h, :], psD[:, hh * 128:(hh + 1) * 128], AF.Exp, [pkD, kLC], [kDE],
                                                  bias=LC[:, 16 + h:17 + h])
                                      pkC, psC = PS()
                                      for gq in range(2):
                                          mm(psC[:, gq * 128:(gq + 1) * 128], bcT[:, gq, tsl], bcT[:, 2 + gq, tsl], True, True,
                                             [("b_bcT", gq), ("b_bcT", 2 + gq)], [pkC])
                                      evac(CB[:].rearrange("p a b -> p (a b)"), psC[:, 0:256], [pkC], [kCB])
                                      for gq in range(2):
                                          tt("dve", M_[:, gq * 4:(gq + 1) * 4, :], DE[:, gq * 4:(gq + 1) * 4, :],
                                             CB[:, gq, :].unsqueeze(1).to_broadcast([128, 4, 128]), ALU.mult, [kDE, kCB], [kM])
                                      tt("pool", XD[:], xs[:, t, :].rearrange("p (h q) -> p h q", h=8),
                                         dt_.unsqueeze(2).to_broadcast([128, 8, 64]), ALU.mult, [("b_xs", t), "b_dt"], [kXD])
                                      tt("pool", XW[:], XD[:], LC[:, 32:40].unsqueeze(2).to_broadcast([128, 8, 64]), ALU.mult,
                                         [kXD, kLC], [kXW])

                                  def beta(n):
                                      s_, d, c, first, last = items[n]
                                      t = s_ * nch + c
                                      tsl = slice(t * 128, (t + 1) * 128)
                                      b = n % 2
                                      M_, XD, XW, LC, TM = MT[b], xdt[b], xdw[b], lcs[b], tmp[b]
                                      kM, kXD, kXW, kLC, kTM = [(n_, b) for n_ in ("b_mt", "b_xdt", "b_xdw", "b_lcs", "b_tmp")]
                                      if first:
                                          if smp:
                                              dma("sp", hst[:], st_in[d][l].rearrange("(a p) n -> p a n", p=128), [], ["b_hst"])
                                              pk, ps = PS()
                                              for a in range(4):
                                                  tr(ps[:, a * 128:(a + 1) * 128], hst[:, a, :], identf[:], ["b_hst"] + KALL, [pk])
                                              cp("dve", hT[:], ps[:, :], [pk], ["b_hT"])
                                              cp("act", hTb[:], hT[:], ["b_hT"], ["b_hTb"])
                                          else:
                                              S.op("pool", (lambda e, hT=hT: e.memset(hT[:], 0.0)), [], ["b_hT"])
                                              S.op("pool", (lambda e, hTb=hTb: e.memset(hTb[:], 0.0)), [], ["b_hTb"])
                                      pkY, psY = PS()
                                      for h in range(8):
                                          mm(psY[:, h * 64:(h + 1) * 64], M_[:, h, :], XD[:, h, :], True, True, [kM, kXD], [pkY])
                                      pkY2, psY2 = PS()
                                      for gq in range(2):
                                          mm(psY2[:, gq * 256:(gq + 1) * 256], bcT[:, 2 + gq, tsl], hTb[:, gq * 256:(gq + 1) * 256],
                                             True, True, [("b_bcT", 2 + gq), "b_hTb"], [pkY2])
                                      tt("dve", TM[:].rearrange("p (h q) -> p h q", h=8), psY2[:, :].rearrange("p (h q) -> p h q", h=8),
                                         LC[:, 24:32].unsqueeze(2).to_broadcast([128, 8, 64]), ALU.mult, [pkY2, kLC], [kTM])
                                      if d == 0:
                                          tt("dve", ysum[:, t, :], psY[:, :], TM[:], ALU.add, [pkY, kTM], [("b_y", t)])
                                      else:
                                          tt("dve", TM[:], psY[:, :], TM[:], ALU.add, [pkY, kTM], [kTM])
                                          tt("pool", ysum[:, t, :], ysum[:, t, :], TM[:], ALU.add, [kTM, ("b_y", t)], [("b_y", t)])
                                      pkS, psS = PS()
                                      for gq in range(2):
                                          mm(psS[:, gq * 256:(gq + 1) * 256], btm[:, t, gq * 128:(gq + 1) * 128],
                                             XW[:, gq * 4:(gq + 1) * 4, :].rearrange("p a b -> p (a b)"), True, True,
                                             [("b_btm", t), kXW], [pkS])
                                      tt("dve", hT[:].rearrange("p (h q) -> p h q", h=8), hT[:].rearrange("p (h q) -> p h q", h=8),
                                         LC[:, 40:48].unsqueeze(2).to_broadcast([128, 8, 64]), ALU.mult, ["b_hT", kLC], ["b_hT"])
                                      tt("dve", hT[:], hT[:], psS[:, :], ALU.add, ["b_hT", pkS], ["b_hT"])
                                      cp("act", hTb[:], hT[:], ["b_hT"], ["b_hTb"])
                                      if last and not smp:
                                          pk, ps = PS()
                                          for a in range(4):
                                              tr(ps[:, a * 128:(a + 1) * 128], hT[:, a * 128:(a + 1) * 128], identf[:], ["b_hT"] + KALL, [pk])
                                          cp("dve", hst[:].rearrange("p a b -> p (a b)"), ps[:, :], [pk], ["b_hst"])
                                          dma("sp", o_st[d][s_, l].rearrange("(a p) n -> p a n", p=128), hst[:], ["b_hst"], ["o_st"])

                                  alpha(0)
                                  for n in range(len(items)):
                                      if n + 1 < len(items):
                                          alpha(n + 1)
                                      beta(n)
                                  ob = [dec[i][:, 0:4, :].rearrange("p a b -> p (a b)") for i in range(2)]
                                  st6 = [sb(f"b_st{i}", [128, 8], F32, s3) for i in range(2)]
                                  for t in range(NT):
                                      b = t % 2
                                      TM, OB, ST = tmp[b], ob[b], st6[b]
                                      kTM, kOB, kST = ("b_tmp", b), ("b_ob", b), ("b_st", b)
                                      yk = ("b_y", t)
                                      tt("pool", TM[:].rearrange("p (h q) -> p h q", h=8), xs[:, t, :].rearrange("p (h q) -> p h q", h=8),
                                         dbc[:].unsqueeze(2).to_broadcast([128, 8, 64]), ALU.mult, [("b_xs", t), "b_dbc"], [kTM])
                                      tt("dve", ysum[:, t, :], ysum[:, t, :], TM[:], ALU.add, [yk, kTM], [yk])
                                      tt("dve", ysum[:, t, :], ysum[:, t, :], sz[:, t, :], ALU.mult, [yk, ("b_sz", t)], [yk])
                                      S.op("dve", (lambda e, ST=ST, t=t, ysum=ysum: e.bn_stats(ST[:, 0:6], ysum[:, t, :])), [yk], [kST])
                                      S.op("dve", (lambda e, ST=ST: e.bn_aggr(ST[:, 6:8], ST[:, 0:6])), [kST], [kST])
                                      stt("dve", ST[:, 0:1], ST[:, 6:7], ST[:, 6:7], ST[:, 7:8], ALU.mult, ALU.add, [kST], [kST])
                                      rstd_op(ST[:, 1:2], ST[:, 0:1], [kST])
                                      stt("dve", OB[:], ysum[:, t, :], ST[:, 1:2], nwb[:], ALU.mult, ALU.mult, [yk, kST, "b_nwb"], [kOB])
                                      pk, ps = PS()
                                      pb = ps[:].bitcast(BF)
                                      for blk in range(4):
                                          tr(pb[:, blk * 128:(blk + 1) * 128], OB[:, blk * 128:(blk + 1) * 128], identb[:], [kOB] + KALL, [pk])
                                      evac(mixedT[:, 4:8, t * 128:(t + 1) * 128], pb[:, 0:512].rearrange("p (a b) -> p a b", a=4),
                                           [pk], [("mx", 4 + i) for i in range(4)])
                              S.barrier(False)
                      else:
                          for fc in range(4, 8):
                              S.op("pool", (lambda e, fc=fc: e.memset(mixedT[:, fc, :], 0.0)), [], [("mx", fc)])

                      S.barrier(False)
                      if DEBUG and l == 0:
                          dma("pool", dbg_mixed[l * 2 + g], mixedT[:], [("mx", i) for i in range(16)], ["dbg"])
                          S.barrier(False)
                      stage((l * 2 + g) * 10 + 3)

                  S.barrier(False)
                  if S.stopped:
                      break

                  stage((l * 2 + g) * 10 + 4)
                  if S.stopped:
                      break
                  with ExitStack() as s1:
                      gt = sb("gt", [128, D], F32, s1)
                      lg = sb("lg", [128, D], F32, s1)
                      lb = sb("lb", [128, D], F32, s1)
                      xt = [sb(f"oxt{i}", [128, D], F32, s1) for i in range(3)]
                      rr = [sb(f"orr{i}", [128, D], F32, s1) for i in range(2)]
                      st6 = [sb(f"ost6{i}", [128, 4, 6], F32, s1) for i in range(2)]
                      mv = [sb(f"omv{i}", [128, 4], F32, s1) for i in range(2)]
                      dma("sp", gt[:], modscr[l, g, 2 * D:3 * D].partition_broadcast(128), ["modscr"], ["gt"])
                      dma("sp", lg[:], ln_g[l, :].partition_broadcast(128), [], ["lg"])
                      dma("sp", lb[:], ln_b[l, :].partition_broadcast(128), [], ["lb"])
                      def xload(t_):
                          dma("sp", xt[t_ % 3][:], xsrc[t_ * 128:(t_ + 1) * 128, :], [("y1", g)] if l else [], [("oxt", t_ % 3)])

                      xload(0)
                      xload(1)
                      for t in range(NT):
                          b = t % 2
                          X, R_, ST, MV = xt[t % 3], rr[b], st6[b], mv[b]
                          kx, kr, kst, kmv = ("oxt", t % 3), ("orr", b), ("ost6", b), ("omv", b)
                          if t + 2 < NT:
                              xload(t + 2)
                          act(X[:], X[:], AF.Copy, [kx], [kx], scale=ALPHA)
                          for c4 in range(4):
                              pk, ps = PS()
                              for kc in range(16):
                                  mm(ps[:, :], mixedT[:, kc, t * 128:(t + 1) * 128], wo[:, kc, c4 * 512:(c4 + 1) * 512],
                                     kc == 0, kc == 15, [("mx", kc), ("wo", kc // 4)], [pk])
                              csl = slice(c4 * 512, (c4 + 1) * 512)
                              tt("dve", R_[:, csl], ps[:, :], gt[:, csl], ALU.mult, [pk, "gt"], [(kr, c4)])
                              tt("dve", R_[:, csl], R_[:, csl], X[:, csl], ALU.add, [kx, (kr, c4)], [(kr, c4)])
                              S.op("dve", (lambda e, R_=R_, ST=ST, c4=c4, csl=csl: e.bn_stats(ST[:, c4, :], R_[:, csl])),
                                   [(kr, c4)], [kst])
                          RK = [(kr, c4) for c4 in range(4)]
                          S.op("dve", (lambda e, ST=ST, MV=MV: e.bn_aggr(MV[:, 0:2], ST[:].rearrange("p a b -> p (a b)"))),
                               [kst], [kmv])
                          rstd_op(MV[:, 2:3], MV[:, 1:2], [kmv])
                          stt("dve", MV[:, 3:4], MV[:, 0:1], -1.0, MV[:, 2:3], ALU.mult, ALU.mult, [kmv], [kmv])
                          act(R_[:], R_[:], AF.Identity, RK + [kmv], RK, bias=MV[:, 3:4], scale=MV[:, 2:3])
                          tt("dve", R_[:], R_[:], lg[:], ALU.mult, RK + ["lg"], RK)
                          tt("pool", R_[:], R_[:], lb[:], ALU.add, RK + ["lb"], RK)
                          dma("sp", ydst[t * 128:(t + 1) * 128, :], R_[:], RK, [("y1", g)] if l == 0 else ["yout"])
                      S.barrier()
                  stage((l * 2 + g) * 10 + 5)
                  if S.stopped:
                      break
                  if l == 0 and g == 1:
                      while ada1["pending"] or ada1["next"] < 24:
                          ada_step()
                      ada_gate_store(1)
                      S.barrier(False)
              if S.stopped:
                  break

        except _Stop:
            pass

        S.stopped = False
        for e in Sched.ENG:
            S.op(e, lambda en: en.nop(), [], [])
        S.finalize()

        with nc.Block() as block:
            @block.tensor
            def _(eng):
                S.emit("pe", eng)

            @block.scalar
            def _(eng):
                S.emit("act", eng)

            @block.vector
            def _(eng):
                S.emit("dve", eng)

            @block.gpsimd
            def _(eng):
                S.emit("pool", eng)

            @block.sync
            def _(eng):
                S.emit("sp", eng)
    return nc


_NC_CACHE = {}


def kernel(x_prompt, x_sample, cache_attn_k, cache_attn_v, cache_na_k, cache_na_v,
           state_ssm_fwd, state_ssm_bwd, c, c_ctx, w_ada, b_ada, w_in, w_out, ln_g, ln_b,
           attn_sink, ssm_conv_w, ssm_conv_b, ssm_a_log, ssm_dt_bias, ssm_d, ssm_norm_w,
           pool_w, pool_b, pool_scale, na_rpb):
    f = lambda a: np.ascontiguousarray(np.asarray(a, dtype=np.float32))
    if "nc" not in _NC_CACHE:
        _NC_CACHE["nc"] = build_program()
    nc = _NC_CACHE["nc"]
    consts = _consts()
    shared = {
        "w_ada": f(w_ada), "b_ada": f(b_ada), "w_in": f(w_in), "w_out": f(w_out),
        "ln_g": f(ln_g), "ln_b": f(ln_b), "attn_sink": f(attn_sink),
        "ssm_conv_w": f(ssm_conv_w), "ssm_conv_b": f(ssm_conv_b),
        "ssm_a_log": f(ssm_a_log).reshape(2, 16), "ssm_dt_bias": f(ssm_dt_bias).reshape(2, 16),
        "ssm_d": f(ssm_d), "ssm_norm_w": f(ssm_norm_w), "pool_w": f(pool_w),
        "pool_b": f(pool_b).reshape(2, 512), "pool_scale": f(pool_scale),
        "na_rpb": f(na_rpb).reshape(2, 4 * 15 * 31),
    }
    for k, v in consts.items():
        shared["c_" + k] = v
    xp = f(x_prompt); xs = f(x_sample)
    in_maps = []
    for ci in range(NCORES):
        b = ci % 4
        m = dict(shared)
        m["xp"] = xp[4 * ci:4 * ci + 4].reshape(T, D)
        m["xs"] = xs[b].reshape(T, D)
        m["cak"] = f(cache_attn_k)[b].reshape(2, 256, 256)
        m["cav"] = f(cache_attn_v)[b].reshape(2, 256, 256)
        m["cnk"] = f(cache_na_k)[b].reshape(2, 256, 512)
        m["cnv"] = f(cache_na_v)[b].reshape(2, 256, 512)
        m["sf"] = f(state_ssm_fwd)[b].reshape(2, 512, 128)
        m["sb"] = f(state_ssm_bwd)[b].reshape(2, 512, 128)
        m["cvec"] = np.stack([f(c_ctx), f(c)[b]], 0)
        in_maps.append(m)
    res = run_bass_kernel_spmd(nc, in_maps, core_ids=list(range(NCORES)))
    R = res.results
    _NC_CACHE["last"] = R
    y_prompt = np.concatenate([np.asarray(R[ci]["yp"]).reshape(4, 256, D) for ci in range(NCORES)], 0)
    y_sample = np.stack([np.asarray(R[ci]["ys"]).reshape(1024, D) for ci in range(4)], 0)
    cat = lambda name, shp: np.concatenate([np.asarray(R[ci][name]).reshape(shp) for ci in range(NCORES)], 0)
    nak = cat("o_ak", (4, 2, 256, 2, 128)); nav = cat("o_av", (4, 2, 256, 2, 128))
    nnk = cat("o_nk", (4, 2, 256, 4, 128)); nnv = cat("o_nv", (4, 2, 256, 4, 128))
    nsf = cat("o_sf", (4, 2, 8, 64, 128)); nsb = cat("o_sb", (4, 2, 8, 64, 128))
    return (y_prompt.astype(np.float32), y_sample.astype(np.float32), nak, nav, nnk, nnv, nsf, nsb)
```

```python
import numpy as np
import concourse.bass as bass
import concourse.mybir as mybir
from concourse.bass_utils import run_bass_kernel_spmd

F32 = mybir.dt.float32
BF = mybir.dt.bfloat16
AF = mybir.ActivationFunctionType
ALU = mybir.AluOpType
AX = mybir.AxisListType

NCORES = 8
D = 2048
DP = 6160
T = 1024
NT = 8
EPS = 1e-6
ALPHA = 4.0 ** 0.25
SCALE = 128.0 ** -0.5
NEG = -1e30
DEBUG = False
STAGE = 99
FEAT = 99


class _Stop(Exception):
    pass

C_QA, C_KA, C_VA, C_GA, C_XBC, C_Z, C_DT, C_PC, C_GC, C_QD, C_KD, C_VD, C_GD = (
    0, 512, 768, 1024, 1536, 2560, 3072, 3088, 3600, 4112, 4624, 5136, 5648)


class Op:
    __slots__ = ("eng", "fn", "deps", "sem", "val", "needed", "isdma", "phase", "idx")


class Sched:
    ENG = ["pe", "act", "dve", "pool", "sp"]

    def __init__(self, nc, dma_sems, prog_sems):
        self.nc = nc
        self.ops = {e: [] for e in self.ENG}
        self.lastw = {}
        self.readers = {}
        self.pending = {e: [] for e in self.ENG}
        self.dma_sems = dma_sems
        self.dma_last = [None] * len(dma_sems)
        self.dma_cnt = [0] * len(dma_sems)
        self.dma_rr = 0
        self.dma_rr_pool = 0
        self.prog_sems = prog_sems
        self.phase = 0
        self.phase_dmas = []
        self.psn = 0
        self.nops = 0
        self.stopped = False

    def op(self, eng, fn, reads=(), writes=(), dma=False, nobarrier=False):
        if self.stopped:
            return None
        o = Op()
        o.eng, o.fn, o.isdma, o.needed, o.phase = eng, fn, dma, False, self.phase
        o.sem = None
        o.val = 0
        o.idx = self.nops
        self.nops += 1
        if nobarrier:
            deps = []
        else:
            deps = list(self.pending[eng])
            self.pending[eng] = []
        for k in reads:
            w = self.lastw.get(k)
            if w is not None:
                deps.append(w)
            self.readers.setdefault(k, []).append(o)
        for k in writes:
            w = self.lastw.get(k)
            if w is not None:
                deps.append(w)
            for r in self.readers.get(k, []):
                if r is not o:
                    deps.append(r)
            self.readers[k] = []
            self.lastw[k] = o
        if dma:
            half = len(self.dma_sems) // 2
            if eng == "pool":
                j = half + self.dma_rr_pool
                self.dma_rr_pool = (self.dma_rr_pool + 1) % half
            else:
                j = self.dma_rr
                self.dma_rr = (self.dma_rr + 1) % half
            if self.dma_last[j] is not None:
                deps.append(self.dma_last[j])
            self.dma_last[j] = o
            self.dma_cnt[j] += 16
            o.sem = self.dma_sems[j]
            o.val = self.dma_cnt[j]
            o.needed = True
            self.phase_dmas.append(o)
        fl = []
        for d in deps:
            if d is o:
                continue
            if (not d.isdma) and (not dma) and d.eng == eng:
                if eng == "pe":
                    continue
            fl.append(d)
            d.needed = True
        o.deps = fl
        self.ops[eng].append(o)
        return o

    def barrier(self, new_phase=True, final=False):
        if self.stopped:
            return
        lasts = []
        for e in self.ENG:
            for o in reversed(self.ops[e]):
                if not o.isdma:
                    lasts.append(o)
                    break
        lasts += self.phase_dmas
        self.phase_dmas = []
        for o in lasts:
            o.needed = True
        for e in self.ENG:
            if e == "pe" and not final:
                continue
            self.pending[e] = self.pending[e] + list(lasts)
        if new_phase:
            self.phase += 1

    def finalize(self):
        cnt = {}
        for e in self.ENG:
            for o in self.ops[e]:
                if o.isdma:
                    continue
                if o.needed:
                    k = (e, o.phase)
                    cnt[k] = cnt.get(k, 0) + 1
                    o.sem = self.prog_sems[k]
                    o.val = cnt[k]

    def emit(self, e, eng):
        seen = {}
        for o in self.ops[e]:
            for d in o.deps:
                sid = id(d.sem)
                if seen.get(sid, 0) < d.val:
                    eng.wait_ge(d.sem, d.val)
                    seen[sid] = d.val
            ins = o.fn(eng)
            if o.needed:
                ins.then_inc(o.sem, 16 if o.isdma else 1)


def _consts():
    c = {}
    i = np.arange(128)
    c["ident"] = np.eye(128, dtype=np.float32)
    c["ones"] = np.ones((128, 128), np.float32)
    c["trif"] = (i[:, None] <= i[None, :]).astype(np.float32)
    c["trib"] = (i[:, None] >= i[None, :]).astype(np.float32)
    nf = np.where(i[None, :] >= i[:, None], 0.0, NEG).astype(np.float32)
    nb = np.where(i[None, :] <= i[:, None], 0.0, NEG).astype(np.float32)
    c["negf"] = np.tile(nf, (1, 4))
    c["negb"] = np.tile(nb, (1, 4))
    c["mprev"] = (i[:, None] >= i[None, :]).astype(np.float32)
    c["mnext"] = (i[:, None] <= i[None, :]).astype(np.float32)
    sw = np.zeros((128, 128), np.float32)
    for m in range(128):
        p = m + 32 if (m % 64) < 32 else m - 32
        sw[p, m] = 1.0
    c["pswap"] = sw
    t = np.arange(1024)
    rows = (t // 64).astype(np.float32)
    cols = (t % 64).astype(np.float32)
    inv = (10000.0 ** (-np.arange(32, dtype=np.float32) / 32.0)).astype(np.float32)
    cos = np.zeros((128, 1024), np.float32)
    sin = np.zeros((128, 1024), np.float32)
    for m in range(128):
        pos = rows if m < 64 else cols
        ang = (pos * inv[m % 32]).astype(np.float32)
        cos[m] = np.cos(ang)
        sgn = -1.0 if (m % 64) < 32 else 1.0
        sin[m] = sgn * np.sin(ang)
    c["cos"] = cos
    c["sin"] = sin

    def rc(Tn):
        tt = np.arange(Tn)
        out = np.zeros((4, Tn), np.float32)
        for g, w in enumerate((2, 4, 8, 16)):
            lo = np.clip(tt - w // 2, 0, Tn)
            hi = np.clip(tt - w // 2 + w, 0, Tn)
            out[g] = 1.0 / (hi - lo).astype(np.float32)
        return out
    c["rcp"] = rc(256)
    c["rcs"] = rc(1024)
    cc = np.arange(64)
    cs = np.clip(cc - 8, 0, 48)
    ok = (cc[:, None] >= cs[None, :]) & (cc[:, None] < cs[None, :] + 16)
    c["colok"] = ok.astype(np.float32)
    c["colneg"] = np.where(ok, 0.0, NEG).astype(np.float32)
    c["jrev"] = np.eye(64, dtype=np.float32)[::-1].copy()
    s0 = np.zeros((64, 128), np.float32)
    s1 = np.zeros((64, 128), np.float32)
    s0[cc, cc] = 1.0
    s1[cc, cc + 64] = 1.0
    c["sel0"] = s0
    c["sel1"] = s1
    c["negblk"] = np.full((64, 64), NEG, np.float32)
    c["zeros"] = np.zeros((128, 2048), np.float32)
    return c


CONST_SHAPES = {k: v.shape for k, v in _consts().items()}

def _na_ranges():
    rs = [min(max(r - 4, 0), 8) for r in range(16)]
    R = []
    for kr in range(16):
        rr = [r for r in range(16) if rs[r] <= kr <= rs[r] + 7]
        R.append((rr[0], rr[-1]))
    return R


NA_R = _na_ranges()


def build_program():
    nc = bass.Bass("TRN2", target_bir_lowering=False)

    def din(name, shape, dt=F32):
        return nc.dram_tensor(name, list(shape), dt, kind="ExternalInput").ap()

    def dout(name, shape, dt=F32):
        return nc.dram_tensor(name, list(shape), dt, kind="ExternalOutput").ap()

    def dscr(name, shape, dt=F32):
        return nc.dram_tensor(name, list(shape), dt, kind="Internal").ap()

    xin = [din("xp", [T, D]), din("xs", [T, D])]
    cak = din("cak", [2, 256, 256]); cav = din("cav", [2, 256, 256])
    cnk = din("cnk", [2, 256, 512]); cnv = din("cnv", [2, 256, 512])
    st_in = [din("sf", [2, 512, 128]), din("sb", [2, 512, 128])]
    cvec = din("cvec", [2, D])
    w_ada = din("w_ada", [2, D, 3 * D]); b_ada = din("b_ada", [2, 3 * D])
    w_in = din("w_in", [2, D, DP]); w_out = din("w_out", [2, D, D])
    ln_g = din("ln_g", [2, D]); ln_b = din("ln_b", [2, D])
    attn_sink = din("attn_sink", [2, 4])
    conv_w = din("ssm_conv_w", [2, 5, 1024]); conv_b = din("ssm_conv_b", [2, 1024])
    a_log = din("ssm_a_log", [2, 16]); dt_bias = din("ssm_dt_bias", [2, 16])
    ssm_d = din("ssm_d", [2, 8]); norm_w = din("ssm_norm_w", [2, 512])
    pool_w = din("pool_w", [2, 4, 128, 128]); pool_b = din("pool_b", [2, 512])
    pool_scale = din("pool_scale", [2, 512]); rpb = din("na_rpb", [2, 4 * 15 * 31])
    cst = {k: din("c_" + k, list(s)) for k, s in CONST_SHAPES.items()}

    yout = [dout("yp", [T, D]), dout("ys", [T, D])]
    o_ak = dout("o_ak", [4, 2, 256, 256]); o_av = dout("o_av", [4, 2, 256, 256])
    o_nk = dout("o_nk", [4, 2, 256, 512]); o_nv = dout("o_nv", [4, 2, 256, 512])
    o_st = [dout("o_sf", [4, 2, 512, 128]), dout("o_sb", [4, 2, 512, 128])]
    if DEBUG:
        dbg_mixed = dout("dbg_mixed", [4, 128, 16, T])

    y1 = [dscr("y1p", [T, D]), dscr("y1s", [T, D])]
    modscr = dscr("modscr", [2, 2, 3 * D])
    RPAD = 64
    rpbscr = dscr("rpbscr", [2, RPAD + 4 * 15 * 31 + RPAD])

    from contextlib import ExitStack
    es = ExitStack()
    with es:
        ndma = 24
        dma_sems = [es.enter_context(nc.semaphore(f"dq{i}")) for i in range(ndma)]
        NPH = 12
        prog_sems = {}
        for e in Sched.ENG:
            for ph in range(NPH):
                prog_sems[(e, ph)] = es.enter_context(nc.semaphore(f"pg_{e}{ph}"))
        S = Sched(nc, dma_sems, prog_sems)

        _nm = [0]

        def sb(name, shape, dt, stack=es):
            _nm[0] += 1
            return stack.enter_context(nc.sbuf_tensor(f"{name}_{_nm[0]}", list(shape), dt))

        psum = [es.enter_context(nc.psum_tensor(f"ps{i}", [128, 512], F32)) for i in range(8)]

        def PS():
            i = S.psn
            S.psn = (S.psn + 1) % 8
            return ("ps", i), psum[i]

        def mm(out, lhsT, rhs, start, stop, reads, writes):
            return S.op("pe", lambda e: e.matmul(out, lhsT, rhs, start=start, stop=stop,
                                                 skip_group_check=True), reads, writes)

        def tr(out, in_, ident, reads, writes):
            return S.op("pe", lambda e: e.transpose(out, in_, ident), reads, writes)

        def act(out, in_, func, reads, writes, bias=0.0, scale=1.0):
            return S.op("act", lambda e: e.activation(out, in_, func, bias=bias, scale=scale),
                        reads, writes)

        def tt(eng, out, in0, in1, op, reads, writes):
            return S.op(eng, lambda e: e.tensor_tensor(out, in0, in1, op), reads, writes)

        def ts(eng, out, in0, s1, s2, op0, op1, reads, writes):
            if s2 is None:
                return S.op(eng, lambda e: e.tensor_scalar(out, in0, s1, None, op0), reads, writes)
            return S.op(eng, lambda e: e.tensor_scalar(out, in0, s1, s2, op0, op1), reads, writes)

        def stt(eng, out, in0, scalar, in1, op0, op1, reads, writes):
            return S.op(eng, lambda e: e.scalar_tensor_tensor(out, in0, scalar, in1, op0, op1),
                        reads, writes)

        def cp(eng, out, in_, reads, writes):
            if eng == "act":
                return S.op("act", lambda e: e.activation(out, in_, AF.Copy), reads, writes)
            return S.op(eng, lambda e: e.tensor_copy(out, in_), reads, writes)

        def dma(q, out, in_, reads, writes, nonc=False, nobarrier=False):
            if nonc:
                return S.op(q, lambda e: e.dma_start(out=out, in_=in_, allow_slow_non_contiguous=True),
                            reads, writes, dma=True, nobarrier=nobarrier)
            return S.op(q, lambda e: e.dma_start(out=out, in_=in_), reads, writes, dma=True, nobarrier=nobarrier)

        evrr = [0]

        def evac(out, in_, reads, writes, scale=None):
            evrr[0] ^= 1
            if evrr[0]:
                if scale is None:
                    return cp("act", out, in_, reads, writes)
                return act(out, in_, AF.Copy, reads, writes, scale=scale)
            if scale is None:
                return cp("dve", out, in_, reads, writes)
            return ts("dve", out, in_, scale, None, ALU.mult, None, reads, writes)

        def cload(name, shape, dt, src=None, q=None):
            t_ = sb("k_" + name, shape, dt)
            src = cst[name] if src is None else src
            dma("pool" if dt == BF else "sp", t_[:], src, [], [("k", name, str(dt))])
            return t_

        identf = cload("ident", [128, 128], F32); identb = cload("ident", [128, 128], BF)
        onesf = cload("ones", [128, 128], F32); onesb = cload("ones", [128, 128], BF)
        trif = cload("trif", [128, 128], F32); trib = cload("trib", [128, 128], F32)
        negf = cload("negf", [128, 512], BF); negb = cload("negb", [128, 512], BF)
        mprev = cload("mprev", [128, 128], BF); mnext = cload("mnext", [128, 128], BF)
        pswap = cload("pswap", [128, 128], BF)
        cosT = cload("cos", [128, 1024], F32); sinT = cload("sin", [128, 1024], F32)
        colok = cload("colok", [64, 64], F32); colneg = cload("colneg", [64, 64], F32)
        jrev = cload("jrev", [64, 64], F32)
        sel0 = cload("sel0", [64, 128], BF); sel1 = cload("sel1", [64, 128], BF)
        negblk = cload("negblk", [64, 64], BF)
        KALL = [("k", n, str(d)) for n in CONST_SHAPES for d in (F32, BF)]

        mixedT = sb("mixedT", [128, 16, T], BF)
        big = sb("big", [128, 32768], BF)
        uT = big[:, 0:16384].rearrange("p (a b) -> p a b", a=16)
        NW = 4
        wbuf = [big[:, 16384 + i * 4096:16384 + (i + 1) * 4096].rearrange("p (a b) -> p a b", a=16) for i in range(NW)]
        wo = big[:, 0:32768].rearrange("p (a b) -> p a b", a=16)
        from collections import deque
        epsT = sb("epsT", [128, 1], F32)
        S.op("pool", lambda e: e.memset(epsT[:], EPS), [], ["epsT"])

        def rstd_op(dst, var, kk):
            act(dst, var, AF.Sqrt, kk + ["epsT"], kk, bias=epsT[:, 0:1])
            S.op("dve", lambda e: e.reciprocal(dst, dst), kk, kk)

        def stage(n):
            if STAGE <= n and not S.stopped:
                S.barrier(False)
                S.stopped = True

        try:
          stage(0)
          modT = sb("modT", [128, 2, 48, 2], F32)
          bT = sb("ada_bT", [128, 2, 48], F32)
          scb = sb("ada_scb", [128, 16, 2], BF)
          adab = [sb(f"adab{i}", [128, 16, 256], BF) for i in range(2)]
          with ExitStack() as s0:
              sc = sb("ada_sc", [128, 16, 2], F32, s0)
              for v in range(2):
                  dma("sp", sc[:, :, v], cvec[v, :].rearrange("(kc p) -> p kc", p=128), [], ["ada_sc"], nonc=True)
              for l_ in range(2):
                  dma("sp", bT[:, l_, :], b_ada[l_, :].rearrange("(c p) -> p c", p=128), [], ["ada_bT"], nonc=True)
              act(scb[:], sc[:], AF.Silu, ["ada_sc"], ["ada_scb"])

              def ada_mm(l_, jb, wk, wb):
                  pk, ps = PS()
                  for half in range(2):
                      for kc in range(16):
                          mm(ps[:, half * 2:half * 2 + 2], wb[:, kc, half * 128:(half + 1) * 128], scb[:, kc, :],
                             kc == 0, kc == 15, ["ada_scb", wk], [pk])
                  tt("dve", modT[:, l_, jb * 2:jb * 2 + 2, :], ps[:, 0:4].rearrange("p (a b) -> p a b", a=2),
                     bT[:, l_, jb * 2:jb * 2 + 2].unsqueeze(2).to_broadcast([128, 2, 2]), ALU.add,
                     [pk, "ada_bT"], [("modT", l_, jb)])

              def ada_gate_store(l_):
                  for v in range(2):
                      dma("sp", modscr[l_, v, 2 * D:3 * D].rearrange("(kc p) -> p kc", p=128), modT[:, l_, 32:48, v],
                          [("modT", l_, jb) for jb in range(16, 24)], ["modscr"], nonc=True)

              for jb in range(24):
                  i_ = jb % NW
                  wk = ("w", i_)
                  dma("pool", wbuf[i_][:], w_ada[0, :, jb * 256:(jb + 1) * 256].rearrange("(kc p) n -> p kc n", p=128), [], [wk])
                  ada_mm(0, jb, wk, wbuf[i_])
              ada_gate_store(0)
              ada1 = {"next": 0, "pending": [], "free": [0, 1]}

              def ada_consume():
                  jb, bi = ada1["pending"].pop(0)
                  ada_mm(1, jb, ("adab", bi), adab[bi])
                  ada1["free"].append(bi)

              def ada_step():
                  if len(ada1["pending"]) == 2 or (ada1["next"] >= 24 and ada1["pending"]):
                      ada_consume()
                  jb = ada1["next"]
                  if jb < 24 and ada1["free"]:
                      bi = ada1["free"].pop(0)
                      dma("pool", adab[bi][:], w_ada[1, :, jb * 256:(jb + 1) * 256].rearrange("(kc p) n -> p kc n", p=128),
                          [], [("adab", bi)], nobarrier=True)
                      ada1["pending"].append((jb, bi))
                      ada1["next"] = jb + 1

              zt = sb("zt", [1, 2048], F32, s0)
              dma("sp", zt[:], cst["zeros"][0:1, :], [], ["zt"])
              for l in range(2):
                  dma("sp", rpbscr[l, :].rearrange("(o n) -> o n", o=1), zt[0:1, 0:RPAD * 2 + 1860],
                      ["zt"], [("rpbscr", l)])
                  dma("sp", rpbscr[l, RPAD:RPAD + 1860].rearrange("(o n) -> o n", o=1),
                      rpb[l, :].rearrange("(o n) -> o n", o=1), [], [("rpbscr", l)])
              S.barrier()
          stage(1)

          for l in range(2):
              for g in range(2):
                  smp = (g == 1)
                  nseq = 1 if smp else 4
                  SL = T // nseq
                  xsrc = xin[g] if l == 0 else y1[g]
                  ydst = y1[g] if l == 0 else yout[g]
                  with ExitStack() as s1:
                      wcnt = [0]
                      PRE = deque()

                      def _issue_w(c0, ncols, nobarrier):
                          i = wcnt[0] % NW
                          wcnt[0] += 1
                          key = ("w", i)
                          dma("pool", wbuf[i][:, :, 0:ncols],
                              w_in[l, :, c0:c0 + ncols].rearrange("(kc p) n -> p kc n", p=128), [], [key],
                              nobarrier=nobarrier)
                          return key, wbuf[i]

                      def load_w(c0, ncols):
                          if PRE:
                              (cc, kb) = PRE.popleft()
                              assert cc == (c0, ncols), (cc, c0, ncols)
                              return kb
                          return _issue_w(c0, ncols, False)

                      def prefetch(lst, nobarrier=True):
                          assert not PRE
                          for (c0, n_) in lst:
                              PRE.append(((c0, n_), _issue_w(c0, n_, nobarrier)))

                      def UKS(tq):
                          return [("uT", tq, kc) for kc in range(16)]

                      def proj_fm(wkey, wb, j0, evac_fn):
                          pss = [PS(), PS()]
                          for kc in range(16):
                              for tc in range(2):
                                  pk, ps = pss[tc]
                                  mm(ps[:, :], wb[:, kc, j0:j0 + 128], uT[:, kc, tc * 512:(tc + 1) * 512],
                                     kc == 0, kc == 15, [wkey, ("uT", tc, kc)], [pk])
                          for tc in range(2):
                              pk, ps = pss[tc]
                              evac_fn(pk, ps, tc)

                      def proj_tm(wkey, wb, ncols, evac_fn, tiles=range(NT)):
                          for t in tiles:
                              pk, ps = PS()
                              for kc in range(16):
                                  mm(ps[:, 0:ncols], uT[:, kc, t * 128:(t + 1) * 128], wb[:, kc, 0:ncols],
                                     kc == 0, kc == 15, [wkey, ("uT", t // 4, kc)], [pk])
                              evac_fn(pk, ps, t)

                      with ExitStack() as s2:
                          scT = sb("scT", [128, 16], F32, s2)
                          shT = sb("shT", [128, 16], F32, s2)
                          xt = [sb(f"xt{i}", [128, D], F32, s2) for i in range(3)]
                          xn = [sb(f"xn{i}", [128, D], BF, s2) for i in range(4)]
                          st6 = [sb(f"st6{i}", [128, 4, 6], F32, s2) for i in range(3)]
                          mv = [sb(f"mv{i}", [128, 4], F32, s2) for i in range(3)]
                          MK = [("modT", l, jb) for jb in range(16)]
                          cp("dve", shT[:], modT[:, l, 0:16, g], MK, ["shT"])
                          ts("dve", scT[:], modT[:, l, 16:32, g], 1.0, None, ALU.add, None, MK, ["scT"])
                          prefetch([(C_QA, 256), (C_QA + 256, 256), (C_KA, 256), (C_VA, 256)], nobarrier=False)
                          for tq in range(2):
                              for i in range(4):
                                  t = tq * 4 + i
                                  b = t % 3
                                  X, ST, MV, XN = xt[b], st6[b], mv[b], xn[i]
                                  kx, kst, kmv, kxn = ("xt", b), ("st6", b), ("mv", b), ("xn", i)
                                  dma("sp", X[:], xsrc[t * 128:(t + 1) * 128, :], [("y1", g)] if l else [], [kx])
                                  for c4 in range(4):
                                      S.op("dve", (lambda e, X=X, ST=ST, c4=c4: e.bn_stats(ST[:, c4, :], X[:, c4 * 512:(c4 + 1) * 512])),
                                           [kx], [kst])
                                  S.op("dve", (lambda e, ST=ST, MV=MV: e.bn_aggr(MV[:, 0:2], ST[:].rearrange("p a b -> p (a b)"))),
                                       [kst], [kmv])
                                  rstd_op(MV[:, 2:3], MV[:, 1:2], [kmv])
                                  stt("dve", MV[:, 3:4], MV[:, 0:1], -1.0, MV[:, 2:3], ALU.mult, ALU.mult, [kmv], [kmv])
                                  act(XN[:], X[:], AF.Identity, [kx, kmv], [kxn], bias=MV[:, 3:4], scale=MV[:, 2:3])
                              for kp in range(8):
                                  pk, ps = PS()
                                  pb = ps[:].bitcast(BF)
                                  for kk in range(2):
                                      kc = kp * 2 + kk
                                      for i in range(4):
                                          tr(pb[:, kk * 512 + i * 128:kk * 512 + (i + 1) * 128], xn[i][:, kc * 128:(kc + 1) * 128],
                                             identb[:], [("xn", i)] + KALL[:2], [pk])
                                  for kk in range(2):
                                      kc = kp * 2 + kk
                                      dst = uT[:, kc, tq * 512:(tq + 1) * 512]
                                      src_ = pb[:, kk * 512:(kk + 1) * 512]
                                      if kp % 2 == 0:
                                          act(dst, src_, AF.Identity, [pk, "scT", "shT"], [("uT", tq, kc)],
                                              bias=shT[:, kc:kc + 1], scale=scT[:, kc:kc + 1])
                                      else:
                                          ts("dve", dst, src_, scT[:, kc:kc + 1], shT[:, kc:kc + 1], ALU.mult, ALU.add,
                                             [pk, "scT", "shT"], [("uT", tq, kc)])
                          S.barrier(False)
                      stage((l * 2 + g) * 10 + 2)

                      def attn_branch(pref, nh, nkv, cq, ck, cv, cg, fc0, o_k, o_v, use_sink, cache_k, cache_v, na):
                          G = nh // nkv
                          kvw = nkv * 128
                          with ExitStack() as s2:
                              qT = sb(pref + "qT", [128, nh, T], BF, s2)
                              kT = sb(pref + "kT", [128, nkv, T], BF, s2)
                              vtm = sb(pref + "v", [128, NT, kvw], BF, s2)
                              sgT = sb(pref + "sg", [128, nh, T], BF, s2)
                              stg = [sb(pref + f"stg{i}", [128, 256], F32, s2) for i in range(2)]
                              NPT = 10
                              pt = [sb(pref + f"pt{i}", [128, 512], BF, s2) for i in range(NPT)]
                              rcp = [sb(pref + f"rc{i}", [128, 512], F32, s2) for i in range(2)]
                              kq, kk_, kv_, ksg = pref + "qT", pref + "kT", pref + "v", pref + "sg"
                              if smp:
                                  kct = sb(pref + "kct", [128, 2, kvw], BF, s2)
                                  vct = sb(pref + "vct", [128, 2, kvw], BF, s2)
                                  dma("pool", kct[:], cache_k[l].rearrange("(a p) n -> p a n", p=128), [], [pref + "kct"])
                                  dma("pool", vct[:], cache_v[l].rearrange("(a p) n -> p a n", p=128), [], [pref + "vct"])
                                  if na:
                                      bmraw = sb(pref + "bmraw", [64, 60, 64], F32, s2)
                                      src = bass.AP(tensor=rpbscr.tensor, offset=l * (2 * RPAD + 1860) + RPAD - 48,
                                                    ap=[[1, 64], [31, 60], [1, 64]])
                                      dma("sp", bmraw[:], src, [("rpbscr", l)], [pref + "bmraw"])
                              if use_sink:
                                  sinkx = sb(pref + "sink", [128, 4], F32, s2)
                                  dma("sp", sinkx[:], attn_sink[l, :].partition_broadcast(128), [], [pref + "sink"])
                                  act(sinkx[:], sinkx[:], AF.Exp, [pref + "sink"], [pref + "sink"])
                              if pref == "a_":
                                  stage((l * 2 + g) * 10 + 2.02)

                              for j in range(nh // 2):
                                  wk, wb = load_w(cq + j * 256, 256)
                                  for hh in range(2):
                                      h = j * 2 + hh
                                      proj_fm(wk, wb, hh * 128,
                                              lambda pk, ps, tc, h=h: evac(qT[:, h, tc * 512:(tc + 1) * 512], ps[:, :],
                                                                           [pk], [(kq, h, tc)], scale=SCALE))
                              if pref == "a_":
                                  stage((l * 2 + g) * 10 + 2.05)
                              if not smp:
                                  kTf = sb(pref + "kTf", [128, nkv, T], F32, s2)
                                  stgk = [sb(pref + f"stgk{i}", [128, kvw], F32, s2) for i in range(2)]
                              for j in range(nkv // 2):
                                  wk, wb = load_w(ck + j * 256, 256)
                                  for hh in range(2):
                                      h = j * 2 + hh

                                      def ev_kf(pk, ps, tc, h=h):
                                          sl_ = slice(tc * 512, (tc + 1) * 512)
                                          if smp:
                                              evac(kT[:, h, sl_], ps[:, :], [pk], [(kk_, h, tc)])
                                          else:
                                              cp("dve", kTf[:, h, sl_], ps[:, :], [pk], [(pref + "kTf", h, tc)])
                                              cp("act", kT[:, h, sl_], kTf[:, h, sl_], [(pref + "kTf", h, tc)], [(kk_, h, tc)])
                                      proj_fm(wk, wb, hh * 128, ev_kf)
                              if not smp:
                                  for t in range(NT):
                                      pk, ps = PS()
                                      for h in range(nkv):
                                          tr(ps[:, h * 128:(h + 1) * 128], kTf[:, h, t * 128:(t + 1) * 128], identf[:],
                                             [(pref + "kTf", h, t // 4)] + KALL, [pk])
                                      b = t % 2
                                      cp("dve", stgk[b][:], ps[:, 0:kvw], [pk], [(pref + "stgk", b)])
                                      s_, tt_ = divmod(t, 2)
                                      dma("sp", o_k[s_, l, tt_ * 128:(tt_ + 1) * 128, :], stgk[b][:],
                                          [(pref + "stgk", b)], [pref + "ok"])
                              if pref == "a_":
                                  stage((l * 2 + g) * 10 + 2.1)
                              for j in range(nkv // 2):
                                  wk, wb = load_w(cv + j * 256, 256)

                                  def ev_v(pk, ps, t, j=j):
                                      if smp:
                                          evac(vtm[:, t, j * 256:(j + 1) * 256], ps[:, 0:256], [pk], [(kv_, t)])
                                      else:
                                          b = t % 2
                                          cp("dve", stg[b][:], ps[:, 0:256], [pk], [(pref + "stg", b)])
                                          cp("act", vtm[:, t, j * 256:(j + 1) * 256], stg[b][:], [(pref + "stg", b)], [(kv_, t)])
                                          s_, tt_ = divmod(t, 2)
                                          dma("sp", o_v[s_, l, tt_ * 128:(tt_ + 1) * 128, j * 256:(j + 1) * 256], stg[b][:],
                                              [(pref + "stg", b)], [pref + "ov"])
                                  proj_tm(wk, wb, 256, ev_v)
                              if pref == "a_":
                                  stage((l * 2 + g) * 10 + 2.15)
                              for j in range(nh // 2):
                                  wk, wb = load_w(cg + j * 256, 256)
                                  for hh in range(2):
                                      h = j * 2 + hh
                                      proj_fm(wk, wb, hh * 128,
                                              lambda pk, ps, tc, h=h: act(sgT[:, h, tc * 512:(tc + 1) * 512], ps[:, :], AF.Silu,
                                                                          [pk], [(ksg, h, tc)]))

                              if pref == "a_":
                                  stage((l * 2 + g) * 10 + 2.2)
                                  prefetch([(C_QD, 256), (C_QD + 256, 256), (C_KD, 256), (C_KD + 256, 256)])
                              else:
                                  prefetch([(C_PC, 256), (C_PC + 256, 256), (C_GC, 256), (C_GC + 256, 256)])
                              pti = [0]

                              def getpt():
                                  i = pti[0] % NPT
                                  pti[0] += 1
                                  return (pref + "pt", i), pt[i]

                              ric = [0]

                              def finish(h, q0, nq, ok_, ops_, dk, dps):
                                  ri = ric[0]
                                  ric[0] += 1
                                  R_ = rcp[ri % 2]
                                  rk = (pref + "rc", ri % 2)
                                  if use_sink:
                                      ts("dve", R_[:, 0:nq], dps[:, 0:nq], sinkx[:, h:h + 1], None, ALU.add, None, [dk, pref + "sink"], [rk])
                                      S.op("dve", lambda e: e.reciprocal(R_[:, 0:nq], R_[:, 0:nq]), [rk], [rk])
                                  else:
                                      S.op("dve", lambda e: e.reciprocal(R_[:, 0:nq], dps[:, 0:nq]), [dk], [rk])
                                  tt("dve", R_[:, 0:nq], ops_[:, 0:nq], R_[:, 0:nq], ALU.mult, [ok_, rk], [rk])
                                  tt("dve", mixedT[:, fc0 + h, q0:q0 + nq], R_[:, 0:nq], sgT[:, h, q0:q0 + nq], ALU.mult,
                                     [rk] + [(ksg, h, tc) for tc in range(2)], [("mx", fc0 + h)])

                              def pv(contrib, h, q0, nq):
                                  ok_, ops_ = PS()
                                  dk, dps = PS()
                                  n_ = len(contrib)
                                  for i_, (ptk, pap, lv, vk, c0, ncl) in enumerate(contrib):
                                      mm(ops_[:, c0:c0 + ncl], lv, pap, i_ == 0, i_ == n_ - 1, [ptk, vk], [ok_])
                                  for i_, (ptk, pap, lv, vk, c0, ncl) in enumerate(contrib):
                                      mm(dps[:, c0:c0 + ncl], onesb[:], pap, i_ == 0, i_ == n_ - 1, [ptk] + KALL, [dk])
                                  finish(h, q0, nq, ok_, ops_, dk, dps)

                              if not smp:
                                  for s_ in range(4):
                                      t0 = s_ * 256
                                      for kv in range(nkv):
                                          pts = []
                                          for kt in range(2):
                                              pk, ps = PS()
                                              for hh in range(G):
                                                  h = kv * G + hh
                                                  mm(ps[:, hh * 256:(hh + 1) * 256], kT[:, kv, t0 + kt * 128:t0 + (kt + 1) * 128],
                                                     qT[:, h, t0:t0 + 256], True, True,
                                                     [(kk_, kv, s_ // 2), (kq, h, s_ // 2)], [pk])
                                              ptk, P_ = getpt()
                                              act(P_[:, 0:G * 256], ps[:, 0:G * 256], AF.Exp, [pk], [ptk])
                                              pts.append((ptk, P_))
                                          ok_, ops_ = PS()
                                          dk, dps = PS()
                                          for kt in range(2):
                                              mm(ops_[:, 0:G * 256], vtm[:, s_ * 2 + kt, kv * 128:(kv + 1) * 128], pts[kt][1][:, 0:G * 256],
                                                 kt == 0, kt == 1, [(kv_, s_ * 2 + kt), pts[kt][0]], [ok_])
                                          for kt in range(2):
                                              mm(dps[:, 0:G * 256], onesb[:], pts[kt][1][:, 0:G * 256], kt == 0, kt == 1,
                                                 [pts[kt][0]] + KALL, [dk])
                                          for hh in range(G):
                                              h = kv * G + hh
                                              finish(h, t0, 256, ok_, ops_[:, hh * 256:(hh + 1) * 256], dk,
                                                     dps[:, hh * 256:(hh + 1) * 256])
                              else:
                                  kcT = sb(pref + "kcT", [128, nkv, 256], BF, s2)
                                  for kv in range(nkv):
                                      pk, ps = PS()
                                      pb = ps[:].bitcast(BF)
                                      for a in range(2):
                                          tr(pb[:, a * 128:(a + 1) * 128], kct[:, a, kv * 128:(kv + 1) * 128], identb[:],
                                             [pref + "kct"] + KALL, [pk])
                                      cp("dve", kcT[:, kv, :], pb[:, 0:256], [pk], [(pref + "kcT", kv)])

                                  def ctx_contrib(h, kv, qc):
                                      out = []
                                      for a in range(2):
                                          pk, ps = PS()
                                          mm(ps[:, :], kcT[:, kv, a * 128:(a + 1) * 128], qT[:, h, qc * 512:(qc + 1) * 512], True, True,
                                             [(pref + "kcT", kv), (kq, h, qc)], [pk])
                                          ptk, P_ = getpt()
                                          act(P_[:, 0:512], ps[:, :], AF.Exp, [pk], [ptk])
                                          out.append((ptk, P_[:, 0:512], vct[:, a, kv * 128:(kv + 1) * 128], pref + "vct", 0, 512))
                                      return out

                                  if not na:
                                      rtmp = [sb(pref + f"rt{i}", [128, 512], F32, s2) for i in range(2)]
                                      for (buf, nm, hh_) in [(qT, kq, nh), (kT, kk_, nkv)]:
                                          for h in range(hh_):
                                              for tc in range(2):
                                                  sl = slice(tc * 512, (tc + 1) * 512)
                                                  pk, ps = PS()
                                                  kk2 = (nm, h, tc)
                                                  mm(ps[:, :], pswap[:], buf[:, h, sl], True, True, [kk2] + KALL, [pk])
                                                  b = (h + tc) % 2
                                                  tt("dve", rtmp[b][:], ps[:, :], sinT[:, sl], ALU.mult, [pk] + KALL, [(pref + "rt", b)])
                                                  tt("dve", buf[:, h, sl], buf[:, h, sl], cosT[:, sl], ALU.mult, [kk2, pk] + KALL, [kk2])
                                                  tt("dve", buf[:, h, sl], buf[:, h, sl], rtmp[b][:], ALU.add, [kk2, (pref + "rt", b)], [kk2])
                                      for h in range(nh):
                                          kv = h // G
                                          for qc in range(2):
                                              contrib = ctx_contrib(h, kv, qc)
                                              for kt in range(4 * qc - 1, 4 * qc + 5):
                                                  if kt < 0 or kt > 7:
                                                      continue
                                                  b0 = max(4 * qc, kt - 1)
                                                  b1 = min(4 * qc + 3, kt + 1)
                                                  nb_ = b1 - b0 + 1
                                                  pk, ps = PS()
                                                  mm(ps[:, 0:nb_ * 128], kT[:, kv, kt * 128:(kt + 1) * 128],
                                                     qT[:, h, b0 * 128:(b1 + 1) * 128], True, True,
                                                     [(kk_, kv, kt // 4), (kq, h, qc)], [pk])
                                                  ptk, P_ = getpt()
                                                  act(P_[:, 0:nb_ * 128], ps[:, 0:nb_ * 128], AF.Exp, [pk], [ptk])
                                                  for bq in range(b0, b1 + 1):
                                                      o_ = (bq - b0) * 128
                                                      if bq == kt + 1:
                                                          tt("dve", P_[:, o_:o_ + 128], P_[:, o_:o_ + 128], mprev[:], ALU.mult,
                                                             [ptk] + KALL, [ptk])
                                                      elif bq == kt - 1:
                                                          tt("dve", P_[:, o_:o_ + 128], P_[:, o_:o_ + 128], mnext[:], ALU.mult,
                                                             [ptk] + KALL, [ptk])
                                                  contrib.append((ptk, P_[:, 0:nb_ * 128], vtm[:, kt, kv * 128:(kv + 1) * 128],
                                                                  (kv_, kt), (b0 - 4 * qc) * 128, nb_ * 128))
                                              pv(contrib, h, qc * 512, 512)
                                  else:
                                      bmd = sb(pref + "bmd", [64, 4, 15, 64], BF, s2)
                                      btmp = [sb(pref + "btmp0", [64, 8, 64], F32, s2)] * 2
                                      bi = 0
                                      for h in range(4):
                                          for (e0, ne) in [(0, 8), (8, 7)]:
                                              pk, ps = PS()
                                              for i_ in range(ne):
                                                  row = h * 15 + 14 - (e0 + i_)
                                                  mm(ps[0:64, i_ * 64:(i_ + 1) * 64], bmraw[:, row, :], jrev[:], True, True,
                                                     [pref + "bmraw"] + KALL, [pk])
                                              bt = btmp[bi % 2]
                                              bk = (pref + "btmp", 0)
                                              bi += 1
                                              tt("dve", bt[:, 0:ne, :], ps[0:64, 0:ne * 64].rearrange("p (a b) -> p a b", a=ne),
                                                 colok[:].unsqueeze(1).to_broadcast([64, ne, 64]), ALU.mult, [pk] + KALL, [bk])
                                              tt("dve", bmd[:, h, e0:e0 + ne, :], bt[:, 0:ne, :],
                                                 colneg[:].unsqueeze(1).to_broadcast([64, ne, 64]), ALU.add, [bk] + KALL, [(pref + "bmd", h)])
                                      for h in range(nh):
                                          for qc in range(2):
                                              contrib = ctx_contrib(h, h, qc)
                                              for kt in range(8):
                                                  (a0, b0_) = NA_R[2 * kt]
                                                  (a1, b1_) = NA_R[2 * kt + 1]
                                                  r1 = max(min(a0, a1), 8 * qc)
                                                  r2 = min(max(b0_, b1_), 8 * qc + 7)
                                                  if r1 > r2:
                                                      continue
                                                  n = r2 - r1 + 1
                                                  pk, ps = PS()
                                                  seq_ = []
                                                  seq_.append((ps[:, 0:n * 64], kT[:, h, kt * 128:(kt + 1) * 128], qT[:, h, r1 * 64:(r2 + 1) * 64],
                                                               [(kk_, h, kt // 4), (kq, h, qc)]))
                                                  for j, sel in ((0, sel0), (1, sel1)):
                                                      kr = 2 * kt + j
                                                      (a, b) = NA_R[kr]
                                                      va, vb = max(a, r1), min(b, r2)
                                                      if va <= vb:
                                                          e1 = 7 - kr + va
                                                          seq_.append((ps[:, (va - r1) * 64:(vb - r1 + 1) * 64], sel[:],
                                                                       bmd[:, h, e1:e1 + (vb - va + 1), :].rearrange("p a b -> p (a b)"),
                                                                       [(pref + "bmd", h)] + KALL))
                                                      for r in range(r1, r2 + 1):
                                                          if r < a or r > b:
                                                              seq_.append((ps[:, (r - r1) * 64:(r - r1 + 1) * 64], sel[:], negblk[:], KALL))
                                                  for i_, (o_, l_, r_, rd) in enumerate(seq_):
                                                      mm(o_, l_, r_, i_ == 0, i_ == len(seq_) - 1, rd, [pk])
                                                  ptk, P_ = getpt()
                                                  act(P_[:, 0:n * 64], ps[:, 0:n * 64], AF.Exp, [pk], [ptk])
                                                  contrib.append((ptk, P_[:, 0:n * 64], vtm[:, kt, h * 128:(h + 1) * 128], (kv_, kt),
                                                                  (r1 - 8 * qc) * 64, n * 64))
                                              pv(contrib, h, qc * 512, 512)
                              S.barrier(False)

                      attn_branch("a_", 4, 2, C_QA, C_KA, C_VA, C_GA, 0, o_ak, o_av, True, cak, cav, False)
                      stage((l * 2 + g) * 10 + 2.5)
                      if FEAT >= 1:
                          attn_branch("d_", 4, 4, C_QD, C_KD, C_VD, C_GD, 12, o_nk, o_nv, False, cnk, cnv, smp)
                      else:
                          for fc in range(12, 16):
                              S.op("pool", (lambda e, fc=fc: e.memset(mixedT[:, fc, :], 0.0)), [], [("mx", fc)])

                      stage((l * 2 + g) * 10 + 2.6)
                      if FEAT >= 2:
                          with ExitStack() as s2:
                              SLp = SL + 16
                              pcT = sb("c_pc", [128, 4, nseq, SLp], F32, s2)
                              sgc = sb("c_sg", [128, 4, T], BF, s2)
                              pooled = sb("c_pl", [128, 4, T], BF, s2)
                              wA = sb("c_wa", [128, nseq, SLp], F32, s2)
                              wB = sb("c_wb", [128, nseq, SLp], F32, s2)
                              rct = sb("c_rc", [128, 4, SL], F32, s2)
                              pw = sb("c_pw", [128, 4, 128], BF, s2)
                              pbs = sb("c_pb", [128, 4], F32, s2)
                              psc = sb("c_ps", [128, 4], F32, s2)
                              ctmp = [sb(f"c_tmp{i}", [128, 512], F32, s2) for i in range(2)]
                              S.op("pool", (lambda e, pcT=pcT: e.memset(pcT[:], 0.0)), [], ["c_pc"])
                              dma("sp", rct[:], (cst["rcs"] if smp else cst["rcp"]).partition_broadcast(128), [], ["c_rc"])
                              dma("pool", pw[:], pool_w[l].rearrange("g c d -> c g d"), [], ["c_pw"])
                              dma("sp", pbs[:], pool_b[l, :].rearrange("(g p) -> p g", p=128), [], ["c_pb"], nonc=True)
                              dma("sp", psc[:], pool_scale[l, :].rearrange("(g p) -> p g", p=128), [], ["c_ps"], nonc=True)
                              for j in range(2):
                                  wk, wb = load_w(C_PC + j * 256, 256)
                                  for hh in range(2):
                                      gq = j * 2 + hh

                                      def ev_pc(pk, ps, tc, gq=gq):
                                          if smp:
                                              dst = pcT[:, gq, 0, 8 + tc * 512:8 + (tc + 1) * 512]
                                              srcp = ps[:, :]
                                          else:
                                              dst = pcT[:, gq, 2 * tc:2 * tc + 2, 8:8 + 256]
                                              srcp = ps[:, :].rearrange("p (s t) -> p s t", s=2)
                                          evac(dst, srcp, [pk, "c_pc"], [("c_pcg", gq)])
                                      proj_fm(wk, wb, hh * 128, ev_pc)
                              for j in range(2):
                                  wk, wb = load_w(C_GC + j * 256, 256)
                                  for hh in range(2):
                                      gq = j * 2 + hh
                                      proj_fm(wk, wb, hh * 128,
                                              lambda pk, ps, tc, gq=gq: act(sgc[:, gq, tc * 512:(tc + 1) * 512], ps[:, :], AF.Silu,
                                                                            [pk], [("c_sg", gq, tc)]))
                              prefetch([(C_XBC, 256), (C_XBC + 256, 256), (C_XBC + 512, 256), (C_XBC + 768, 256)])
                              for gq in range(4):
                                  P_ = pcT[:, gq]
                                  kp = ("c_pcg", gq)
                                  tt("dve", wA[:, :, 1:SLp], P_[:, :, 0:SLp - 1], P_[:, :, 1:SLp], ALU.add, [kp], ["c_wa"])
                                  cur, ck_ = wA, "c_wa"
                                  if gq >= 1:
                                      tt("dve", wB[:, :, 2:SLp - 1], wA[:, :, 1:SLp - 2], wA[:, :, 3:SLp], ALU.add, ["c_wa"], ["c_wb"])
                                      cur, ck_ = wB, "c_wb"
                                  if gq >= 2:
                                      tt("dve", wA[:, :, 4:SLp - 3], wB[:, :, 2:SLp - 5], wB[:, :, 6:SLp - 1], ALU.add, ["c_wb"], ["c_wa"])
                                      cur, ck_ = wA, "c_wa"
                                  if gq >= 3:
                                      tt("dve", wB[:, :, 8:SLp - 7], wA[:, :, 4:SLp - 11], wA[:, :, 12:SLp - 3], ALU.add, ["c_wa"], ["c_wb"])
                                      cur, ck_ = wB, "c_wb"
                                  tt("dve", cur[:, :, 8:8 + SL], cur[:, :, 8:8 + SL],
                                     rct[:, gq, :].unsqueeze(1).to_broadcast([128, nseq, SL]), ALU.mult, [ck_, "c_rc"], [ck_])
                                  tt("dve", pooled[:, gq, :].rearrange("p (s t) -> p s t", s=nseq), cur[:, :, 8:8 + SL],
                                     P_[:, :, 8:8 + SL], ALU.subtract, [ck_, kp], [("c_pl", gq)])
                                  for tc in range(2):
                                      pk, ps = PS()
                                      mm(ps[:, :], pw[:, gq, :], pooled[:, gq, tc * 512:(tc + 1) * 512], True, True,
                                         [("c_pl", gq), "c_pw"], [pk])
                                      b = tc % 2
                                      ts("dve", ctmp[b][:], ps[:, :], pbs[:, gq:gq + 1], psc[:, gq:gq + 1], ALU.add, ALU.mult,
                                         [pk, "c_pb", "c_ps"], [("c_tmp", b)])
                                      tt("dve", mixedT[:, 8 + gq, tc * 512:(tc + 1) * 512], ctmp[b][:], sgc[:, gq, tc * 512:(tc + 1) * 512],
                                         ALU.mult, [("c_tmp", b), ("c_sg", gq, tc)], [("mx", 8 + gq)])
                              S.barrier(False)
                      else:
                          for fc in range(8, 12):
                              S.op("pool", (lambda e, fc=fc: e.memset(mixedT[:, fc, :], 0.0)), [], [("mx", fc)])

                      stage((l * 2 + g) * 10 + 2.7)
                      if FEAT >= 3:
                          with ExitStack() as s2:
                              xsT = sb("b_xsT", [128, 4, T], BF, s2)
                              bcT = sb("b_bcT", [128, 4, T], BF, s2)
                              xs = sb("b_xs", [128, NT, 512], BF, s2)
                              btm = sb("b_btm", [128, NT, 256], BF, s2)
                              sz = sb("b_sz", [128, NT, 512], BF, s2)
                              dtr = sb("b_dt", [128, NT, 16], F32, s2)
                              dta = sb("b_dta", [128, NT, 16], F32, s2)
                              ysum = sb("b_y", [128, NT, 512], F32, s2)
                              cw = sb("b_cw", [128, 8, 5], F32, s2)
                              cb = sb("b_cb", [128, 8], F32, s2)
                              dtb = sb("b_dtb", [128, 16], F32, s2)
                              nga = sb("b_nga", [128, 16], F32, s2)
                              dbc = sb("b_dbc", [128, 8], F32, s2)
                              nwb = sb("b_nwb", [128, 512], F32, s2)
                              for k5 in range(5):
                                  dma("sp", cw[:, :, k5], conv_w[l, k5, :].rearrange("(b p) -> p b", p=128), [], ["b_cw"], nonc=True)
                              dma("sp", cb[:], conv_b[l, :].rearrange("(b p) -> p b", p=128), [], ["b_cb"], nonc=True)
                              dma("sp", dtb[:], dt_bias[l, :].partition_broadcast(128), [], ["b_dtb"])
                              dma("sp", nga[:], a_log[l, :].partition_broadcast(128), [], ["b_nga"])
                              dma("sp", dbc[:], ssm_d[l, :].partition_broadcast(128), [], ["b_dbc"])
                              dma("sp", nwb[:], norm_w[l, :].partition_broadcast(128), [], ["b_nwb"])
                              act(nga[:], nga[:], AF.Exp, ["b_nga"], ["b_nga"])
                              ts("dve", nga[:], nga[:], -1.0, None, ALU.mult, None, ["b_nga"], ["b_nga"])
                              with ExitStack() as s3:
                                  raw = [sb(f"b_raw{i}", [128, nseq, SL + 4], F32, s3) for i in range(2)]
                                  acc = [sb(f"b_acc{i}", [128, nseq, SL], F32, s3) for i in range(2)]
                                  for i in range(2):
                                      S.op("pool", (lambda e, r=raw[i]: e.memset(r[:], 0.0)), [], [("b_rawz", i)])
                                  for j in range(4):
                                      wk, wb = load_w(C_XBC + j * 256, 256)
                                      for hh in range(2):
                                          blk = j * 2 + hh
                                          rb = blk % 2
                                          R_, A_ = raw[rb], acc[rb]

                                          def ev_x(pk, ps, tc, R_=R_, rb=rb):
                                              if smp:
                                                  dst = R_[:, 0, 2 + tc * 512:2 + (tc + 1) * 512]
                                                  srcp = ps[:, :]
                                              else:
                                                  dst = R_[:, 2 * tc:2 * tc + 2, 2:2 + 256]
                                                  srcp = ps[:, :].rearrange("p (s t) -> p s t", s=2)
                                              evac(dst, srcp, [pk, ("b_rawz", rb)], [("b_raw", rb, tc)])
                                          proj_fm(wk, wb, hh * 128, ev_x)
                                          rk = [("b_raw", rb, 0), ("b_raw", rb, 1)]
                                          ak = ("b_acc", rb)
                                          ts("dve", A_[:], R_[:, :, 2:2 + SL], cw[:, blk, 2:3], None, ALU.mult, None, rk + ["b_cw"], [ak])
                                          for k in (0, 1, 3, 4):
                                              stt("dve", A_[:], R_[:, :, k:k + SL], cw[:, blk, k:k + 1], A_[:], ALU.mult, ALU.add,
                                                  rk + [ak, "b_cw"], [ak])
                                          if blk < 4:
                                              dst = xsT[:, blk, :].rearrange("p (s t) -> p s t", s=nseq)
                                              dk_ = ("b_xsT", blk)
                                          else:
                                              dst = bcT[:, blk - 4, :].rearrange("p (s t) -> p s t", s=nseq)
                                              dk_ = ("b_bcT", blk - 4)
                                          act(dst, A_[:], AF.Silu, [ak, "b_cb"], [dk_], bias=cb[:, blk:blk + 1])
                              for j in range(2):
                                  wk, wb = load_w(C_Z + j * 256, 256)
                                  proj_tm(wk, wb, 256, lambda pk, ps, t, j=j: act(sz[:, t, j * 256:(j + 1) * 256], ps[:, 0:256], AF.Silu,
                                                                                  [pk], [("b_sz", t)]))
                              wk, wb = load_w(C_DT, 16)
                              proj_tm(wk, wb, 16, lambda pk, ps, t: tt("dve", dtr[:, t, :], ps[:, 0:16], dtb[:], ALU.add,
                                                                        [pk, "b_dtb"], ["b_dt"]))
                              act(dtr[:], dtr[:], AF.Exp, ["b_dt"], ["b_dt"])
                              act(dtr[:], dtr[:], AF.Ln, ["b_dt"], ["b_dt"], bias=1.0)
                              tt("dve", dta[:], dtr[:], nga[:].unsqueeze(1).to_broadcast([128, NT, 16]), ALU.mult, ["b_dt", "b_nga"], ["b_dta"])
                              for q4 in range(4):
                                  if q4 < 2:
                                      wr = [("uT", tq, kc) for tq in range(2) for kc in range(q4 * 8, (q4 + 1) * 8)]
                                  elif q4 == 2:
                                      wr = [("w", 0), ("w", 1)]
                                  else:
                                      wr = [("w", 2), ("w", 3)]
                                  dma("pool", wo[:, q4 * 4:(q4 + 1) * 4, :],
                                      w_out[l, q4 * 512:(q4 + 1) * 512, :].rearrange("(kc p) n -> p kc n", p=128),
                                      [], wr + [("wo", q4)], nobarrier=True)
                              for t in range(NT):
                                  tsl = slice(t * 128, (t + 1) * 128)
                                  pk, ps = PS()
                                  pb = ps[:].bitcast(BF)
                                  for blk in range(4):
                                      tr(pb[:, blk * 128:(blk + 1) * 128], xsT[:, blk, tsl], identb[:], [("b_xsT", blk)] + KALL, [pk])
                                  evac(xs[:, t, :], pb[:, 0:512], [pk], [("b_xs", t)])
                                  pk, ps = PS()
                                  pb = ps[:].bitcast(BF)
                                  for gq in range(2):
                                      tr(pb[:, gq * 128:(gq + 1) * 128], bcT[:, gq, tsl], identb[:], [("b_bcT", gq)] + KALL, [pk])
                                  evac(btm[:, t, :], pb[:, 0:256], [pk], [("b_btm", t)])
                              with ExitStack() as s3:
                                  dec = [sb(f"b_dec{i}", [128, 8, 128], BF, s3) for i in range(2)]
                                  MT = [sb(f"b_mt{i}", [128, 8, 128], BF, s3) for i in range(2)]
                                  xdt = [sb(f"b_xdt{i}", [128, 8, 64], BF, s3) for i in range(2)]
                                  xdw = [sb(f"b_xdw{i}", [128, 8, 64], BF, s3) for i in range(2)]
                                  cbs = [sb(f"b_cbs{i}", [128, 2, 128], F32, s3) for i in range(2)]
                                  lcs = [sb(f"b_lcs{i}", [128, 48], F32, s3) for i in range(2)]
                                  tmp = [sb(f"b_tmp{i}", [128, 512], F32, s3) for i in range(2)]
                                  hT = sb("b_hT", [128, 512], F32, s3)
                                  hTb = sb("b_hTb", [128, 512], BF, s3)
                                  hst = sb("b_hst", [128, 4, 128], F32, s3)
                                  nch = SL // 128
                                  items = []
                                  for s_ in range(nseq):
                                      for d in range(2):
                                          order = list(range(nch)) if d == 0 else list(range(nch - 1, -1, -1))
                                          for oi, c in enumerate(order):
                                              items.append((s_, d, c, oi == 0, oi == nch - 1))
                                  ctxs = {}

                                  def alpha(n):
                                      s_, d, c, first, last = items[n]
                                      tri = trif if d == 0 else trib
                                      neg4 = negf if d == 0 else negb
                                      t = s_ * nch + c
                                      tsl = slice(t * 128, (t + 1) * 128)
                                      b = n % 2
                                      if l == 0:
                                          ada_step()
                                      a_ = dta[:, t, d * 8:(d + 1) * 8]
                                      dt_ = dtr[:, t, d * 8:(d + 1) * 8]
                                      DE, M_, XD, XW, CB, LC = dec[b], MT[b], xdt[b], xdw[b], cbs[b], lcs[b]
                                      kDE, kM, kXD, kXW, kCB, kLC = [(n_, b) for n_ in
                                                                     ("b_dec", "b_mt", "b_xdt", "b_xdw", "b_cbs", "b_lcs")]
                                      pkL, psL = PS()
                                      mm(psL[:, 0:8], tri[:], a_, True, True, ["b_dta"] + KALL, [pkL])
                                      mm(psL[:, 8:16], onesf[:], a_, True, True, ["b_dta"] + KALL, [pkL])
                                      cp("dve", LC[:, 0:16], psL[:, 0:16], [pkL], [kLC])
                                      ts("dve", LC[:, 16:24], LC[:, 0:8], -1.0, None, ALU.mult, None, [kLC], [kLC])
                                      tt("dve", LC[:, 32:40], LC[:, 8:16], LC[:, 0:8], ALU.subtract, [kLC], [kLC])
                                      act(LC[:, 24:32], LC[:, 0:8], AF.Exp, [kLC], [kLC])
                                      act(LC[:, 32:40], LC[:, 32:40], AF.Exp, [kLC], [kLC])
                                      act(LC[:, 40:48], LC[:, 8:16], AF.Exp, [kLC], [kLC])
                                      for hg in range(2):
                                          pkD, psD = PS()
                                          mm(psD[:, :], identb[:], neg4[:], True, False, KALL, [pkD])
                                          for hh in range(4):
                                              h = hg * 4 + hh
                                              mm(psD[:, hh * 128:(hh + 1) * 128], a_[:, h:h + 1].to_broadcast([128, 128]), tri[:],
                                                 False, hh == 3, ["b_dta"] + KALL, [pkD])
                                          for hh in range(4):
                                              h = hg * 4 + hh
                                              act(DE[:, h, :], psD[:, hh * 128:(hh + 1) * 128], AF.Exp, [pkD, kLC], [kDE],
                                                  bias=LC[:, 16 + h:17 + h])
                                      pkC, psC = PS()
                                      for gq in range(2):
                                          mm(psC[:, gq * 128:(gq + 1) * 128], bcT[:, gq, tsl], bcT[:, 2 + gq, tsl], True, True,
                                             [("b_bcT", gq), ("b_bcT", 2 + gq)], [pkC])
                                      evac(CB[:].rearrange("p a b -> p (a b)"), psC[:, 0:256], [pkC], [kCB])
                                      for gq in range(2):
                                          tt("dve", M_[:, gq * 4:(gq + 1) * 4, :], DE[:, gq * 4:(gq + 1) * 4, :],
                                             CB[:, gq, :].unsqueeze(1).to_broadcast([128, 4, 128]), ALU.mult, [kDE, kCB], [kM])
                                      tt("pool", XD[:], xs[:, t, :].rearrange("p (h q) -> p h q", h=8),
                                         dt_.unsqueeze(2).to_broadcast([128, 8, 64]), ALU.mult, [("b_xs", t), "b_dt"], [kXD])
                                      tt("pool", XW[:], XD[:], LC[:, 32:40].unsqueeze(2).to_broadcast([128, 8, 64]), ALU.mult,
                                         [kXD, kLC], [kXW])

                                  def beta(n):
                                      s_, d, c, first, last = items[n]
                                      t = s_ * nch + c
                                      tsl = slice(t * 128, (t + 1) * 128)
                                      b = n % 2
                                      M_, XD, XW, LC, TM = MT[b], xdt[b], xdw[b], lcs[b], tmp[b]
                                      kM, kXD, kXW, kLC, kTM = [(n_, b) for n_ in ("b_mt", "b_xdt", "b_xdw", "b_lcs", "b_tmp")]
                                      if first:
                                          if smp:
                                              dma("sp", hst[:], st_in[d][l].rearrange("(a p) n -> p a n", p=128), [], ["b_hst"])
                                              pk, ps = PS()
                                              for a in range(4):
                                                  tr(ps[:, a * 128:(a + 1) * 128], hst[:, a, :], identf[:], ["b_hst"] + KALL, [pk])
                                              cp("dve", hT[:], ps[:, :], [pk], ["b_hT"])
                                              cp("act", hTb[:], hT[:], ["b_hT"], ["b_hTb"])
                                          else:
                                              S.op("pool", (lambda e, hT=hT: e.memset(hT[:], 0.0)), [], ["b_hT"])
                                              S.op("pool", (lambda e, hTb=hTb: e.memset(hTb[:], 0.0)), [], ["b_hTb"])
                                      pkY, psY = PS()
                                      for h in range(8):
                                          mm(psY[:, h * 64:(h + 1) * 64], M_[:, h, :], XD[:, h, :], True, True, [kM, kXD], [pkY])
                                      pkY2, psY2 = PS()
                                      for gq in range(2):
                                          mm(psY2[:, gq * 256:(gq + 1) * 256], bcT[:, 2 + gq, tsl], hTb[:, gq * 256:(gq + 1) * 256],
                                             True, True, [("b_bcT", 2 + gq), "b_hTb"], [pkY2])
                                      tt("dve", TM[:].rearrange("p (h q) -> p h q", h=8), psY2[:, :].rearrange("p (h q) -> p h q", h=8),
                                         LC[:, 24:32].unsqueeze(2).to_broadcast([128, 8, 64]), ALU.mult, [pkY2, kLC], [kTM])
                                      if d == 0:
                                          tt("dve", ysum[:, t, :], psY[:, :], TM[:], ALU.add, [pkY, kTM], [("b_y", t)])
                                      else:
                                          tt("dve", TM[:], psY[:, :], TM[:], ALU.add, [pkY, kTM], [kTM])
                                          tt("pool", ysum[:, t, :], ysum[:, t, :], TM[:], ALU.add, [kTM, ("b_y", t)], [("b_y", t)])
                                      pkS, psS = PS()
                                      for gq in range(2):
                                          mm(psS[:, gq * 256:(gq + 1) * 256], btm[:, t, gq * 128:(gq + 1) * 128],
                                             XW[:, gq * 4:(gq + 1) * 4, :].rearrange("p a b -> p (a b)"), True, True,
                                             [("b_btm", t), kXW], [pkS])
                                      tt("dve", hT[:].rearrange("p (h q) -> p h q", h=8), hT[:].rearrange("p (h q) -> p h q", h=8),
                                         LC[:, 40:48].unsqueeze(2).to_broadcast([128, 8, 64]), ALU.mult, ["b_hT", kLC], ["b_hT"])
                                      tt("dve", hT[:], hT[:], psS[:, :], ALU.add, ["b_hT", pkS], ["b_hT"])
                                      cp("dve", hTb[:], hT[:], ["b_hT"], ["b_hTb"])
                                      if last and not smp:
                                          pk, ps = PS()
                                          for a in range(4):
                                              tr(ps[:, a * 128:(a + 1) * 128], hT[:, a * 128:(a + 1) * 128], identf[:], ["b_hT"] + KALL, [pk])
                                          cp("dve", hst[:].rearrange("p a b -> p (a b)"), ps[:, :], [pk], ["b_hst"])
                                          dma("sp", o_st[d][s_, l].rearrange("(a p) n -> p a n", p=128), hst[:], ["b_hst"], ["o_st"])

                                  alpha(0)
                                  for n in range(len(items)):
                                      if n + 1 < len(items):
                                          alpha(n + 1)
                                      beta(n)
                                  ob = [dec[i][:, 0:4, :].rearrange("p a b -> p (a b)") for i in range(2)]
                                  st6 = [sb(f"b_st{i}", [128, 8], F32, s3) for i in range(2)]
                                  for t in range(NT):
                                      b = t % 2
                                      TM, OB, ST = tmp[b], ob[b], st6[b]
                                      kTM, kOB, kST = ("b_tmp", b), ("b_ob", b), ("b_st", b)
                                      yk = ("b_y", t)
                                      tt("pool", TM[:].rearrange("p (h q) -> p h q", h=8), xs[:, t, :].rearrange("p (h q) -> p h q", h=8),
                                         dbc[:].unsqueeze(2).to_broadcast([128, 8, 64]), ALU.mult, [("b_xs", t), "b_dbc"], [kTM])
                                      tt("dve", ysum[:, t, :], ysum[:, t, :], TM[:], ALU.add, [yk, kTM], [yk])
                                      tt("dve", ysum[:, t, :], ysum[:, t, :], sz[:, t, :], ALU.mult, [yk, ("b_sz", t)], [yk])
                                      S.op("dve", (lambda e, ST=ST, t=t, ysum=ysum: e.bn_stats(ST[:, 0:6], ysum[:, t, :])), [yk], [kST])
                                      S.op("dve", (lambda e, ST=ST: e.bn_aggr(ST[:, 6:8], ST[:, 0:6])), [kST], [kST])
                                      stt("dve", ST[:, 0:1], ST[:, 6:7], ST[:, 6:7], ST[:, 7:8], ALU.mult, ALU.add, [kST], [kST])
                                      rstd_op(ST[:, 1:2], ST[:, 0:1], [kST])
                                      stt("dve", OB[:], ysum[:, t, :], ST[:, 1:2], nwb[:], ALU.mult, ALU.mult, [yk, kST, "b_nwb"], [kOB])
                                      pk, ps = PS()
                                      pb = ps[:].bitcast(BF)
                                      for blk in range(4):
                                          tr(pb[:, blk * 128:(blk + 1) * 128], OB[:, blk * 128:(blk + 1) * 128], identb[:], [kOB] + KALL, [pk])
                                      evac(mixedT[:, 4:8, t * 128:(t + 1) * 128], pb[:, 0:512].rearrange("p (a b) -> p a b", a=4),
                                           [pk], [("mx", 4 + i) for i in range(4)])
                              S.barrier(False)
                      else:
                          for fc in range(4, 8):
                              S.op("pool", (lambda e, fc=fc: e.memset(mixedT[:, fc, :], 0.0)), [], [("mx", fc)])

                      S.barrier(False)
                      if DEBUG and l == 0:
                          dma("pool", dbg_mixed[l * 2 + g], mixedT[:], [("mx", i) for i in range(16)], ["dbg"])
                          S.barrier(False)
                      stage((l * 2 + g) * 10 + 3)

                  S.barrier(False)
                  if S.stopped:
                      break

                  stage((l * 2 + g) * 10 + 4)
                  if S.stopped:
                      break
                  with ExitStack() as s1:
                      gt = sb("gt", [128, D], F32, s1)
                      lg = sb("lg", [128, D], F32, s1)
                      lb = sb("lb", [128, D], F32, s1)
                      xt = [sb(f"oxt{i}", [128, D], F32, s1) for i in range(3)]
                      rr = [sb(f"orr{i}", [128, D], F32, s1) for i in range(2)]
                      st6 = [sb(f"ost6{i}", [128, 4, 6], F32, s1) for i in range(2)]
                      mv = [sb(f"omv{i}", [128, 4], F32, s1) for i in range(2)]
                      dma("sp", gt[:], modscr[l, g, 2 * D:3 * D].partition_broadcast(128), ["modscr"], ["gt"])
                      dma("sp", lg[:], ln_g[l, :].partition_broadcast(128), [], ["lg"])
                      dma("sp", lb[:], ln_b[l, :].partition_broadcast(128), [], ["lb"])
                      def xload(t_):
                          dma("sp", xt[t_ % 3][:], xsrc[t_ * 128:(t_ + 1) * 128, :], [("y1", g)] if l else [], [("oxt", t_ % 3)])

                      xload(0)
                      xload(1)
                      for t in range(NT):
                          b = t % 2
                          X, R_, ST, MV = xt[t % 3], rr[b], st6[b], mv[b]
                          kx, kr, kst, kmv = ("oxt", t % 3), ("orr", b), ("ost6", b), ("omv", b)
                          if t + 2 < NT:
                              xload(t + 2)
                          act(X[:], X[:], AF.Copy, [kx], [kx], scale=ALPHA)
                          for c4 in range(4):
                              pk, ps = PS()
                              for kc in range(16):
                                  mm(ps[:, :], mixedT[:, kc, t * 128:(t + 1) * 128], wo[:, kc, c4 * 512:(c4 + 1) * 512],
                                     kc == 0, kc == 15, [("mx", kc), ("wo", kc // 4)], [pk])
                              csl = slice(c4 * 512, (c4 + 1) * 512)
                              tt("dve", R_[:, csl], ps[:, :], gt[:, csl], ALU.mult, [pk, "gt"], [(kr, c4)])
                              tt("dve", R_[:, csl], R_[:, csl], X[:, csl], ALU.add, [kx, (kr, c4)], [(kr, c4)])
                              S.op("dve", (lambda e, R_=R_, ST=ST, c4=c4, csl=csl: e.bn_stats(ST[:, c4, :], R_[:, csl])),
                                   [(kr, c4)], [kst])
                          RK = [(kr, c4) for c4 in range(4)]
                          S.op("dve", (lambda e, ST=ST, MV=MV: e.bn_aggr(MV[:, 0:2], ST[:].rearrange("p a b -> p (a b)"))),
                               [kst], [kmv])
                          rstd_op(MV[:, 2:3], MV[:, 1:2], [kmv])
                          stt("dve", MV[:, 3:4], MV[:, 0:1], -1.0, MV[:, 2:3], ALU.mult, ALU.mult, [kmv], [kmv])
                          act(R_[:], R_[:], AF.Identity, RK + [kmv], RK, bias=MV[:, 3:4], scale=MV[:, 2:3])
                          tt("dve", R_[:], R_[:], lg[:], ALU.mult, RK + ["lg"], RK)
                          tt("pool", R_[:], R_[:], lb[:], ALU.add, RK + ["lb"], RK)
                          dma("sp", ydst[t * 128:(t + 1) * 128, :], R_[:], RK, [("y1", g)] if l == 0 else ["yout"])
                      S.barrier()
                  stage((l * 2 + g) * 10 + 5)
                  if S.stopped:
                      break
                  if l == 0 and g == 1:
                      while ada1["pending"] or ada1["next"] < 24:
                          ada_step()
                      ada_gate_store(1)
                      S.barrier(False)
              if S.stopped:
                  break

        except _Stop:
            pass

        S.stopped = False
        for e in Sched.ENG:
            S.op(e, lambda en: en.nop(), [], [])
        S.finalize()

        with nc.Block() as block:
            @block.tensor
            def _(eng):
                S.emit("pe", eng)

            @block.scalar
            def _(eng):
                S.emit("act", eng)

            @block.vector
            def _(eng):
                S.emit("dve", eng)

            @block.gpsimd
            def _(eng):
                S.emit("pool", eng)

            @block.sync
            def _(eng):
                S.emit("sp", eng)
    return nc


_NC_CACHE = {}


def kernel(x_prompt, x_sample, cache_attn_k, cache_attn_v, cache_na_k, cache_na_v,
           state_ssm_fwd, state_ssm_bwd, c, c_ctx, w_ada, b_ada, w_in, w_out, ln_g, ln_b,
           attn_sink, ssm_conv_w, ssm_conv_b, ssm_a_log, ssm_dt_bias, ssm_d, ssm_norm_w,
           pool_w, pool_b, pool_scale, na_rpb):
    f = lambda a: np.ascontiguousarray(np.asarray(a, dtype=np.float32))
    if "nc" not in _NC_CACHE:
        _NC_CACHE["nc"] = build_program()
    nc = _NC_CACHE["nc"]
    consts = _consts()
    shared = {
        "w_ada": f(w_ada), "b_ada": f(b_ada), "w_in": f(w_in), "w_out": f(w_out),
        "ln_g": f(ln_g), "ln_b": f(ln_b), "attn_sink": f(attn_sink),
        "ssm_conv_w": f(ssm_conv_w), "ssm_conv_b": f(ssm_conv_b),
        "ssm_a_log": f(ssm_a_log).reshape(2, 16), "ssm_dt_bias": f(ssm_dt_bias).reshape(2, 16),
        "ssm_d": f(ssm_d), "ssm_norm_w": f(ssm_norm_w), "pool_w": f(pool_w),
        "pool_b": f(pool_b).reshape(2, 512), "pool_scale": f(pool_scale),
        "na_rpb": f(na_rpb).reshape(2, 4 * 15 * 31),
    }
    for k, v in consts.items():
        shared["c_" + k] = v
    xp = f(x_prompt); xs = f(x_sample)
    in_maps = []
    for ci in range(NCORES):
        b = ci % 4
        m = dict(shared)
        m["xp"] = xp[4 * ci:4 * ci + 4].reshape(T, D)
        m["xs"] = xs[b].reshape(T, D)
        m["cak"] = f(cache_attn_k)[b].reshape(2, 256, 256)
        m["cav"] = f(cache_attn_v)[b].reshape(2, 256, 256)
        m["cnk"] = f(cache_na_k)[b].reshape(2, 256, 512)
        m["cnv"] = f(cache_na_v)[b].reshape(2, 256, 512)
        m["sf"] = f(state_ssm_fwd)[b].reshape(2, 512, 128)
        m["sb"] = f(state_ssm_bwd)[b].reshape(2, 512, 128)
        m["cvec"] = np.stack([f(c_ctx), f(c)[b]], 0)
        in_maps.append(m)
    res = run_bass_kernel_spmd(nc, in_maps, core_ids=list(range(NCORES)))
    R = res.results
    _NC_CACHE["last"] = R
    y_prompt = np.concatenate([np.asarray(R[ci]["yp"]).reshape(4, 256, D) for ci in range(NCORES)], 0)
    y_sample = np.stack([np.asarray(R[ci]["ys"]).reshape(1024, D) for ci in range(4)], 0)
    cat = lambda name, shp: np.concatenate([np.asarray(R[ci][name]).reshape(shp) for ci in range(NCORES)], 0)
    nak = cat("o_ak", (4, 2, 256, 2, 128)); nav = cat("o_av", (4, 2, 256, 2, 128))
    nnk = cat("o_nk", (4, 2, 256, 4, 128)); nnv = cat("o_nv", (4, 2, 256, 4, 128))
    nsf = cat("o_sf", (4, 2, 8, 64, 128)); nsb = cat("o_sb", (4, 2, 8, 64, 128))
    return (y_prompt.astype(np.float32), y_sample.astype(np.float32), nak, nav, nnk, nnv, nsf, nsb)
```
